# Optimizing a Trainium2 kernel written in Bass

```python
import math
import jax, jax.numpy as jnp
from jax import lax
import numpy as np

D_MODEL = 1024
BATCH = 4
SEQ = 4096
DEPTH = 4
DEC_BATCH = 16
DEC_SEQ = 64
PAST_LEN = 1024

CHUNK = 64
N_MIXERS = 3
N_RG = (DEPTH + 2) // 3
N_SWA = (DEPTH + 1) // 3
N_FOX = DEPTH // 3
D_FF = 4 * D_MODEL
HEAD_DIM = 64
N_HEADS = D_MODEL // HEAD_DIM
SWA_KV_HEADS = 4
SWA_GROUP = N_HEADS // SWA_KV_HEADS
WINDOW = 128
WIN_CHUNKS = WINDOW // CHUNK
FOX_Q_BLOCK = 128
LRU_WIDTH = D_MODEL
LRU_BLOCKS = 4
LRU_BLOCK_W = LRU_WIDTH // LRU_BLOCKS
CONV_WIDTH = 4
LRU_C = 8.0
N_BUCKETS = 32
MAX_DISTANCE = 128
ALPHA = (2.0 * DEPTH) ** 0.25
BETA = (8.0 * DEPTH) ** -0.25
LN_EPS = 1e-5
ATTN_SCALE = HEAD_DIM ** -0.5

kernel_name = "hybrid_streaming_encoder_step"


def layer_norm(x, g, b):
    xf = x.astype(jnp.float32)
    mu = jnp.mean(xf, axis=-1, keepdims=True)
    var = jnp.mean(jnp.square(xf - mu), axis=-1, keepdims=True)
    return ((xf - mu) * lax.rsqrt(var + LN_EPS) * g.astype(jnp.float32) + b.astype(jnp.float32)).astype(x.dtype)


def sq_relu_mlp(x, w_up, w_down):
    return jnp.square(jax.nn.relu(x @ w_up)) @ w_down


def _linear_combine(e1, e2):
    a1, b1 = e1
    a2, b2 = e2
    return a1 * a2, a2 * b1 + b2


def rglru_mixer(x, conv_buf, h0, w_in, conv_w, conv_b, gate_w, gate_b, lam, w_out):
    B, T, _ = x.shape
    gate, u = jnp.split(x @ w_in, 2, axis=-1)
    u_pad = jnp.concatenate([conv_buf.astype(u.dtype), u], axis=1)
    conv = conv_b + u_pad[:, 0:T] * conv_w[0]
    for k in range(1, CONV_WIDTH):
        conv = conv + u_pad[:, k:k + T] * conv_w[k]
    new_buf = u_pad[:, T:]
    gates = jnp.einsum('btnd,nde->btne', conv.reshape(B, T, LRU_BLOCKS, LRU_BLOCK_W), gate_w) + gate_b
    gates = jax.nn.sigmoid(gates.astype(jnp.float32))
    r = gates[..., :LRU_BLOCK_W].reshape(B, T, LRU_WIDTH)
    i_g = gates[..., LRU_BLOCK_W:].reshape(B, T, LRU_WIDTH)
    log_a = -LRU_C * r * jax.nn.softplus(-lam.astype(jnp.float32))
    a = jnp.exp(log_a)
    b = jnp.sqrt(-jnp.expm1(2.0 * log_a)) * (i_g * conv.astype(jnp.float32))
    a_cum, b_cum = lax.associative_scan(_linear_combine, (a, b), axis=1)
    h = a_cum * h0.astype(jnp.float32)[:, None, :] + b_cum
    y = (h.astype(x.dtype) * jax.nn.gelu(gate)) @ w_out
    return y, new_buf, h[:, -1].astype(x.dtype)


def t5_relative_bias(rel, table):
    half = N_BUCKETS // 2
    max_exact = half // 2
    n = jnp.abs(rel)
    n_f = jnp.maximum(n, 1).astype(jnp.float32)
    large = max_exact + (jnp.log(n_f / max_exact) / math.log(MAX_DISTANCE / max_exact) * (half - max_exact)).astype(jnp.int32)
    large = jnp.minimum(large, half - 1)
    bucket = jnp.where(rel > 0, half, 0) + jnp.where(n < max_exact, n, large)
    return jnp.transpose(table[bucket].astype(jnp.float32), (2, 0, 1))


def swa_project(x, w_qkv):
    B, T, _ = x.shape
    hq = N_HEADS * HEAD_DIM
    hk = SWA_KV_HEADS * HEAD_DIM
    qkv = x @ w_qkv
    q = qkv[..., :hq].reshape(B, T, SWA_KV_HEADS, SWA_GROUP, HEAD_DIM)
    k = qkv[..., hq:hq + hk].reshape(B, T, SWA_KV_HEADS, HEAD_DIM)
    v = qkv[..., hq + hk:].reshape(B, T, SWA_KV_HEADS, HEAD_DIM)
    return q, k, v


def swa_attend(q, k, v, key_valid, bias, sinks):
    s = jnp.einsum('bnqhgd,bnshd->bnhgqs', q, k).astype(jnp.float32) * ATTN_SCALE
    s = s + bias.reshape(SWA_KV_HEADS, SWA_GROUP, *bias.shape[1:])
    s = jnp.where(key_valid[None, :, None, None, None, :], s, -jnp.inf)
    sink = sinks.astype(jnp.float32).reshape(SWA_KV_HEADS, SWA_GROUP)[:, :, None, None]
    m = jnp.maximum(jnp.max(s, axis=-1, keepdims=True), sink)
    p = jnp.exp(s - m)
    denom = jnp.sum(p, axis=-1, keepdims=True) + jnp.exp(sink - m)
    return jnp.einsum('bnhgqs,bnshd->bnqhgd', (p / denom).astype(v.dtype), v)


def swa_prompt(x, w_qkv, sinks, w_out, table):
    B, T, _ = x.shape
    nc = T // CHUNK
    hist = WIN_CHUNKS * CHUNK
    q, k, v = swa_project(x, w_qkv)
    q = q.reshape(B, nc, CHUNK, SWA_KV_HEADS, SWA_GROUP, HEAD_DIM)

    def band(t):
        tp = jnp.pad(t, ((0, 0), (hist, 0), (0, 0), (0, 0))).reshape(B, nc + WIN_CHUNKS, CHUNK, SWA_KV_HEADS, HEAD_DIM)
        return jnp.concatenate([tp[:, c:c + nc] for c in range(WIN_CHUNKS + 1)], axis=2)

    n_keys = (WIN_CHUNKS + 1) * CHUNK
    key_pos = jnp.arange(nc)[:, None] * CHUNK - hist + jnp.arange(n_keys)[None, :]
    rel = jnp.arange(n_keys)[None, :] - hist - jnp.arange(CHUNK)[:, None]
    o = swa_attend(q, band(k), band(v), key_pos >= 0, t5_relative_bias(rel, table), sinks)
    y = o.reshape(B, T, N_HEADS * HEAD_DIM) @ w_out
    return y, k[:, -WINDOW:], v[:, -WINDOW:]


def swa_sample(x, k_cache, v_cache, w_qkv, sinks, w_out, table):
    B, T, _ = x.shape
    hist = k_cache.shape[1]
    q, k, v = swa_project(x, w_qkv)
    k_all = jnp.concatenate([k_cache.astype(k.dtype), k], axis=1)
    v_all = jnp.concatenate([v_cache.astype(v.dtype), v], axis=1)
    rel = jnp.arange(hist + T)[None, :] - hist - jnp.arange(T)[:, None]
    valid = jnp.ones((1, hist + T), dtype=bool)
    o = swa_attend(q[:, None], k_all[:, None], v_all[:, None], valid, t5_relative_bias(rel, table), sinks)
    y = o.reshape(B, T, N_HEADS * HEAD_DIM) @ w_out
    return y, k_all[:, -hist:], v_all[:, -hist:]


def fox_project(x, w_in, b_f):
    B, T, _ = x.shape
    hd = N_HEADS * HEAD_DIM
    proj = x @ w_in
    q = proj[..., :hd].reshape(B, T, N_HEADS, HEAD_DIM)
    k = proj[..., hd:2 * hd].reshape(B, T, N_HEADS, HEAD_DIM)
    v = proj[..., 2 * hd:3 * hd].reshape(B, T, N_HEADS, HEAD_DIM)
    logf = jax.nn.log_sigmoid(proj[..., 3 * hd:].astype(jnp.float32) + b_f.astype(jnp.float32))
    return q, k, v, logf


def fox_attend(q, cq, qpos, k, v, ck, kpos):
    s = jnp.einsum('bqhd,bshd->bhqs', q, k).astype(jnp.float32) * ATTN_SCALE
    s = s + jnp.swapaxes(cq, 1, 2)[..., :, None] - jnp.swapaxes(ck, 1, 2)[..., None, :]
    s = jnp.where(kpos[None, :] <= qpos[:, None], s, -jnp.inf)
    p = jax.nn.softmax(s, axis=-1)
    return jnp.einsum('bhqs,bshd->bqhd', p.astype(v.dtype), v)


def fox_prompt(x, w_in, b_f, w_out):
    B, T, _ = x.shape
    q, k, v, logf = fox_project(x, w_in, b_f)
    c = jnp.cumsum(logf, axis=1)
    nqb = T // FOX_Q_BLOCK
    pos = jnp.arange(T)
    qb = q.reshape(B, nqb, FOX_Q_BLOCK, N_HEADS, HEAD_DIM).swapaxes(0, 1)
    cqb = c.reshape(B, nqb, FOX_Q_BLOCK, N_HEADS).swapaxes(0, 1)
    pb = pos.reshape(nqb, FOX_Q_BLOCK)
    o = lax.map(lambda blk: fox_attend(blk[0], blk[1], blk[2], k, v, c, pos), (qb, cqb, pb))
    o = o.swapaxes(0, 1).reshape(B, T, N_HEADS * HEAD_DIM)
    return o @ w_out, k, v, logf.astype(x.dtype)


def fox_sample(x, k_cache, v_cache, logf_cache, w_in, b_f, w_out):
    B, T, _ = x.shape
    L = k_cache.shape[1]
    q, k, v, logf = fox_project(x, w_in, b_f)
    k_all = jnp.concatenate([k_cache.astype(k.dtype), k], axis=1)
    v_all = jnp.concatenate([v_cache.astype(v.dtype), v], axis=1)
    c = jnp.cumsum(jnp.concatenate([logf_cache.astype(jnp.float32), logf], axis=1), axis=1)
    pos = jnp.arange(L + T)
    o = fox_attend(q, c[:, L:], pos[L:], k_all, v_all, c, pos)
    return o.reshape(B, T, N_HEADS * HEAD_DIM) @ w_out, k, v, logf.astype(x.dtype)


def setup_inputs(seed: int = 0) -> dict:
    key = jax.random.key(seed)
    ks = jax.random.split(key, 32)

    def nrm(i, shape, scale=1.0):
        return scale * jax.random.normal(ks[i], shape, jnp.float32)

    hd = N_HEADS * HEAD_DIM
    kvd = SWA_KV_HEADS * HEAD_DIM
    swa_rows = min(WINDOW, PAST_LEN)
    lam_u = jax.random.uniform(ks[18], (N_RG, LRU_WIDTH), jnp.float32, 0.9, 0.999)
    lam_s = lam_u ** (1.0 / LRU_C)
    return {
        "x_prompt": nrm(0, (BATCH, SEQ, D_MODEL)),
        "x_sample": nrm(1, (DEC_BATCH, DEC_SEQ, D_MODEL)),
        "state_rg_conv": nrm(2, (N_RG, DEC_BATCH, CONV_WIDTH - 1, LRU_WIDTH)),
        "state_rg_h": nrm(3, (N_RG, DEC_BATCH, LRU_WIDTH), 0.5),
        "cache_swa_k": nrm(4, (N_SWA, DEC_BATCH, swa_rows, SWA_KV_HEADS, HEAD_DIM)),
        "cache_swa_v": nrm(5, (N_SWA, DEC_BATCH, swa_rows, SWA_KV_HEADS, HEAD_DIM), BETA),
        "cache_fox_k": nrm(6, (N_FOX, DEC_BATCH, PAST_LEN, N_HEADS, HEAD_DIM)),
        "cache_fox_v": nrm(7, (N_FOX, DEC_BATCH, PAST_LEN, N_HEADS, HEAD_DIM), BETA),
        "cache_fox_logf": jax.nn.log_sigmoid(2.5 + nrm(8, (N_FOX, DEC_BATCH, PAST_LEN, N_HEADS))),
        "ln_gain": 1.0 + nrm(9, (DEPTH, 2, D_MODEL), 0.01),
        "ln_bias": nrm(10, (DEPTH, 2, D_MODEL), 0.01),
        "ffn_w_up": nrm(11, (DEPTH, D_MODEL, D_FF), D_MODEL ** -0.5),
        "ffn_w_down": nrm(12, (DEPTH, D_FF, D_MODEL), BETA * D_FF ** -0.5),
        "rg_w_in": nrm(13, (N_RG, D_MODEL, 2 * LRU_WIDTH), D_MODEL ** -0.5),
        "rg_conv_w": nrm(14, (N_RG, CONV_WIDTH, LRU_WIDTH), CONV_WIDTH ** -0.5),
        "rg_conv_b": nrm(15, (N_RG, LRU_WIDTH), 0.01),
        "rg_gate_w": nrm(16, (N_RG, LRU_BLOCKS, LRU_BLOCK_W, 2 * LRU_BLOCK_W), LRU_BLOCK_W ** -0.5),
        "rg_gate_b": nrm(17, (N_RG, LRU_BLOCKS, 2 * LRU_BLOCK_W), 0.01),
        "rg_lambda": jnp.log(lam_s) - jnp.log1p(-lam_s),
        "rg_w_out": nrm(19, (N_RG, LRU_WIDTH, D_MODEL), BETA * LRU_WIDTH ** -0.5),
        "swa_w_qkv": jnp.concatenate([nrm(20, (N_SWA, D_MODEL, hd), D_MODEL ** -0.5),
                                      nrm(21, (N_SWA, D_MODEL, kvd), D_MODEL ** -0.5),
                                      nrm(22, (N_SWA, D_MODEL, kvd), BETA * D_MODEL ** -0.5)], axis=-1),
        "swa_sinks": nrm(23, (N_SWA, N_HEADS), 0.5),
        "swa_w_out": nrm(24, (N_SWA, hd, D_MODEL), BETA * hd ** -0.5),
        "rel_bias_table": nrm(25, (N_BUCKETS, N_HEADS), 0.5),
        "fox_w_in": jnp.concatenate([nrm(26, (N_FOX, D_MODEL, 2 * hd), D_MODEL ** -0.5),
                                     nrm(27, (N_FOX, D_MODEL, hd), BETA * D_MODEL ** -0.5),
                                     nrm(28, (N_FOX, D_MODEL, N_HEADS), 0.5 * D_MODEL ** -0.5)], axis=-1),
        "fox_b_f": jax.random.uniform(ks[29], (N_FOX, N_HEADS), jnp.float32, 1.0, 4.0),
        "fox_w_out": nrm(30, (N_FOX, hd, D_MODEL), BETA * hd ** -0.5),
    }


def reference(x_prompt, x_sample, state_rg_conv, state_rg_h, cache_swa_k, cache_swa_v,
              cache_fox_k, cache_fox_v, cache_fox_logf, ln_gain, ln_bias, ffn_w_up, ffn_w_down,
              rg_w_in, rg_conv_w, rg_conv_b, rg_gate_w, rg_gate_b, rg_lambda, rg_w_out,
              swa_w_qkv, swa_sinks, swa_w_out, rel_bias_table, fox_w_in, fox_b_f, fox_w_out):
    xp, xs = x_prompt, x_sample
    B = xp.shape[0]
    rg_conv_p, rg_conv_s, rg_h_p, rg_h_s = [], [], [], []
    swa_k_p, swa_k_s, swa_v_p, swa_v_s = [], [], [], []
    fox_k_p, fox_k_s, fox_v_p, fox_v_s, fox_f_p, fox_f_s = [], [], [], [], [], []
    for i in range(DEPTH):
        kind, j = i % N_MIXERS, i // N_MIXERS
        if kind == 0:
            w = (rg_w_in[j], rg_conv_w[j], rg_conv_b[j], rg_gate_w[j], rg_gate_b[j], rg_lambda[j], rg_w_out[j])
            mp, cb_p, h_p = rglru_mixer(xp, jnp.zeros((B, CONV_WIDTH - 1, LRU_WIDTH), xp.dtype),
                                        jnp.zeros((B, LRU_WIDTH), xp.dtype), *w)
            ms, cb_s, h_s = rglru_mixer(xs, state_rg_conv[j], state_rg_h[j], *w)
            rg_conv_p.append(cb_p); rg_conv_s.append(cb_s); rg_h_p.append(h_p); rg_h_s.append(h_s)
        elif kind == 1:
            mp, kp, vp = swa_prompt(xp, swa_w_qkv[j], swa_sinks[j], swa_w_out[j], rel_bias_table)
            ms, kk, vv = swa_sample(xs, cache_swa_k[j], cache_swa_v[j], swa_w_qkv[j], swa_sinks[j],
                                    swa_w_out[j], rel_bias_table)
            swa_k_p.append(kp); swa_v_p.append(vp); swa_k_s.append(kk); swa_v_s.append(vv)
        else:
            mp, kp, vp, fp = fox_prompt(xp, fox_w_in[j], fox_b_f[j], fox_w_out[j])
            ms, kk, vv, fs = fox_sample(xs, cache_fox_k[j], cache_fox_v[j], cache_fox_logf[j],
                                        fox_w_in[j], fox_b_f[j], fox_w_out[j])
            fox_k_p.append(kp); fox_v_p.append(vp); fox_f_p.append(fp)
            fox_k_s.append(kk); fox_v_s.append(vv); fox_f_s.append(fs)
        xp = layer_norm(ALPHA * xp + mp, ln_gain[i, 0], ln_bias[i, 0])
        xs = layer_norm(ALPHA * xs + ms, ln_gain[i, 0], ln_bias[i, 0])
        xp = layer_norm(ALPHA * xp + sq_relu_mlp(xp, ffn_w_up[i], ffn_w_down[i]), ln_gain[i, 1], ln_bias[i, 1])
        xs = layer_norm(ALPHA * xs + sq_relu_mlp(xs, ffn_w_up[i], ffn_w_down[i]), ln_gain[i, 1], ln_bias[i, 1])
    return (xp, xs,
            jnp.stack(rg_conv_p), jnp.stack(rg_conv_s), jnp.stack(rg_h_p), jnp.stack(rg_h_s),
            jnp.stack(swa_k_p), jnp.stack(swa_k_s), jnp.stack(swa_v_p), jnp.stack(swa_v_s),
            jnp.stack(fox_k_p), jnp.stack(fox_k_s), jnp.stack(fox_v_p), jnp.stack(fox_v_s),
            jnp.stack(fox_f_p), jnp.stack(fox_f_s))
```

```python
import math
import numpy as np
import ml_dtypes
import concourse.bass as bass
import concourse.mybir as mybir
from concourse.bass_utils import run_bass_kernel_spmd

F32 = mybir.dt.float32
BF16 = mybir.dt.bfloat16
AF = mybir.ActivationFunctionType
ALU = mybir.AluOpType

D = 1024
KC = 8
DFF = 4096
NH = 16
HD = 64
ALPHA = (2.0 * 4) ** 0.25
LN_EPS = 1e-5
SCALE = HD ** -0.5
NEG = -30000.0


class Buf:
    __slots__ = ("name", "wr", "wr_dma", "rd", "rd_dma", "sem", "cnt", "persist", "excl")

    def __init__(self, name, persist=False, excl=False):
        self.name = name
        self.persist = persist
        self.excl = excl
        self.wr = {}
        self.wr_dma = None
        self.rd = {}
        self.rd_dma = []
        self.sem = None
        self.cnt = 0


class Op:
    __slots__ = ("eng", "fn", "deps", "dma", "buf", "val", "sig", "cnt", "ep", "sem", "inc")

    def __init__(self, eng, fn, deps, dma, buf):
        self.eng = eng
        self.fn = fn
        self.deps = deps
        self.dma = dma
        self.buf = buf
        self.val = 0
        self.sig = False
        self.cnt = 0
        self.ep = 0


EPOCH = 20000


class Prog:
    def __init__(self, nc):
        self.nc = nc
        self.ops = []
        self.last = {}
        self.dma_since = []
        self.sem_counts = []
        self.sem_free = []
        self.local_bufs = []

    def add(self, eng, fn, reads=(), writes=(), dma=False, ndma=1, inc=16):
        idx = len(self.ops)
        deps = set()
        ex = [b for b in reads if b.excl and b not in writes]
        if ex:
            for b in ex:
                for e, i in b.wr.items():
                    deps.add((i, True))
            reads = [b for b in reads if not b.excl]
            writes = list(writes) + ex
        for b in reads:
            for e, i in b.wr.items():
                deps.add((i, True))
            if b.wr_dma is not None:
                deps.add((b.wr_dma, True))
        for b in writes:
            for e, i in b.wr.items():
                deps.add((i, False))
            if b.wr_dma is not None:
                deps.add((b.wr_dma, False))
            for e, i in b.rd.items():
                deps.add((i, False))
            for i in b.rd_dma:
                deps.add((i, False))
        buf = None
        if dma:
            buf = writes[0]
            if buf.sem is None:
                if self.sem_free:
                    buf.sem = self.sem_free.pop()
                else:
                    buf.sem = len(self.sem_counts)
                    self.sem_counts.append(0)
                buf.cnt = self.sem_counts[buf.sem]
                if not buf.persist:
                    self.local_bufs.append(buf)
            buf.cnt += inc * ndma
            self.sem_counts[buf.sem] = buf.cnt
        op = Op(eng, fn, deps, dma, buf)
        op.inc = inc
        if dma:
            op.val = buf.cnt
            op.sem = buf.sem
            self.dma_since.append(idx)
        for b in reads:
            if dma:
                b.rd_dma.append(idx)
            else:
                b.rd[eng] = idx
        for b in writes:
            b.rd = {}
            b.rd_dma = []
            if dma:
                b.wr_dma = idx
            else:
                b.wr[eng] = idx
        self.ops.append(op)
        if not dma:
            self.last[eng] = idx
        return idx

    def barrier(self):
        deps = set((i, True) for i in self.last.values())
        deps.update((i, True) for i in self.dma_since)
        self.dma_since = []
        for eng in ("pe", "act", "dve", "pool", "sp"):
            op = Op(eng, None, set(deps), False, None)
            op.inc = 0
            self.ops.append(op)
        for b in self.local_bufs:
            self.sem_free.append(b.sem)
            b.sem = None
        self.local_bufs = []

    def emit(self):
        nc = self.nc
        ops = self.ops
        for op in ops:
            for (p, raw) in op.deps:
                po = ops[p]
                if po.dma:
                    continue
                if po.eng == op.eng:
                    if op.dma or op.eng == "sp":
                        pass
                    elif po.eng == "pe":
                        continue
                po.sig = True
        counts = {}
        for op in ops:
            if op.dma or not op.sig:
                continue
            c = counts.get(op.eng, 0) + 1
            counts[op.eng] = c
            op.ep = (c - 1) // EPOCH
            op.cnt = (c - 1) % EPOCH + 1
        sems = {}

        def engsem(eng, ep):
            k = (eng, ep)
            if k not in sems:
                sems[k] = nc.alloc_semaphore(f"s_{eng}_{ep}")
            return sems[k]

        dsem = [nc.alloc_semaphore(f"d_{i}") for i in range(len(self.sem_counts))]
        print(f"[build] ops={len(ops)} dma_sems={len(dsem)}", flush=True)
        per = {"pe": [], "act": [], "dve": [], "pool": [], "sp": []}
        for i, op in enumerate(ops):
            per[op.eng].append(op)

        def run(eng, h):
            waited = {}
            for op in per[eng]:
                evs = {}
                for (p, raw) in op.deps:
                    po = ops[p]
                    if po.dma:
                        s, v = dsem[po.sem], po.val
                    else:
                        if not po.sig:
                            continue
                        if po.eng == eng and not (op.dma or eng == "sp"):
                            if po.eng == "pe":
                                continue
                        s, v = engsem(po.eng, po.ep), po.cnt
                    k = id(s)
                    if waited.get(k, 0) >= v:
                        continue
                    if k not in evs or evs[k][1] < v:
                        evs[k] = (s, v)
                for k, (s, v) in evs.items():
                    h.wait_ge(s, v)
                    waited[k] = v
                if op.fn is None:
                    continue
                r = op.fn(h)
                if op.dma:
                    for ins in r:
                        ins.then_inc(dsem[op.sem], op.inc)
                elif op.sig:
                    r.then_inc(engsem(op.eng, op.ep), 1)
            if eng == "sp":
                for i, sm in enumerate(dsem):
                    if waited.get(id(sm), 0) < self.sem_counts[i]:
                        h.wait_ge(sm, self.sem_counts[i])

        with nc.Block() as block:
            @block.tensor
            def _(h):
                run("pe", h)

            @block.scalar
            def _(h):
                run("act", h)

            @block.vector
            def _(h):
                run("dve", h)

            @block.gpsimd
            def _(h):
                run("pool", h)

            @block.sync
            def _(h):
                run("sp", h)


def _t5_bucket(rel):
    half = 16
    max_exact = 8
    n = np.abs(rel)
    n_f = np.maximum(n, 1).astype(np.float32)
    large = max_exact + (np.log(n_f / max_exact) / math.log(128 / max_exact) * (half - max_exact)).astype(np.int32)
    large = np.minimum(large, half - 1)
    return np.where(rel > 0, half, 0) + np.where(n < max_exact, n, large)


def _swa_bands():
    out = np.zeros((2, 33, 128, 128), np.float32)
    k = np.arange(128)[:, None]
    q = np.arange(128)[None, :]
    for kind in range(2):
        kpos = k - 128 * kind
        rel = kpos - q
        kch = np.floor_divide(kpos, 64)
        qch = q // 64
        valid = (kch <= qch) & (kch >= qch - 2)
        bk = _t5_bucket(rel)
        for b in range(32):
            out[kind, b] = ((bk == b) & valid).astype(np.float32)
        out[kind, 32] = (~valid).astype(np.float32)
    return out


def build(T, PAST):
    nc = bass.Bass("TRN2", target_bir_lowering=False)
    P = Prog(nc)
    TT = T + 128
    NB = TT // 128
    NBP = T // 128
    KB = PAST // 128

    def din(name, shape, dt=F32):
        return nc.dram_tensor(name, list(shape), dt, kind="ExternalInput").ap()

    def dout(name, shape, dt=F32):
        return nc.dram_tensor(name, list(shape), dt, kind="ExternalOutput").ap()

    def dscr(name, shape, dt=F32):
        return nc.dram_tensor(name, list(shape), dt).ap()

    I = {}
    I["xp"] = din("xp", [T, D])
    I["xs"] = din("xs", [128, D])
    I["prm"] = din("prm", [3, 128, 128])
    I["cswk"] = din("cswk", [2, 128, 256])
    I["cswv"] = din("cswv", [2, 128, 256])
    I["cfk"] = din("cfk", [2, PAST, D])
    I["cfv"] = din("cfv", [2, PAST, D])
    I["cff"] = din("cff", [2, PAST, 16])
    I["w_up"] = din("w_up", [4, D, DFF // 2])
    I["w_dn"] = din("w_dn", [4, DFF // 2, D])
    I["mk"] = din("mk", [128, 2])
    I["rg_in"] = din("rg_in", [2, D, 2 * D])
    I["rg_gw"] = din("rg_gw", [2, 4, 256, 512])
    I["rg_out"] = din("rg_out", [2, D, D])
    I["sw_qkv"] = din("sw_qkv", [1, D, 1536])
    I["sw_out"] = din("sw_out", [1, D, D])
    I["sinks"] = din("sinks", [1, 16])
    I["table"] = din("table", [1, 512])
    I["fx_in"] = din("fx_in", [1, D, 3088])
    I["fx_bf"] = din("fx_bf", [16, 1])
    I["fx_out"] = din("fx_out", [1, D, D])
    I["ident"] = din("ident", [128, 128])
    I["bands"] = din("bands", [128, 2 * 33 * 128], BF16)
    I["tri"] = din("tri", [128, 128], BF16)
    I["sel"] = din("sel", [96, 16 * 128], BF16)

    O = {}
    O["y_p"] = dout("y_p", [T, D])
    O["y_s"] = dout("y_s", [128, D])
    O["rgc_p"] = dout("rgc_p", [2, 3, D])
    O["rgc_s"] = dout("rgc_s", [2, 2, 3, D])
    O["rgh_p"] = dout("rgh_p", [2, D])
    O["rgh_s"] = dout("rgh_s", [2, 2, D])
    O["swk_p"] = dout("swk_p", [128, 256])
    O["swk_s"] = dout("swk_s", [2, 128, 256])
    O["swv_p"] = dout("swv_p", [128, 256])
    O["swv_s"] = dout("swv_s", [2, 128, 256])
    O["fk_p"] = dout("fk_p", [T, D])
    O["fk_s"] = dout("fk_s", [128, D])
    O["fv_p"] = dout("fv_p", [T, D])
    O["fv_s"] = dout("fv_s", [128, D])
    O["ff_p"] = dout("ff_p", [T, 16])
    O["ff_s"] = dout("ff_s", [128, 16])
    import os
    DBG = os.environ.get("KDBG", "0") == "1"
    FUSE_OUT = os.environ.get("KFUSE_OUT", "1") == "1"
    if DBG:
        for nm in ("dbg_a", "dbg_b", "dbg_cv", "dbg_h", "dbg_r", "dbg_i"):
            O[nm] = dout(nm, [128, KC, 128])
    OB = {k: Buf("o_" + k, True) for k in O}

    XD = [dscr("XA", [128, KC, TT]), dscr("XB", [128, KC, TT])]
    XDB = [[Buf(f"X{i}_{b}", True) for b in range(NB // 4 + 1)] for i in range(2)]
    NR = 3
    CCI = [dscr(f"cci{i}", [128, KC * 512]) for i in range(NR)]
    CCO = [dscr(f"cco{i}", [128, KC * 512]) for i in range(NR)]
    CCIB = [Buf(f"cci{i}", True) for i in range(NR)]
    CCOB = [Buf(f"cco{i}", True) for i in range(NR)]
    CC_GROUPS = [[0, 4], [1, 5], [2, 6], [3, 7]]
    SX = [dscr(f"ccs{i}", [128, KC * 256]) for i in range(4)]
    SXB = [Buf(f"ccs{i}", True) for i in range(4)]
    QAD = dscr("QAD", [16, 67, TT], BF16)
    KAD = dscr("KAD", [16, 67, T + 2 * (PAST + 64)], BF16)
    QADB = Buf("QAD", True)
    KADB = Buf("KAD", True)

    BASE = ((nc.SBUF_PARTITION_SIZE_BYTES - nc.sbuf_bytes_remaining + 63) // 64) * 64
    LIMIT = nc.SBUF_PARTITION_SIZE_BYTES - 64
    st = {"off": BASE, "n": 0, "mark": BASE, "persist": True}

    def alloc(shape, dt=F32):
        esz = 4 if dt == F32 else 2
        nbytes = int(np.prod(shape[1:])) * esz
        nbytes = (nbytes + 63) // 64 * 64
        off = st["off"]
        assert off + nbytes <= LIMIT, f"SBUF overflow {off + nbytes} > {LIMIT}"
        st["off"] = off + nbytes
        st["n"] += 1
        return nc.alloc_sbuf_tensor_at(f"t{st['n']}", list(shape), dt, offset=off)

    def tb(shape, dt=F32, name=""):
        return alloc(shape, dt), Buf(name, st["persist"])

    def tbl(shape, dt=F32, name="", n=KC):
        return alloc(shape, dt), [Buf(f"{name}{i}", st["persist"]) for i in range(n)]

    def phase_reset():
        P.barrier()
        st["persist"] = False
        st["off"] = st["mark"]

    PS = [nc.alloc_psum_tensor(f"ps{i}", [128, 512], F32) for i in range(7)]
    PSB = [Buf(f"ps{i}", True, True) for i in range(7)]
    PSH = nc.alloc_psum_tensor("psh", [128, 1024], BF16)
    PSHB = Buf("psh", True, True)

    ident, identB = tb([128, 128], F32, "ident")
    identh, identhB = tb([128, 128], BF16, "identh")
    onesb, onesbB = tb([128, 128], BF16, "ones")
    PT, PTB = tb([128, 384], F32, "PT")
    cvec, cvecB = tb([128, 16], F32, "cvec")
    cvec2, cvec2B = tb([128, 16], F32, "cvec2")
    P.add("sp", lambda h: [h.dma_start(out=ident[:], in_=I["ident"])], writes=[identB], dma=True)
    P.add("pool", lambda h: h.memset(onesb[:], 1.0), writes=[onesbB])
    P.add("dve", lambda h: h.tensor_copy(out=identh[:], in_=ident[:]), reads=[identB], writes=[identhB])
    MK, MKB = tb([128, 2], F32, "MK")
    P.add("sp", lambda h: [h.dma_start(out=MK[:], in_=I["mk"])], writes=[MKB], dma=True)
    prm, prmB = tb([128, 3, 128], F32, "prm")
    P.add("sp", lambda h: [h.dma_start(out=prm[:], in_=I["prm"].rearrange("g r c -> r g c"))], writes=[prmB], dma=True)
    for g in range(3):
        P.add("pe", lambda h, g=g: h.transpose(out=PS[0][:, g * 128:(g + 1) * 128], in_=prm[:, g, :], identity=ident[:]),
              reads=[prmB, identB], writes=[PSB[0]])
    P.add("dve", lambda h: h.tensor_copy(out=PT[:], in_=PS[0][:, 0:384]), reads=[PSB[0]], writes=[PTB])
    C_LNG, C_LNB, C_CW, C_CB, C_GB, C_LAM, C_H0, C_CV = 0, 64, 128, 192, 208, 240, 256, 288
    tA, tAB = tb([128, 16], F32, "tA")
    tW, tWB = tb([128, 16], F32, "tW")
    tW2, tW2B = tb([128, 16], F32, "tW2")
    st["mark"] = st["off"]
    P.add("act", lambda h: h.activation(out=tA[:], in_=PT[:, C_LAM:C_LAM + 16], func=AF.Exp, scale=-1.0), reads=[PTB], writes=[tAB])
    P.add("dve", lambda h: h.tensor_scalar(out=tW[:], in0=tA[:], scalar1=2.0, scalar2=None, op0=ALU.add), reads=[tAB], writes=[tWB])
    P.add("dve", lambda h: h.reciprocal(out=tW[:], in_=tW[:]), reads=[tWB], writes=[tWB])
    P.add("dve", lambda h: h.tensor_tensor(out=tW[:], in0=tW[:], in1=tA[:], op=ALU.mult), reads=[tWB, tAB], writes=[tWB])
    P.add("dve", lambda h: h.tensor_tensor(out=tW2[:], in0=tW[:], in1=tW[:], op=ALU.mult), reads=[tWB], writes=[tW2B])
    P.add("dve", lambda h: h.tensor_scalar(out=tA[:], in0=tW2[:], scalar1=1.0 / 9, scalar2=1.0 / 7, op0=ALU.mult, op1=ALU.add), reads=[tW2B], writes=[tAB])
    for cst in (1.0 / 5, 1.0 / 3, 1.0):
        P.add("dve", lambda h: h.tensor_tensor(out=tA[:], in0=tA[:], in1=tW2[:], op=ALU.mult), reads=[tAB, tW2B], writes=[tAB])
        P.add("dve", lambda h, cst=cst: h.tensor_scalar(out=tA[:], in0=tA[:], scalar1=cst, scalar2=None, op0=ALU.add), reads=[tAB], writes=[tAB])
    P.add("dve", lambda h: h.tensor_tensor(out=tA[:], in0=tA[:], in1=tW[:], op=ALU.mult), reads=[tAB, tWB], writes=[tAB])
    P.add("dve", lambda h: h.tensor_scalar(out=cvec[:], in0=tA[:], scalar1=-16.0, scalar2=None, op0=ALU.mult), reads=[tAB], writes=[cvecB])
    P.add("dve", lambda h: h.tensor_scalar(out=cvec2[:], in0=tA[:], scalar1=-32.0, scalar2=None, op0=ALU.mult), reads=[tAB], writes=[cvec2B])

    rr = {"cast": 0}

    def cast_eng():
        rr["cast"] += 1
        return ("pool", "dve", "act")[rr["cast"] % 3]

    def copy_op(eng, out, in_):
        if eng == "act":
            return lambda h: h.activation(out=out, in_=in_, func=AF.Copy)
        return lambda h: h.tensor_copy(out=out, in_=in_)

    def load_w(W, K, M, Wb, WbB, stg, c0=0, kc0=0):
        nkc = K // 128
        cstep = min(M, 2048)
        per = max(1, 2048 // cstep)
        kc = 0
        while kc < nkc:
            n = min(per, nkc - kc)
            for cc in range(0, M, cstep):
                cw = min(cstep, M - cc)
                s, sB = stg[rr["cast"] % len(stg)]
                src = W[kc * 128:(kc + n) * 128, cc:cc + cw].rearrange("(k p) m -> p k m", p=128)
                sv = s[:, 0:n * cw].rearrange("p (k m) -> p k m", k=n)
                P.add("sp", lambda h, sv=sv, src=src: [h.dma_start(out=sv, in_=src)], writes=[sB], dma=True)
                e = cast_eng()
                dst = Wb[:, kc0 + kc:kc0 + kc + n, c0 + cc:c0 + cc + cw]
                P.add(e, copy_op(e, dst, sv), reads=[sB], writes=[WbB])
            kc += n

    def mk_stage():
        return [tb([128, 2048], F32, f"stg{i}") for i in range(2)]

    def layer_norm(z, zB, N, li, lj, scr, pb, gen=False):
        for _ in layer_norm_g(z, zB, N, li, lj, scr, pb):
            pass

    def layer_norm_g(z, zB, N, li, lj, scr, pb, pool_ok=True):
        zb, zbB, zsq, zsqB, mean, meanB, msq, msqB, rstd, rstdB = scr
        PL = "pool" if pool_ok else "dve"
        P.add("dve", lambda h: h.tensor_copy(out=zb[:, :, 0:N], in_=z[:, :, 0:N]), reads=zB, writes=[zbB])
        P.add(PL, lambda h: h.tensor_tensor(out=zsq[:, :, 0:N], in0=z[:, :, 0:N], in1=z[:, :, 0:N], op=ALU.mult), reads=zB, writes=[zsqB])
        yield
        b1, b2 = pb

        def s1(h):
            for c in range(KC):
                r = h.matmul(PS[b1][:, 0:N], lhsT=onesb[:], rhs=zb[:, c, 0:N], start=(c == 0), stop=(c == KC - 1))
            return r

        def s2(h):
            for c in range(KC):
                r = h.matmul(PS[b2][:, 0:N], lhsT=onesb[:], rhs=zsq[:, c, 0:N], start=(c == 0), stop=(c == KC - 1))
            return r
        P.add("pe", s1, reads=[zbB, onesbB], writes=[PSB[b1]])
        P.add("pe", s2, reads=[zsqB, onesbB], writes=[PSB[b2]])
        yield
        P.add("act", lambda h: h.activation(out=mean[:, 0:N], in_=PS[b1][:, 0:N], func=AF.Copy, scale=1.0 / D), reads=[PSB[b1]], writes=[meanB])
        P.add("dve", lambda h: h.tensor_tensor(out=msq[:, 0:N], in0=mean[:, 0:N], in1=mean[:, 0:N], op=ALU.mult), reads=[meanB], writes=[msqB])
        P.add("dve", lambda h: h.scalar_tensor_tensor(out=msq[:, 0:N], in0=PS[b2][:, 0:N], scalar=1.0 / D, in1=msq[:, 0:N], op0=ALU.mult, op1=ALU.subtract),
              reads=[PSB[b2], msqB], writes=[msqB])
        P.add("dve", lambda h: h.tensor_scalar(out=msq[:, 0:N], in0=msq[:, 0:N], scalar1=LN_EPS, scalar2=None, op0=ALU.add), reads=[msqB], writes=[msqB])
        yield
        P.add("act", lambda h: h.activation(out=rstd[:, 0:N], in_=msq[:, 0:N], func=AF.Ln), reads=[msqB], writes=[rstdB])
        P.add("act", lambda h: h.activation(out=rstd[:, 0:N], in_=rstd[:, 0:N], func=AF.Exp, scale=-0.5), reads=[rstdB], writes=[rstdB])
        yield
        for c in range(KC):
            e = "dve" if c % 2 == 0 else PL
            P.add(e, lambda h, c=c: h.tensor_tensor(out=z[:, c, 0:N], in0=z[:, c, 0:N], in1=mean[:, 0:N], op=ALU.subtract), reads=[zB[c], meanB], writes=[zB[c]])
            e2 = PL if c % 2 == 0 else "dve"
            P.add(e2, lambda h, c=c: h.tensor_tensor(out=z[:, c, 0:N], in0=z[:, c, 0:N], in1=rstd[:, 0:N], op=ALU.mult), reads=[zB[c], rstdB], writes=[zB[c]])
            col = (li * 2 + lj) * 8 + c
            P.add("act", lambda h, c=c, col=col: h.activation(out=z[:, c, 0:N], in_=z[:, c, 0:N], func=AF.Identity,
                                                              scale=PT[:, C_LNG + col:C_LNG + col + 1], bias=PT[:, C_LNB + col:C_LNB + col + 1]),
                  reads=[zB[c], PTB], writes=[zB[c]])
            yield

    def ln_scratch(N):
        st["zb_off"] = st["off"]
        zb = tb([128, KC, N], BF16, "zb")
        zsq = tb([128, KC, N], BF16, "zsq")
        mean = tb([128, N], F32, "mean")
        msq = tb([128, N], F32, "msq")
        rstd = tb([128, N], F32, "rstd")
        return (*zb, *zsq, *mean, *msq, *rstd)

    def x_load(src, t0, N, xt, xtB):
        blks = [XDB[src][t0 // 512]]
        P.add("sp", lambda h: [h.dma_start(out=xt[:, :, 0:N], in_=XD[src][:, :, t0:t0 + N])], reads=blks, writes=xtB, dma=True)

    def x_store(dst, t0, N, xt, xtB):
        P.add("sp", lambda h: [h.dma_start(out=XD[dst][:, :, t0:t0 + N], in_=xt[:, :, 0:N])],
              reads=xtB, writes=[XDB[dst][t0 // 512]], dma=True)

    def mixer_tiles(n=256):
        tl = [(i * n, n) for i in range(T // n)]
        tl.append((T, 128))
        return tl

    def phase_in():
        tin = [tb([128, 4, D], F32, f"tin{i}") for i in range(2)]
        xo = [tbl([128, KC, 512], F32, f"xo{i}_") for i in range(2)]
        it = 0
        for (t0, N) in mixer_tiles(512):
            nb = N // 128
            ti, tiB = tin[it % 2]
            xt, xtB = xo[it % 2]
            src = (I["xp"][t0:t0 + N, :] if t0 < T else I["xs"]).rearrange("(b p) d -> p b d", p=128)
            P.add("sp", lambda h, ti=ti, src=src, nb=nb: [h.dma_start(out=ti[:, 0:nb, :], in_=src)], writes=[tiB], dma=True)
            for c in range(KC):
                pb = c % 2

                def tr(h, ti=ti, c=c, nb=nb, pb=pb):
                    for b in range(nb):
                        r = h.transpose(out=PS[pb][:, b * 128:(b + 1) * 128], in_=ti[:, b, c * 128:(c + 1) * 128], identity=ident[:])
                    return r
                P.add("pe", tr, reads=[tiB, identB], writes=[PSB[pb]])
                e = "dve" if c % 2 == 0 else "act"
                P.add(e, copy_op(e, xt[:, c, 0:N], PS[pb][:, 0:N]), reads=[PSB[pb]], writes=[xtB[c]])
            x_store(0, t0, N, xt, xtB)
            it += 1

    def phase_out(src):
        xi = [tbl([128, KC, 512], F32, f"xi{i}_") for i in range(2)]
        to = [tb([128, 4, D], F32, f"to{i}") for i in range(2)]
        it = 0
        for (t0, N) in mixer_tiles(512):
            nb = N // 128
            xt, xtB = xi[it % 2]
            tt, ttB = to[it % 2]
            x_load(src, t0, N, xt, xtB)
            for b in range(nb):
                for half in range(2):
                    pb = (b * 2 + half) % 2

                    def tr(h, xt=xt, b=b, half=half, pb=pb):
                        for c4 in range(4):
                            c = half * 4 + c4
                            r = h.transpose(out=PS[pb][:, c4 * 128:(c4 + 1) * 128], in_=xt[:, c, b * 128:(b + 1) * 128], identity=ident[:])
                        return r
                    P.add("pe", tr, reads=xtB[half * 4:half * 4 + 4] + [identB], writes=[PSB[pb]])
                    e = "dve" if half == 0 else "act"
                    P.add(e, copy_op(e, tt[:, b, half * 512:(half + 1) * 512], PS[pb][:, 0:512]), reads=[PSB[pb]], writes=[ttB])
            if t0 < T:
                dst, dB = O["y_p"][t0:t0 + N, :], OB["y_p"]
            else:
                dst, dB = O["y_s"], OB["y_s"]
            dst = dst.rearrange("(b p) d -> p b d", p=128)
            P.add("sp", lambda h, tt=tt, dst=dst, nb=nb: [h.dma_start(out=dst, in_=tt[:, 0:nb, :])], reads=[ttB], writes=[dB], dma=True)
            it += 1

    def phase_ffn(li, src, dst):
        NF = 512
        HC = 16
        w1, w1B = tb([128, KC, DFF // 2], BF16, "w1")
        w2, w2B = tb([128, HC, D], BF16, "w2")
        stg = mk_stage()
        xts = [tbl([128, KC, NF], F32, f"xt{i}_") for i in range(2)]
        zts = [tbl([128, KC, NF], F32, f"zt{i}_") for i in range(2)]
        xb, xbB = tb([128, KC, NF], BF16, "xb")
        hT, hTB = tbl([128, HC, NF], BF16, "hT", HC)
        rl = [tb([128, NF], F32, f"rl{i}") for i in range(2)]
        scr = ln_scratch(NF)
        tout = nc.alloc_sbuf_tensor_at(f"tout{li}", [128, 4, D], F32, offset=st["zb_off"])
        zt0, zt0B = zts[0]
        P.add("sp", lambda h: [h.dma_start(out=zt0[:, :, 0:128], in_=XD[src][:, :, T:T + 128])], reads=[XDB[src][T // 512]], writes=zt0B, dma=True)
        P.add("dve", lambda h: h.tensor_scalar(out=zt0[:, :, 128:256], in0=zt0[:, :, 0:128], scalar1=MK[:, 1:2], scalar2=None, op0=ALU.mult), reads=zt0B + [MKB], writes=zt0B)
        P.add("dve", lambda h: h.tensor_scalar(out=zt0[:, :, 0:128], in0=zt0[:, :, 0:128], scalar1=MK[:, 0:1], scalar2=None, op0=ALU.mult), reads=zt0B + [MKB], writes=zt0B)
        SI, SO = SX[0], SX[1]
        sview = lambda d: d.rearrange("p (c t) -> p c t", c=KC)
        P.add("sp", lambda h: [h.dma_start(out=sview(SI), in_=zt0[:, :, 0:256])], reads=zt0B, writes=[SXB[0]], dma=True)
        P.add("pool", lambda h: [h.collective_compute("AllReduce", ALU.add, replica_groups=CC_GROUPS, ins=[SI], outs=[SO])],
              reads=[SXB[0]], writes=[SXB[1]], dma=True, inc=1)
        load_w(I["w_up"][li], D, DFF // 2, w1, w1B, stg)
        load_w(I["w_dn"][li], DFF // 2, D, w2, w2B, stg)
        tiles = [(i * NF, NF) for i in range(T // NF)] + [(T, 256)]
        nt = len(tiles)

        def load_x(it):
            t0, N = tiles[it]
            xt, xtB = xts[it % 2]
            if t0 < T:
                x_load(src, t0, N, xt, xtB)
            else:
                P.add("sp", lambda h: [h.dma_start(out=xt[:, :, 0:256], in_=sview(SO))], reads=[SXB[1]], writes=xtB, dma=True)

        def front(it):
            t0, N = tiles[it]
            xt, xtB = xts[it % 2]
            k = it % NR
            P.add("dve", lambda h: h.tensor_copy(out=xb[:, :, 0:N], in_=xt[:, :, 0:N]), reads=xtB, writes=[xbB])
            for j in range(HC):
                pb = j % 2

                def up(h, j=j, pb=pb):
                    for kk in range(KC):
                        r = h.matmul(PS[pb][:, 0:N], lhsT=w1[:, kk, j * 128:(j + 1) * 128], rhs=xb[:, kk, 0:N], start=(kk == 0), stop=(kk == KC - 1))
                    return r
                P.add("pe", up, reads=[w1B, xbB], writes=[PSB[pb]])
                r_, rB = rl[j % 2]
                P.add("act", lambda h, r_=r_, pb=pb: h.activation(out=r_[:, 0:N], in_=PS[pb][:, 0:N], func=AF.Relu), reads=[PSB[pb]], writes=[rB])
                P.add("dve", lambda h, r_=r_, j=j: h.tensor_tensor(out=hT[:, j, 0:N], in0=r_[:, 0:N], in1=r_[:, 0:N], op=ALU.mult), reads=[rB], writes=[hTB[j]])
            for m in range(KC):
                pb = 2 + m % 2

                def dn(h, m=m, pb=pb):
                    for j in range(HC):
                        r = h.matmul(PS[pb][:, 0:N], lhsT=w2[:, j, m * 128:(m + 1) * 128], rhs=hT[:, j, 0:N], start=(j == 0), stop=(j == HC - 1))
                    return r
                P.add("pe", dn, reads=[w2B] + hTB, writes=[PSB[pb]])
                P.add("dve", lambda h, m=m, pb=pb: h.scalar_tensor_tensor(out=xt[:, m, 0:N], in0=xt[:, m, 0:N], scalar=ALPHA / 2, in1=PS[pb][:, 0:N],
                                                                          op0=ALU.mult, op1=ALU.add), reads=[xtB[m], PSB[pb]], writes=[xtB[m]])
            if N == NF:
                ci, co, ciB, coB = CCI[k], CCO[k], CCIB[k], CCOB[k]
            else:
                ci, co, ciB, coB = SX[2], SX[3], SXB[2], SXB[3]
            vi = ci.rearrange("p (c t) -> p c t", c=KC)
            P.add("sp", lambda h: [h.dma_start(out=vi, in_=xt[:, :, 0:N])], reads=xtB, writes=[ciB], dma=True)
            P.add("pool", lambda h, ci=ci, co=co: [h.collective_compute("AllReduce", ALU.add, replica_groups=CC_GROUPS, ins=[ci], outs=[co])],
                  reads=[ciB], writes=[coB], dma=True, inc=1)

        def back(it):
            t0, N = tiles[it]
            zt, ztB = zts[it % 2]
            k = it % NR
            co, coB = (CCO[k], CCOB[k]) if N == NF else (SX[3], SXB[3])
            vo = co.rearrange("p (c t) -> p c t", c=KC)
            P.add("sp", lambda h, N=N: [h.dma_start(out=zt[:, :, 0:N], in_=vo)], reads=[coB], writes=ztB, dma=True)
            if t0 >= T:
                P.add("dve", lambda h: h.tensor_scalar(out=zt[:, :, 0:128], in0=zt[:, :, 0:128], scalar1=MK[:, 0:1], scalar2=None, op0=ALU.mult), reads=ztB + [MKB], writes=ztB)
                P.add("dve", lambda h: h.scalar_tensor_tensor(out=zt[:, :, 0:128], in0=zt[:, :, 128:256], scalar=MK[:, 1:2], in1=zt[:, :, 0:128], op0=ALU.mult, op1=ALU.add),
                      reads=ztB + [MKB], writes=ztB)
                N = 128
            for _ in layer_norm_g(zt, ztB, N, li, 1, scr, (4, 5), pool_ok=False):
                pass
            if li != 3 or not FUSE_OUT:
                x_store(dst, t0, N, zt, ztB)
                return
            zbB_, zsqB_ = scr[1], scr[3]
            nb = N // 128
            for b in range(nb):
                for half in range(2):
                    pb = 6 if (b * 2 + half) % 2 == 0 else 4

                    def tr(h, b=b, half=half, pb=pb):
                        for c4 in range(4):
                            c = half * 4 + c4
                            r = h.transpose(out=PS[pb][:, c4 * 128:(c4 + 1) * 128], in_=zt[:, c, b * 128:(b + 1) * 128], identity=ident[:])
                        return r
                    P.add("pe", tr, reads=ztB[half * 4:half * 4 + 4] + [identB], writes=[PSB[pb]])
                    e = "dve" if half == 0 else "act"
                    P.add(e, copy_op(e, tout[:, b, half * 512:(half + 1) * 512], PS[pb][:, 0:512]), reads=[PSB[pb], zbB_, zsqB_], writes=[zbB_, zsqB_])
            if t0 < T:
                dd, dB = O["y_p"][t0:t0 + N, :], OB["y_p"]
            else:
                dd, dB = O["y_s"], OB["y_s"]
            dd = dd.rearrange("(b p) d -> p b d", p=128)
            P.add("sp", lambda h, dd=dd, nb=nb: [h.dma_start(out=dd, in_=tout[:, 0:nb, :])], reads=[zbB_, zsqB_], writes=[dB], dma=True)

        LAG = 2
        load_x(0)
        if nt > 1:
            load_x(1)
        for it in range(nt + LAG):
            if it < nt:
                front(it)
                if it + 2 < nt:
                    load_x(it + 2)
            if it - LAG >= 0:
                back(it - LAG)

    def phase_rg(l, li, src, dst):
        NM = 256
        win, winB = tb([128, KC, 2 * D], BF16, "win")
        gw, gwB = tb([128, 2, 4 * 512], BF16, "gw")
        wo, woB = tb([128, KC, D], BF16, "wo")
        stg = mk_stage()
        load_w(I["rg_in"][l], D, 2 * D, win, winB, stg)
        for n in range(4):
            load_w(I["rg_gw"][l, n], 256, 512, gw, gwB, stg, c0=n * 512)
        load_w(I["rg_out"][l], D, D, wo, woB, stg)
        xts = [tbl([128, KC, NM], F32, f"xt{i}_") for i in range(2)]
        xb, xbB = tb([128, KC, NM], BF16, "xb")
        ggs = [tbl([128, KC, NM], F32, f"gg{i}_") for i in range(2)]
        U, UB = tbl([128, KC, NM + 6], F32, "U")
        cv, cvB = tbl([128, KC, NM], F32, "cv")
        cvb, cvbB = tbl([128, KC, NM], BF16, "cvb")
        rgs = [tbl([128, KC, NM], F32, f"r{i}_") for i in range(2)]
        igs = [tbl([128, KC, NM], F32, f"i{i}_") for i in range(2)]
        sq, sqB = tbl([128, KC, NM], F32, "sq")
        hh, hhB = tbl([128, KC, NM], F32, "h")
        mb, mbB = tbl([128, KC, NM], BF16, "mb")
        hc, hcB = tbl([128, KC, 2], F32, "hc")
        scr = ln_scratch(NM)
        P.add("pool", lambda h: h.memset(hc[:], 0.0), writes=hcB)
        P.add("pool", lambda h: h.memset(U[:], 0.0), writes=UB)
        tiles = mixer_tiles()

        def info(it):
            t0, N = tiles[it]
            samp = t0 >= T
            segs = [(0, N, 0)] if not samp else [(0, 64, 0), (64, 64, 67)]
            return t0, N, samp, segs

        def A1(it):
            t0, N, samp, segs = info(it)
            xt, xtB = xts[it % 2]
            gg, ggB = ggs[it % 2]
            P.add("dve", lambda h: h.tensor_copy(out=xb[:, :, 0:N], in_=xt[:, :, 0:N]), reads=xtB, writes=[xbB])
            if samp:
                for s_ in range(2):
                    for c in range(KC):
                        for k in range(3):
                            col = C_CV + ((l * 2 + s_) * 3 + k) * 8 + c
                            P.add("pool", lambda h, s_=s_, c=c, k=k, col=col: h.tensor_copy(out=U[:, c, s_ * 67 + k:s_ * 67 + k + 1], in_=PT[:, col:col + 1]),
                                  reads=[PTB, UB[c]], writes=[UB[c]])
            for c in range(KC):
                pb = c % 2

                def g1(h, c=c, pb=pb):
                    for k in range(KC):
                        r = h.matmul(PS[pb][:, 0:N], lhsT=win[:, k, c * 128:(c + 1) * 128], rhs=xb[:, k, 0:N], start=(k == 0), stop=(k == KC - 1))
                    return r
                P.add("pe", g1, reads=[winB, xbB], writes=[PSB[pb]])
                P.add("act", lambda h, c=c, pb=pb: h.activation(out=gg[:, c, 0:N], in_=PS[pb][:, 0:N], func=AF.Gelu_apprx_tanh), reads=[PSB[pb]], writes=[ggB[c]])
                if c % 2 == 1:
                    yield
            for c in range(KC):
                pb = 2 + c % 2

                def u1(h, c=c, pb=pb):
                    for k in range(KC):
                        r = h.matmul(PS[pb][:, 0:N], lhsT=win[:, k, D + c * 128:D + (c + 1) * 128], rhs=xb[:, k, 0:N], start=(k == 0), stop=(k == KC - 1))
                    return r
                P.add("pe", u1, reads=[winB, xbB], writes=[PSB[pb]])
                for (s0, L, u0) in segs:
                    P.add("dve", lambda h, c=c, pb=pb, s0=s0, L=L, u0=u0: h.tensor_copy(out=U[:, c, u0 + 3:u0 + 3 + L], in_=PS[pb][:, s0:s0 + L]),
                          reads=[PSB[pb], UB[c]], writes=[UB[c]])
                for (s0, L, u0) in segs:
                    cw = [PT[:, C_CW + (l * 4 + k) * 8 + c:C_CW + (l * 4 + k) * 8 + c + 1] for k in range(4)]
                    cbv = PT[:, C_CB + l * 8 + c:C_CB + l * 8 + c + 1]
                    P.add("pool", lambda h, c=c, s0=s0, L=L, u0=u0, cw=cw, cbv=cbv: h.tensor_scalar(out=cv[:, c, s0:s0 + L], in0=U[:, c, u0:u0 + L], scalar1=cw[0], scalar2=cbv,
                                                                                                      op0=ALU.mult, op1=ALU.add), reads=[UB[c], PTB], writes=[cvB[c]])
                    for k in range(1, 4):
                        P.add("dve", lambda h, c=c, s0=s0, L=L, u0=u0, k=k, cw=cw: h.scalar_tensor_tensor(out=cv[:, c, s0:s0 + L], in0=U[:, c, u0 + k:u0 + k + L], scalar=cw[k],
                                                                                                            in1=cv[:, c, s0:s0 + L], op0=ALU.mult, op1=ALU.add),
                              reads=[UB[c], PTB, cvB[c]], writes=[cvB[c]])
                P.add("pool", lambda h, c=c: h.tensor_copy(out=cvb[:, c, 0:N], in_=cv[:, c, 0:N]), reads=[cvB[c]], writes=[cvbB[c]])
                yield
            last_prompt = (t0 + N == T)
            if last_prompt or samp:
                for (si, (s0, L, u0)) in enumerate(segs):
                    for k in range(3):
                        dd = (O["rgc_p"][l, k, :] if not samp else O["rgc_s"][l, si, k, :]).rearrange("(c p) -> p c", p=128)
                        dB = OB["rgc_p"] if not samp else OB["rgc_s"]
                        P.add("sp", lambda h, dd=dd, u0=u0, L=L, k=k: [h.dma_start(out=dd, in_=U[:, :, u0 + L + k], allow_slow_non_contiguous=True)],
                              reads=UB, writes=[dB], dma=True)
            if not samp:
                P.add("pool", lambda h: h.tensor_copy(out=U[:, :, 0:3], in_=U[:, :, N:N + 3]), reads=UB, writes=UB)

        def A2(it):
            t0, N, samp, segs = info(it)
            rg_, rgB = rgs[it % 2]
            ig, igB = igs[it % 2]
            for n in range(4):
                for oc in range(4):
                    pb = (n * 4 + oc) % 2

                    def gt(h, n=n, oc=oc, pb=pb):
                        for kk in range(2):
                            r = h.matmul(PS[pb][:, 0:N], lhsT=gw[:, kk, n * 512 + oc * 128:n * 512 + (oc + 1) * 128], rhs=cvb[:, 2 * n + kk, 0:N], start=(kk == 0), stop=(kk == 1))
                        return r
                    P.add("pe", gt, reads=[gwB, cvbB[2 * n], cvbB[2 * n + 1]], writes=[PSB[pb]])
                    tgt, tgtB = (rg_, rgB) if oc < 2 else (ig, igB)
                    cc = 2 * n + (oc % 2)
                    gb = PT[:, C_GB + l * 16 + n * 4 + oc:C_GB + l * 16 + n * 4 + oc + 1]
                    P.add("act", lambda h, tgt=tgt, cc=cc, pb=pb, gb=gb: h.activation(out=tgt[:, cc, 0:N], in_=PS[pb][:, 0:N], func=AF.Sigmoid, bias=gb), reads=[PSB[pb], PTB], writes=[tgtB[cc]])
                    if oc % 2 == 1:
                        yield
            for c in range(KC):
                cvb2 = cvec2[:, l * 8 + c:l * 8 + c + 1]
                P.add("act", lambda h, c=c, cvb2=cvb2: h.activation(out=sq[:, c, 0:N], in_=rg_[:, c, 0:N], func=AF.Exp, scale=cvb2), reads=[rgB[c], cvec2B], writes=[sqB[c]])
                P.add("pool", lambda h, c=c: h.tensor_tensor(out=ig[:, c, 0:N], in0=ig[:, c, 0:N], in1=cv[:, c, 0:N], op=ALU.mult), reads=[igB[c], cvB[c]], writes=[igB[c]])
                if c % 2 == 1:
                    yield
            for c in range(KC):
                cva = cvec[:, l * 8 + c:l * 8 + c + 1]
                P.add("act", lambda h, c=c, cva=cva: h.activation(out=rg_[:, c, 0:N], in_=rg_[:, c, 0:N], func=AF.Exp, scale=cva), reads=[rgB[c], cvecB], writes=[rgB[c]])
                if c % 2 == 1:
                    yield
            for c in range(KC):
                P.add("act", lambda h, c=c: h.activation(out=sq[:, c, 0:N], in_=sq[:, c, 0:N], func=AF.Sqrt, scale=-1.0, bias=1.0), reads=[sqB[c]], writes=[sqB[c]])
                P.add("pool", lambda h, c=c: h.tensor_tensor(out=ig[:, c, 0:N], in0=ig[:, c, 0:N], in1=sq[:, c, 0:N], op=ALU.mult), reads=[igB[c], sqB[c]], writes=[igB[c]])
                if c % 2 == 1:
                    yield

        def B(it):
            t0, N, samp, segs = info(it)
            xt, xtB = xts[it % 2]
            gg, ggB = ggs[it % 2]
            rg_, rgB = rgs[it % 2]
            ig, igB = igs[it % 2]
            last_prompt = (t0 + N == T)
            for c in range(KC):
                for (si, (s0, L, u0)) in enumerate(segs):
                    if samp:
                        init = PT[:, C_H0 + (l * 2 + si) * 8 + c:C_H0 + (l * 2 + si) * 8 + c + 1]
                        rds = [rgB[c], igB[c], PTB]
                    else:
                        init = hc[:, c, 0:1]
                        rds = [rgB[c], igB[c], hcB[c]]
                    P.add("dve", lambda h, c=c, s0=s0, L=L, init=init: h.tensor_tensor_scan(out=hh[:, c, s0:s0 + L], data0=rg_[:, c, s0:s0 + L], data1=ig[:, c, s0:s0 + L],
                                                                                           initial=init, op0=ALU.mult, op1=ALU.add), reads=rds, writes=[hhB[c]])
                if not samp:
                    P.add("dve", lambda h, c=c: h.tensor_copy(out=hc[:, c, 0:1], in_=hh[:, c, N - 1:N]), reads=[hhB[c], hcB[c]], writes=[hcB[c]])
                P.add("pool", lambda h, c=c: h.tensor_tensor(out=mb[:, c, 0:N], in0=hh[:, c, 0:N], in1=gg[:, c, 0:N], op=ALU.mult), reads=[hhB[c], ggB[c]], writes=[mbB[c]])
                if c % 2 == 1:
                    yield
            if last_prompt or samp:
                for (si, (s0, L, u0)) in enumerate(segs):
                    dd = (O["rgh_p"][l, :] if not samp else O["rgh_s"][l, si, :]).rearrange("(c p) -> p c", p=128)
                    dB = OB["rgh_p"] if not samp else OB["rgh_s"]
                    P.add("sp", lambda h, dd=dd, s0=s0, L=L: [h.dma_start(out=dd, in_=hh[:, :, s0 + L - 1], allow_slow_non_contiguous=True)], reads=hhB, writes=[dB], dma=True)
            for m in range(KC):
                pb = 4 + m % 2

                def op_(h, m=m, pb=pb):
                    for k in range(KC):
                        r = h.matmul(PS[pb][:, 0:N], lhsT=wo[:, k, m * 128:(m + 1) * 128], rhs=mb[:, k, 0:N], start=(k == 0), stop=(k == KC - 1))
                    return r
                P.add("pe", op_, reads=[woB] + mbB, writes=[PSB[pb]])
                P.add("dve", lambda h, m=m, pb=pb: h.scalar_tensor_tensor(out=xt[:, m, 0:N], in0=xt[:, m, 0:N], scalar=ALPHA, in1=PS[pb][:, 0:N],
                                                                          op0=ALU.mult, op1=ALU.add), reads=[xtB[m], PSB[pb]], writes=[xtB[m]])
                if m % 2 == 1:
                    yield
            yield from layer_norm_g(xt, xtB, N, li, 0, scr, (6, 5))
            x_store(dst, t0, N, xt, xtB)

        def A(it):
            yield from A1(it)
            yield from A2(it)

        def interleave(*gens):
            gens = list(gens)
            while gens:
                for g_ in list(gens):
                    try:
                        next(g_)
                    except StopIteration:
                        gens.remove(g_)

        nt = len(tiles)
        x_load(src, tiles[0][0], tiles[0][1], *xts[0])
        if nt > 1:
            x_load(src, tiles[1][0], tiles[1][1], *xts[1])
        interleave(A(0))
        for it in range(nt):
            if it + 1 < nt:
                interleave(A(it + 1), B(it))
            else:
                interleave(B(it))
            if it + 2 < nt:
                x_load(src, tiles[it + 2][0], tiles[it + 2][1], *xts[it % 2])

    def out_proj_ln(li, wo, woB, Osb, OsbB, oblocks, xt, xtB, N, scr, OT, OTB, nkc=KC, rscale=ALPHA, do_ln=True):
        for bi, (ob, nr, c0) in enumerate(oblocks):
            def tr(h, ob=ob, nr=nr):
                for c in range(nkc):
                    r = h.transpose(out=PSH[:, c * 128:c * 128 + nr], in_=Osb[0:nr, ob, c * 128:(c + 1) * 128], identity=identh[0:nr, 0:nr])
                return r
            P.add("pe", tr, reads=[OsbB, identhB], writes=[PSHB])
            e = "dve" if bi % 2 == 0 else "act"
            P.add(e, copy_op(e, OT[:, 0:nkc, c0:c0 + nr], PSH[:].rearrange("p (c t) -> p c t", c=KC)[:, 0:nkc, 0:nr]), reads=[PSHB, OTB], writes=[OTB])
        for m in range(KC):
            pb = m % 2

            def op_(h, m=m, pb=pb):
                for k in range(nkc):
                    r = h.matmul(PS[pb][:, 0:N], lhsT=wo[:, k, m * 128:(m + 1) * 128], rhs=OT[:, k, 0:N], start=(k == 0), stop=(k == nkc - 1))
                return r
            P.add("pe", op_, reads=[woB, OTB], writes=[PSB[pb]])
            P.add("dve", lambda h, m=m, pb=pb: h.scalar_tensor_tensor(out=xt[:, m, 0:N], in0=xt[:, m, 0:N], scalar=rscale, in1=PS[pb][:, 0:N],
                                                                      op0=ALU.mult, op1=ALU.add), reads=[xtB[m], PSB[pb]], writes=[xtB[m]])
        if do_ln:
            layer_norm(xt, xtB, N, li, 0, scr, (2, 3))

    def phase_swa(li, src, dst):
        NM = 256
        CUT = int(os.environ.get("KSWA_CUT", "9"))
        SKIP = os.environ.get("KSWA_SKIP", "").split(",")

        def stub_tail(xt, xtB, N, t0):
            P.add("dve", lambda h: h.tensor_scalar(out=xt[:, :, 0:N], in0=xt[:, :, 0:N], scalar1=ALPHA, scalar2=None, op0=ALU.mult), reads=xtB, writes=xtB)
            layer_norm(xt, xtB, N, li, 0, scr, (2, 3))
            x_store(dst, t0, N, xt, xtB)
        wq, wqB = tb([128, KC, 1536], BF16, "wqkv")
        wo, woB = tb([128, KC, D], BF16, "wo")
        stg = mk_stage()
        load_w(I["sw_qkv"][0], D, 1536, wq, wqB, stg)
        load_w(I["sw_out"][0], D, D, wo, woB, stg)
        bands, bandsB = tb([128, 2 * 33 * 128], BF16, "bands")
        TBt, TBB = tb([128, 512], F32, "table")
        esk, eskB = tb([128, 16], F32, "esink")
        EBf, EBfB = tb([128, 2, 16, 128], F32, "EBf")
        EB, EBB = tb([128, 2, 16, 128], BF16, "EB")
        P.add("sp", lambda h: [h.dma_start(out=bands[:], in_=I["bands"])], writes=[bandsB], dma=True)
        P.add("sp", lambda h: [h.dma_start(out=TBt[:], in_=I["table"].to_broadcast([128, 512]))], writes=[TBB], dma=True)
        P.add("sp", lambda h: [h.dma_start(out=esk[:], in_=I["sinks"].to_broadcast([128, 16]))], writes=[eskB], dma=True)
        P.add("act", lambda h: h.activation(out=esk[:], in_=esk[:], func=AF.Exp), reads=[eskB], writes=[eskB])
        EBk = [[Buf(f"EBf{k}_{h_}") for h_ in range(16)] for k in range(2)]
        EBall = [x for r_ in EBk for x in r_]
        P.add("pool", lambda h: h.memset(EBf[:], 0.0), writes=EBall)
        _bn = _swa_bands()
        for b in range(33):
            for kind in range(2):
                if not _bn[kind, b].any():
                    continue
                bm = bands[:, (kind * 33 + b) * 128:(kind * 33 + b + 1) * 128]
                for hd in range(16):
                    sc = TBt[:, b * 16 + hd:b * 16 + hd + 1] if b < 32 else NEG
                    P.add("dve", lambda h, kind=kind, hd=hd, bm=bm, sc=sc: h.scalar_tensor_tensor(out=EBf[:, kind, hd, :], in0=bm, scalar=sc, in1=EBf[:, kind, hd, :],
                                                                                                 op0=ALU.mult, op1=ALU.add), reads=[bandsB, TBB, EBk[kind][hd]], writes=[EBk[kind][hd]])
        P.add("act", lambda h: h.activation(out=EB[:].rearrange("p a b c -> p (a b c)"), in_=EBf[:].rearrange("p a b c -> p (a b c)"), func=AF.Exp), reads=EBall, writes=[EBB])

        xts = [tbl([128, KC, NM], F32, f"xt{i}_") for i in range(2)]
        xb, xbB = tb([128, KC, NM], BF16, "xb")
        QT, QTB = tb([128, 16, NM], BF16, "QT")
        KT, KTB = tb([128, 2, 128 + NM], BF16, "KT")
        KT2, KT2B = tb([128, 2, 128 + NM], BF16, "KT2")
        VA, VAB = tb([128, 5, 4, 65], BF16, "VA")
        kvo, kvoB = tb([128, 512], F32, "kvo")
        ctm, ctmB = tb([128, 512], F32, "ctm")
        PTs = [tb([128, 512], BF16, f"pt{i}") for i in range(4)]
        pi_state = {"n": 0}
        Osb, OsbB = tb([128, 4, D], BF16, "Osb")
        OT, OTB = tb([128, KC, NM], BF16, "OT")
        rec, recB = tb([128, 16], F32, "rec")
        scr = ln_scratch(NM)
        P.add("pool", lambda h: h.memset(VA[:], 1.0), writes=[VAB])
        P.add("pool", lambda h: h.memset(KT[:], 0.0), writes=[KTB])
        P.add("pool", lambda h: h.memset(KT2[:], 0.0), writes=[KT2B])
        P.add("pool", lambda h: h.memset(QT[:], 0.0), writes=[QTB])
        wk2, wk2B = tb([128, KC, 256], BF16, "wk2")
        for g in range(4):
            g2 = g ^ 1
            e = cast_eng()
            P.add(e, copy_op(e, wk2[:, :, g * 64:(g + 1) * 64], wq[:, :, D + g2 * 64:D + (g2 + 1) * 64]), reads=[wqB], writes=[wk2B])
        tiles = mixer_tiles()
        x_load(src, tiles[0][0], tiles[0][1], *xts[0])
        pi = 0
        for it, (t0, N) in enumerate(tiles):
            xt, xtB = xts[it % 2]
            if it + 1 < len(tiles):
                x_load(src, tiles[it + 1][0], tiles[it + 1][1], *xts[(it + 1) % 2])
            samp = t0 >= T
            nb = N // 128
            if CUT < 1:
                stub_tail(xt, xtB, N, t0)
                continue
            P.add("pool", lambda h, xt=xt, N=N: h.tensor_copy(out=xb[:, :, 0:N], in_=xt[:, :, 0:N]), reads=xtB, writes=[xbB])
            for c in range(KC):
                pb = c % 2

                def qp(h, c=c, pb=pb, N=N):
                    for k in range(KC):
                        r = h.matmul(PS[pb][:, 0:N], lhsT=wq[:, k, c * 128:(c + 1) * 128], rhs=xb[:, k, 0:N], start=(k == 0), stop=(k == KC - 1))
                    return r
                P.add("pe", qp, reads=[wqB, xbB], writes=[PSB[pb]])
                P.add("act", copy_op("act", QT[0:64, 2 * c, 0:N], PS[pb][0:64, 0:N]), reads=[PSB[pb], QTB], writes=[QTB])
                P.add("dve", copy_op("dve", QT[64:128, 2 * c + 1, 0:N], PS[pb][64:128, 0:N]), reads=[PSB[pb], QTB], writes=[QTB])
            if not samp:
                for (wsrc, wB, c0w, kt, ktB) in (((wq, wqB, D, KT, KTB), (wk2, wk2B, 0, KT2, KT2B)) if "kp" not in SKIP else ()):
                    for c in range(2):
                        pb = 2 + c

                        def kp(h, c=c, pb=pb, wsrc=wsrc, c0w=c0w, N=N):
                            for k in range(KC):
                                r = h.matmul(PS[pb][:, 0:N], lhsT=wsrc[:, k, c0w + c * 128:c0w + (c + 1) * 128], rhs=xb[:, k, 0:N], start=(k == 0), stop=(k == KC - 1))
                            return r
                        P.add("pe", kp, reads=[wB, xbB], writes=[PSB[pb]])
                        P.add("dve", lambda h, c=c, pb=pb, kt=kt, N=N: h.tensor_copy(out=kt[:, c, 128:128 + N], in_=PS[pb][:, 0:N]), reads=[PSB[pb], ktB], writes=[ktB])
                for b in range(nb):
                    def kvp(h, b=b):
                        for k in range(KC):
                            r = h.matmul(PS[4][:, 0:512], lhsT=xb[:, k, b * 128:(b + 1) * 128], rhs=wq[:, k, D:D + 512], start=(k == 0), stop=(k == KC - 1))
                        return r
                    if "kvp" in SKIP:
                        continue
                    P.add("pe", kvp, reads=[wqB, xbB], writes=[PSB[4]])
                    if "va" not in SKIP:
                        P.add("act", lambda h, b=b: h.activation(out=VA[:, 1 + b, :, 0:64], in_=PS[4][:, 256:512].rearrange("p (g d) -> p g d", g=4), func=AF.Copy),
                              reads=[PSB[4], VAB], writes=[VAB])
                    if t0 + (b + 1) * 128 == T and "kvo" not in SKIP:
                        P.add("dve", lambda h: h.tensor_copy(out=kvo[:], in_=PS[4][:, 0:512]), reads=[PSB[4]], writes=[kvoB])
                        P.add("sp", lambda h: [h.dma_start(out=O["swk_p"], in_=kvo[:, 0:256])], reads=[kvoB], writes=[OB["swk_p"]], dma=True)
                        P.add("sp", lambda h: [h.dma_start(out=O["swv_p"], in_=kvo[:, 256:512])], reads=[kvoB], writes=[OB["swv_p"]], dma=True)
                segl = [(0, nb, None)]
            else:
                segl = [(0, 1, 0), (0, 1, 1)]
            if CUT < 2 or (CUT == 2 and samp):
                stub_tail(xt, xtB, N, t0)
                continue
            for (sg_i, (q0blk, nqb, sseq)) in enumerate(segl):
                if samp:
                    s = sseq
                    P.add("sp", lambda h, s=s: [h.dma_start(out=ctm[:, 0:256], in_=I["cswk"][s]), h.dma_start(out=ctm[:, 256:512], in_=I["cswv"][s])],
                          writes=[ctmB], dma=True, ndma=2)
                    P.add("act", lambda h: h.activation(out=VA[:, 0, :, 0:64], in_=ctm[:, 256:512].rearrange("p (g d) -> p g d", g=4), func=AF.Copy), reads=[ctmB, VAB], writes=[VAB])
                    def trk(h):
                        for c in range(2):
                            r = h.transpose(out=PS[2][:, c * 128:(c + 1) * 128], in_=ctm[:, c * 128:(c + 1) * 128], identity=ident[:])
                        return r
                    P.add("pe", trk, reads=[ctmB, identB], writes=[PSB[2]])
                    P.add("dve", lambda h: h.tensor_copy(out=KT[:, :, 0:128], in_=PS[2][:, 0:256].rearrange("p (c t) -> p c t", c=2)), reads=[PSB[2], KTB], writes=[KTB])
                    for g in range(4):
                        g2 = g ^ 1
                        P.add("pool", lambda h, g=g, g2=g2: h.tensor_copy(out=kvo[:, g * 64:(g + 1) * 64], in_=ctm[:, g2 * 64:(g2 + 1) * 64]), reads=[ctmB, kvoB], writes=[kvoB])

                    def trk2(h):
                        for c in range(2):
                            r = h.transpose(out=PS[3][:, c * 128:(c + 1) * 128], in_=kvo[:, c * 128:(c + 1) * 128], identity=ident[:])
                        return r
                    P.add("pe", trk2, reads=[kvoB, identB], writes=[PSB[3]])
                    P.add("dve", lambda h: h.tensor_copy(out=KT2[:, :, 0:128], in_=PS[3][:, 0:256].rearrange("p (c t) -> p c t", c=2)), reads=[PSB[3], KT2B], writes=[KT2B])
                    for (wsrc, wB, c0w, kt, ktB) in ((wq, wqB, D, KT, KTB), (wk2, wk2B, 0, KT2, KT2B)):
                        for c in range(2):
                            pb = 2 + c

                            def kp(h, c=c, pb=pb, wsrc=wsrc, c0w=c0w, s=s):
                                for k in range(KC):
                                    r = h.matmul(PS[pb][:, 0:64], lhsT=wsrc[:, k, c0w + c * 128:c0w + (c + 1) * 128], rhs=xb[:, k, s * 64:(s + 1) * 64], start=(k == 0), stop=(k == KC - 1))
                                return r
                            P.add("pe", kp, reads=[wB, xbB], writes=[PSB[pb]])
                            P.add("dve", lambda h, c=c, pb=pb, kt=kt: h.tensor_copy(out=kt[:, c, 128:192], in_=PS[pb][:, 0:64]), reads=[PSB[pb], ktB], writes=[ktB])

                    def kvp(h, s=s):
                        for k in range(KC):
                            r = h.matmul(PS[4][0:64, 0:512], lhsT=xb[:, k, s * 64:(s + 1) * 64], rhs=wq[:, k, D:D + 512], start=(k == 0), stop=(k == KC - 1))
                        return r
                    P.add("pe", kvp, reads=[wqB, xbB], writes=[PSB[4]])
                    P.add("act", lambda h: h.activation(out=VA[0:64, 1, :, 0:64], in_=PS[4][0:64, 256:512].rearrange("p (g d) -> p g d", g=4), func=AF.Copy),
                          reads=[PSB[4], VAB], writes=[VAB])
                    P.add("dve", lambda h: h.tensor_copy(out=kvo[0:64, :], in_=PS[4][0:64, 0:512]), reads=[PSB[4], kvoB], writes=[kvoB])
                    P.add("sp", lambda h, s=s: [h.dma_start(out=O["swk_s"][s, 64:128, :], in_=kvo[0:64, 0:256]), h.dma_start(out=O["swk_s"][s, 0:64, :], in_=ctm[64:128, 0:256])],
                          reads=[kvoB, ctmB], writes=[OB["swk_s"]], dma=True, ndma=2)
                    P.add("sp", lambda h, s=s: [h.dma_start(out=O["swv_s"][s, 64:128, :], in_=kvo[0:64, 256:512]), h.dma_start(out=O["swv_s"][s, 0:64, :], in_=ctm[64:128, 256:512])],
                          reads=[kvoB, ctmB], writes=[OB["swv_s"]], dma=True, ndma=2)
                tasks = []
                for qb in range(nqb):
                    if samp:
                        qc0, nq = sseq * 64, 64
                        kblocks = [(1, 0, 128, 0), (0, 128, 64, 1)]
                    else:
                        qc0, nq = qb * 128, 128
                        kblocks = [(1, qb * 128, 128, qb), (0, 128 + qb * 128, 128, qb + 1)]
                    if (not samp) and t0 == 0 and qb == 0:
                        kblocks = kblocks[1:]
                    for g in range(4):
                        for ki, (kind, kc0, nk, vb) in enumerate(kblocks):
                            tasks.append(dict(qb=qb, g=g, ki=ki, kind=kind, kc0=kc0, nk=nk, vb=vb, qc0=qc0, nq=nq, nkb=len(kblocks)))
                LA = 2
                unit = {"n": 0}

                def emit_front(i):
                    tk = tasks[i]
                    g, kc0, nk, qc0, nq, kind = tk["g"], tk["kc0"], tk["nk"], tk["qc0"], tk["nq"], tk["kind"]
                    pb = (2, 5, 6)[pi_state["n"] % 3]
                    pt, ptB = PTs[pi_state["n"] % 4]
                    pi_state["n"] += 1
                    tk["pt"] = (pt, ptB)

                    def st_(h, g=g, kc0=kc0, nk=nk, pb=pb, qc0=qc0, nq=nq):
                        for j in range(4):
                            hd = 4 * g + j
                            half = hd % 2
                            ktile = KT if (g % 2) == half else KT2
                            r = h.matmul(PS[pb][0:nk, j * nq:(j + 1) * nq], lhsT=ktile[:, g // 2, kc0:kc0 + nk],
                                         rhs=QT[:, hd, qc0:qc0 + nq], start=True, stop=True)
                        return r
                    P.add("pe", st_, reads=[KTB, KT2B, QTB], writes=[PSB[pb]])
                    P.add("act", lambda h, pb=pb, pt=pt, nk=nk, nq=nq: h.activation(out=pt[0:nk, 0:4 * nq], in_=PS[pb][0:nk, 0:4 * nq], func=AF.Exp, scale=SCALE), reads=[PSB[pb]], writes=[ptB])
                    ebv = EB[0:nk, kind, 4 * g:4 * g + 4, 0:nq]
                    P.add("pool", lambda h, pt=pt, nk=nk, nq=nq, ebv=ebv: h.tensor_tensor(out=pt[0:nk, 0:4 * nq].rearrange("p (j q) -> p j q", j=4),
                                                                                          in0=pt[0:nk, 0:4 * nq].rearrange("p (j q) -> p j q", j=4), in1=ebv, op=ALU.mult),
                          reads=[ptB, EBB], writes=[ptB])

                def emit_back(i):
                    tk = tasks[i]
                    g, ki, nk, nq, vb, nkb, qb = tk["g"], tk["ki"], tk["nk"], tk["nq"], tk["vb"], tk["nkb"], tk["qb"]
                    pt, ptB = tk["pt"]
                    if ki == 0:
                        unit["n"] += 1
                    acc = 3 + unit["n"] % 2

                    def pv(h, g=g, ki=ki, pt=pt, nk=nk, nq=nq, vb=vb, nkb=nkb, acc=acc):
                        for j in range(4):
                            r = h.matmul(PS[acc][0:nq, j * 65:(j + 1) * 65], lhsT=pt[0:nk, j * nq:(j + 1) * nq], rhs=VA[0:nk, vb, g, :],
                                         start=(ki == 0 and j == 0), stop=(ki == nkb - 1 and j == 3), skip_group_check=True)
                        return r
                    P.add("pe", pv, reads=[ptB, VAB], writes=[PSB[acc]])
                    if ki == nkb - 1:
                        ob = qb if not samp else sseq
                        pv4 = PS[acc][0:nq, 0:260].rearrange("p (j e) -> p j e", j=4)
                        P.add("dve", lambda h, g=g, nq=nq, pv4=pv4: h.tensor_tensor(out=rec[0:nq, 4 * g:4 * g + 4], in0=pv4[:, :, 64], in1=esk[0:nq, 4 * g:4 * g + 4], op=ALU.add),
                              reads=[PSB[acc], eskB, recB], writes=[recB])
                        P.add("dve", lambda h, g=g, nq=nq: h.reciprocal(out=rec[0:nq, 4 * g:4 * g + 4], in_=rec[0:nq, 4 * g:4 * g + 4]), reads=[recB], writes=[recB])
                        for j in range(4):
                            hd = 4 * g + j
                            P.add("act", lambda h, j=j, hd=hd, nq=nq, ob=ob, pv4=pv4: h.activation(out=Osb[0:nq, ob, hd * 64:(hd + 1) * 64], in_=pv4[:, j, 0:64], func=AF.Copy,
                                                                                                    scale=rec[0:nq, hd:hd + 1]),
                                  reads=[PSB[acc], recB, OsbB], writes=[OsbB])

                for i in range(min(LA, len(tasks))):
                    emit_front(i)
                for i in range(len(tasks)):
                    if i + LA < len(tasks):
                        emit_front(i + LA)
                    emit_back(i)
            if not samp:
                P.add("pool", lambda h, N=N: h.tensor_copy(out=KT[:, :, 0:128], in_=KT[:, :, N:N + 128]), reads=[KTB], writes=[KTB])
                P.add("pool", lambda h, N=N: h.tensor_copy(out=KT2[:, :, 0:128], in_=KT2[:, :, N:N + 128]), reads=[KT2B], writes=[KT2B])
                P.add("pool", lambda h, nb=nb: h.tensor_copy(out=VA[:, 0, :, :], in_=VA[:, nb, :, :]), reads=[VAB], writes=[VAB])
            if CUT < 4:
                stub_tail(xt, xtB, N, t0)
                continue
            obl = [(b, 128, b * 128) for b in range(nb)] if not samp else [(0, 64, 0), (1, 64, 64)]
            out_proj_ln(li, wo, woB, Osb, OsbB, obl, xt, xtB, N, scr, OT, OTB)
            x_store(dst, t0, N, xt, xtB)

    def phase_fox(li, src, dst):
        NM = 256
        PK = PAST + 64
        TK = T + 2 * PK
        NVB = NBP + 2 * (KB + 1)
        QD = dscr("QD", [128, 16, TT], BF16)
        KD = dscr("KD", [128, KC, TK], BF16)
        VD = dscr("VD", [NVB, 128, 16 * 65], BF16)
        QDB, KDB, VDB = Buf("QD", True), Buf("KD", True), Buf("VD", True)
        CKT, CKTB = tb([128, NVB, 16], F32, "CKT")
        CS, CSB = tb([96, TT], BF16, "CS")
        sel, selB = tb([96, 16, 128], BF16, "sel")
        tri, triB = tb([128, 128], BF16, "tri")
        nbf, nbfB = tb([96, 1], F32, "nbf")
        P.add("sp", lambda h: [h.dma_start(out=sel[:].rearrange("p a b -> p (a b)"), in_=I["sel"])], writes=[selB], dma=True)
        P.add("sp", lambda h: [h.dma_start(out=tri[:], in_=I["tri"])], writes=[triB], dma=True)
        P.add("pool", lambda h: h.memset(nbf[:], 0.0), writes=[nbfB])
        P.add("pool", lambda h: h.memset(CS[:], 0.0), writes=[CSB])
        for s3 in range(3):
            P.add("sp", lambda h, s3=s3: [h.dma_start(out=nbf[32 * s3:32 * s3 + 16, :], in_=I["fx_bf"])], reads=[nbfB], writes=[nbfB], dma=True)
        P.add("dve", lambda h: h.tensor_scalar(out=nbf[:], in0=nbf[:], scalar1=-1.0, scalar2=None, op0=ALU.mult), reads=[nbfB], writes=[nbfB])
        wo, woB = tb([128, KC, D], BF16, "wo")
        mark2 = st["off"]

        wqk, wqkB = tb([128, KC, 2048], BF16, "wqk")
        wv, wvB = tb([128, KC, 1024], BF16, "wv")
        wf3, wf3B = tb([128, KC, 96], BF16, "wf3")
        wfs, wfsB = tb([128, KC, 16], F32, "wfs")
        stg = mk_stage()
        load_w(I["fx_in"][0][:, 0:2048], D, 2048, wqk, wqkB, stg)
        load_w(I["fx_in"][0][:, 2048:3072], D, 1024, wv, wvB, stg)
        load_w(I["fx_out"][0], D, D, wo, woB, stg)
        P.add("sp", lambda h: [h.dma_start(out=wfs[:], in_=I["fx_in"][0][:, 3072:3088].rearrange("(k p) m -> p k m", p=128))], writes=[wfsB], dma=True)
        P.add("pool", lambda h: h.memset(wf3[:], 0.0), writes=[wf3B])
        for s3 in range(3):
            P.add("dve", lambda h, s3=s3: h.tensor_copy(out=wf3[:, :, 32 * s3:32 * s3 + 16], in_=wfs[:]), reads=[wfsB, wf3B], writes=[wf3B])
        xts = [tbl([128, KC, NM], F32, f"xt{i}_") for i in range(2)]
        xb, xbB = tb([128, KC, NM], BF16, "xb")
        QTm, QTmB = tb([128, 16, NM], BF16, "QTm")
        KTt, KTtB = tb([128, KC, NM], BF16, "KTt")
        kvo = [tb([128, 1024], F32, f"kvo{i}") for i in range(2)]
        VAt, VAtB = tb([128, 16, 65], BF16, "VAt")
        LF, LFB = tb([96, NM], F32, "LF")
        C3, C3B = tb([96, NM], F32, "C3")
        HI, HIB = tb([96, NM], BF16, "HI")
        R1, R1B = tb([96, NM], F32, "R1")
        ones96, ones96B = tb([96, NM], F32, "ones96")
        ccar, ccarB = tb([96, 1], F32, "ccar")
        tmo, tmoB = tb([128, 96], F32, "tmo")
        ctm, ctmB = tb([128, 1024], F32, "ctm")
        lfc, lfcB = tb([128, 96], F32, "lfc")
        P.add("pool", lambda h: h.memset(QTm[:], 0.0), writes=[QTmB])
        P.add("pool", lambda h: h.memset(VAt[:], 1.0), writes=[VAtB])
        P.add("pool", lambda h: h.memset(ones96[:], 1.0), writes=[ones96B])
        P.add("pool", lambda h: h.memset(ccar[:], 0.0), writes=[ccarB])
        P.add("pool", lambda h: h.memset(lfc[:], 0.0), writes=[lfcB])

        def c_chain(n, col0, vb0, nblk, carry_in, rows_last):
            P.add("dve", lambda h: h.tensor_tensor_scan(out=C3[:, 0:n], data0=ones96[:, 0:n], data1=LF[:, 0:n], initial=carry_in, op0=ALU.mult, op1=ALU.add),
                  reads=[ones96B, LFB, ccarB], writes=[C3B])
            P.add("dve", lambda h: h.tensor_copy(out=ccar[:], in_=C3[:, n - 1:n]), reads=[C3B], writes=[ccarB])
            if col0 is not None:
                P.add("act", lambda h: h.activation(out=HI[:, 0:n], in_=C3[:, 0:n], func=AF.Copy, scale=8.0), reads=[C3B], writes=[HIB])
                P.add("pool", lambda h: h.tensor_copy(out=CS[0:16, col0:col0 + n], in_=HI[0:16, 0:n]), reads=[HIB, CSB], writes=[CSB])
                P.add("dve", lambda h: h.scalar_tensor_tensor(out=R1[:, 0:n], in0=C3[:, 0:n], scalar=8.0, in1=HI[:, 0:n], op0=ALU.mult, op1=ALU.subtract), reads=[C3B, HIB], writes=[R1B])
                P.add("act", lambda h: h.activation(out=HI[:, 0:n], in_=R1[:, 0:n], func=AF.Copy), reads=[R1B], writes=[HIB])
                P.add("pool", lambda h: h.tensor_copy(out=CS[32:48, col0:col0 + n], in_=HI[32:48, 0:n]), reads=[HIB, CSB], writes=[CSB])
                P.add("dve", lambda h: h.tensor_tensor(out=R1[:, 0:n], in0=R1[:, 0:n], in1=HI[:, 0:n], op=ALU.subtract), reads=[R1B, HIB], writes=[R1B])
                P.add("pool", lambda h: h.tensor_copy(out=CS[64:80, col0:col0 + n], in_=R1[64:80, 0:n]), reads=[R1B, CSB], writes=[CSB])
            for b in range(nblk):
                nr = 128 if b < nblk - 1 else rows_last
                P.add("pe", lambda h, b=b, nr=nr: h.transpose(out=PS[6][0:nr, 0:96], in_=C3[:, b * 128:b * 128 + nr], identity=ident[0:96, 0:96]), reads=[C3B, identB], writes=[PSB[6]])
                P.add("act", lambda h, b=b, nr=nr: h.activation(out=CKT[0:nr, vb0 + b, :], in_=PS[6][0:nr, 0:16], func=AF.Copy, scale=-1.0), reads=[PSB[6], CKTB], writes=[CKTB])

        def logf_from_psum(n):
            P.add("act", lambda h: h.activation(out=LF[:, 0:n], in_=PS[6][0:96, 0:n], func=AF.Exp, scale=-1.0, bias=nbf[:]), reads=[PSB[6], nbfB], writes=[LFB])
            P.add("act", lambda h: h.activation(out=LF[:, 0:n], in_=LF[:, 0:n], func=AF.Ln, bias=1.0), reads=[LFB], writes=[LFB])
            P.add("dve", lambda h: h.tensor_scalar(out=LF[:, 0:n], in0=LF[:, 0:n], scalar1=-1.0, scalar2=None, op0=ALU.mult), reads=[LFB], writes=[LFB])

        def proj_tile(xt, xtB, t0, N, samp):
            nb = N // 128
            P.add("pool", lambda h: h.tensor_copy(out=xb[:, :, 0:N], in_=xt[:, :, 0:N]), reads=xtB, writes=[xbB])
            for c in range(KC):
                pb = c % 2

                def qp(h, c=c, pb=pb):
                    for k in range(KC):
                        r = h.matmul(PS[pb][:, 0:N], lhsT=wqk[:, k, c * 128:(c + 1) * 128], rhs=xb[:, k, 0:N], start=(k == 0), stop=(k == KC - 1))
                    return r
                P.add("pe", qp, reads=[wqkB, xbB], writes=[PSB[pb]])
                P.add("act", copy_op("act", QTm[0:64, 2 * c, 0:N], PS[pb][0:64, 0:N]), reads=[PSB[pb], QTmB], writes=[QTmB])
                P.add("dve", copy_op("dve", QTm[64:128, 2 * c + 1, 0:N], PS[pb][64:128, 0:N]), reads=[PSB[pb], QTmB], writes=[QTmB])
            P.add("sp", lambda h: [h.dma_start(out=QD[:, :, t0:t0 + N], in_=QTm[:, :, 0:N])], reads=[QTmB], writes=[QDB], dma=True)
            for c in range(KC):
                pb = 2 + c % 2

                def kp(h, c=c, pb=pb):
                    for k in range(KC):
                        r = h.matmul(PS[pb][:, 0:N], lhsT=wqk[:, k, D + c * 128:D + (c + 1) * 128], rhs=xb[:, k, 0:N], start=(k == 0), stop=(k == KC - 1))
                    return r
                P.add("pe", kp, reads=[wqkB, xbB], writes=[PSB[pb]])
                e = "act" if c % 2 == 0 else "dve"
                P.add(e, copy_op(e, KTt[:, c, 0:N], PS[pb][:, 0:N]), reads=[PSB[pb], KTtB], writes=[KTtB])
            if not samp:
                P.add("sp", lambda h: [h.dma_start(out=KD[:, :, t0:t0 + N], in_=KTt[:, :, 0:N])], reads=[KTtB], writes=[KDB], dma=True)
            else:
                for s_ in range(2):
                    kc0 = T + s_ * PK + PAST
                    P.add("sp", lambda h, s_=s_, kc0=kc0: [h.dma_start(out=KD[:, :, kc0:kc0 + 64], in_=KTt[:, :, s_ * 64:(s_ + 1) * 64])], reads=[KTtB], writes=[KDB], dma=True)
            for b in range(nb):
                ko, koB = kvo[0]
                vo, voB = kvo[1]
                for (wsrc, wB, c0w, dstt, dstB) in ((wqk, wqkB, D, ko, koB), (wv, wvB, 0, vo, voB)):
                    for hf in range(2):
                        pb = 4 + hf

                        def tp(h, b=b, wsrc=wsrc, c0w=c0w, hf=hf, pb=pb):
                            for k in range(KC):
                                r = h.matmul(PS[pb][:, 0:512], lhsT=xb[:, k, b * 128:(b + 1) * 128], rhs=wsrc[:, k, c0w + hf * 512:c0w + (hf + 1) * 512], start=(k == 0), stop=(k == KC - 1))
                            return r
                        P.add("pe", tp, reads=[wB, xbB], writes=[PSB[pb]])
                        e = "act" if hf == 0 else "dve"
                        P.add(e, copy_op(e, dstt[:, hf * 512:(hf + 1) * 512], PS[pb][:, 0:512]), reads=[PSB[pb], dstB], writes=[dstB])
                P.add("pool", lambda h: h.tensor_copy(out=VAt[:, :, 0:64], in_=vo[:].rearrange("p (g d) -> p g d", g=16)), reads=[voB, VAtB], writes=[VAtB])
                if not samp:
                    r0 = t0 + b * 128
                    P.add("sp", lambda h, r0=r0: [h.dma_start(out=O["fk_p"][r0:r0 + 128, :], in_=ko[:])], reads=[koB], writes=[OB["fk_p"]], dma=True)
                    P.add("sp", lambda h, r0=r0: [h.dma_start(out=O["fv_p"][r0:r0 + 128, :], in_=vo[:])], reads=[voB], writes=[OB["fv_p"]], dma=True)
                    vb = r0 // 128
                    P.add("sp", lambda h, vb=vb: [h.dma_start(out=VD[vb], in_=VAt[:].rearrange("p g d -> p (g d)"))], reads=[VAtB], writes=[VDB], dma=True)
                else:
                    P.add("sp", lambda h: [h.dma_start(out=O["fk_s"], in_=ko[:])], reads=[koB], writes=[OB["fk_s"]], dma=True)
                    P.add("sp", lambda h: [h.dma_start(out=O["fv_s"], in_=vo[:])], reads=[voB], writes=[OB["fv_s"]], dma=True)
                    for s_ in range(2):
                        vb = NBP + s_ * (KB + 1) + KB
                        P.add("sp", lambda h, s_=s_, vb=vb: [h.dma_start(out=VD[vb, 0:64, :], in_=VAt[s_ * 64:(s_ + 1) * 64].rearrange("p g d -> p (g d)"))], reads=[VAtB], writes=[VDB], dma=True)
            def fp_(h):
                for k in range(KC):
                    r = h.matmul(PS[6][0:96, 0:N], lhsT=wf3[:, k, :], rhs=xb[:, k, 0:N], start=(k == 0), stop=(k == KC - 1))
                return r
            P.add("pe", fp_, reads=[wf3B, xbB], writes=[PSB[6]])
            logf_from_psum(N)
            for b in range(nb):
                P.add("pe", lambda h, b=b: h.transpose(out=PS[6][:, 0:96], in_=LF[:, b * 128:(b + 1) * 128], identity=ident[0:96, 0:96]), reads=[LFB, identB], writes=[PSB[6]])
                P.add("dve", lambda h: h.tensor_copy(out=tmo[:], in_=PS[6][:, 0:96]), reads=[PSB[6]], writes=[tmoB])
                if not samp:
                    r0 = t0 + b * 128
                    P.add("sp", lambda h, r0=r0: [h.dma_start(out=O["ff_p"][r0:r0 + 128, :], in_=tmo[:, 0:16])], reads=[tmoB], writes=[OB["ff_p"]], dma=True)
                else:
                    P.add("sp", lambda h: [h.dma_start(out=O["ff_s"], in_=tmo[:, 0:16])], reads=[tmoB], writes=[OB["ff_s"]], dma=True)

        tiles = mixer_tiles(NM)
        x_load(src, tiles[0][0], tiles[0][1], *xts[0])
        for it, (t0, N) in enumerate(tiles):
            xt, xtB = xts[it % 2]
            if it + 1 < len(tiles):
                x_load(src, tiles[it + 1][0], tiles[it + 1][1], *xts[(it + 1) % 2])
            samp = t0 >= T
            proj_tile(xt, xtB, t0, N, samp)
            if not samp:
                c_chain(N, t0, t0 // 128, N // 128, ccar[:], 128)
            else:
                newlf, newlfB = R1, R1B
                P.add("pool", lambda h: h.tensor_copy(out=newlf[:, 0:128], in_=LF[:, 0:128]), reads=[LFB], writes=[newlfB])
                NLF, NLFB = tb([96, 128], F32, "NLF")
                P.add("pool", lambda h: h.tensor_copy(out=NLF[:], in_=LF[:, 0:128]), reads=[LFB], writes=[NLFB])
                for s_ in range(2):
                    P.add("pool", lambda h: h.memset(ccar[:], 0.0), reads=[ccarB], writes=[ccarB])
                    vb0 = NBP + s_ * (KB + 1)
                    for cb in range(0, KB, 2):
                        n2 = min(2, KB - cb)
                        for b2 in range(n2):
                            b = cb + b2
                            P.add("sp", lambda h, s_=s_, b=b: [h.dma_start(out=ctm[:], in_=I["cfk"][s_, b * 128:(b + 1) * 128, :])], writes=[ctmB], dma=True)
                            for c in range(KC):
                                pb = c % 2
                                P.add("pe", lambda h, c=c, pb=pb: h.transpose(out=PS[pb][:, 0:128], in_=ctm[:, c * 128:(c + 1) * 128], identity=ident[:]), reads=[ctmB, identB], writes=[PSB[pb]])
                                e = "act" if c % 2 == 0 else "dve"
                                P.add(e, copy_op(e, KTt[:, c, 0:128], PS[pb][:, 0:128]), reads=[PSB[pb], KTtB], writes=[KTtB])
                            kc0 = T + s_ * PK + b * 128
                            P.add("sp", lambda h, kc0=kc0: [h.dma_start(out=KD[:, :, kc0:kc0 + 128], in_=KTt[:, :, 0:128])], reads=[KTtB], writes=[KDB], dma=True)
                            vo, voB = kvo[1]
                            P.add("sp", lambda h, s_=s_, b=b: [h.dma_start(out=vo[:], in_=I["cfv"][s_, b * 128:(b + 1) * 128, :])], writes=[voB], dma=True)
                            P.add("pool", lambda h: h.tensor_copy(out=VAt[:, :, 0:64], in_=vo[:].rearrange("p (g d) -> p g d", g=16)), reads=[voB, VAtB], writes=[VAtB])
                            P.add("sp", lambda h, vb=vb0 + b: [h.dma_start(out=VD[vb], in_=VAt[:].rearrange("p g d -> p (g d)"))], reads=[VAtB], writes=[VDB], dma=True)
                            for s3 in range(3):
                                P.add("sp", lambda h, s_=s_, b=b, s3=s3: [h.dma_start(out=lfc[:, 32 * s3:32 * s3 + 16], in_=I["cff"][s_, b * 128:(b + 1) * 128, :])],
                                      reads=[lfcB], writes=[lfcB], dma=True)
                            P.add("pe", lambda h: h.transpose(out=PS[6][0:96, 0:128], in_=lfc[:], identity=ident[:]), reads=[lfcB, identB], writes=[PSB[6]])
                            P.add("dve", lambda h, b2=b2: h.tensor_copy(out=LF[:, b2 * 128:(b2 + 1) * 128], in_=PS[6][0:96, 0:128]), reads=[PSB[6], LFB], writes=[LFB])
                        c_chain(n2 * 128, None, vb0 + cb, n2, ccar[:], 128)
                    P.add("pool", lambda h, s_=s_: h.tensor_copy(out=LF[:, 0:64], in_=NLF[:, s_ * 64:(s_ + 1) * 64]), reads=[NLFB, LFB], writes=[LFB])
                    c_chain(64, T + s_ * 64, vb0 + KB, 1, ccar[:], 64)

        P.barrier()
        st["off"] = mark2
        NMB = 512
        NKM = max(T, PK)
        VA, VAB = tb([128, max(NBP, 2 * (KB + 1)), 16 * 65], BF16, "VA")
        KTc = [tb([128, NKM], BF16, f"KTc{i}") for i in range(1)]
        QTf, QTsB = tb([128, 8 * NMB], BF16, "QTs")
        QTs = QTf[:].rearrange("p (g t) -> p g t", t=NMB)
        QTs_s = QTf[:, 0:16 * 128].rearrange("p (g t) -> p g t", t=128)
        xts = [tbl([128, KC, NMB], F32, f"xt{i}_") for i in range(1)]
        zt, ztB = tbl([128, KC, NMB], F32, "zt_")
        PTs = [tb([128, NMB], BF16, f"pt{i}") for i in range(4)]
        Osb, OsbB = tb([128, 4, D], BF16, "Osb")
        OT, OTB = tb([128, KC, NMB], BF16, "OT")
        recs = [tb([128, 4], F32, f"rec{i}") for i in range(2)]
        rec, recB = recs[0]
        scr = ln_scratch(NMB)
        tiles = mixer_tiles(NMB)
        VA4 = VA[:].rearrange("p b (g e) -> p b g e", g=16)
        pi = 0
        kci = 0
        pend = {"it": None}

        def fox_back(jt):
            tt0, NN = tiles[jt]
            kk = jt % NR
            vo = CCO[kk].rearrange("p (c t) -> p c t", c=KC)
            P.add("sp", lambda h, vo=vo: [h.dma_start(out=zt[:, :, 0:NMB], in_=vo)], reads=[CCOB[kk]], writes=ztB, dma=True)
            layer_norm(zt, ztB, NN, li, 0, scr, (2, 3))
            x_store(dst, tt0, NN, zt, ztB)

        for it, (t0, N) in enumerate(tiles):
            xt, xtB = xts[0]
            samp = t0 >= T
            NHL = 8 if not samp else 16
            QV = QTs if not samp else QTs_s
            P.add("sp", lambda h, t0=t0, N=N, QV=QV, NHL=NHL: [h.dma_start(out=QV[:, 0:NHL, 0:N], in_=QD[:, 0:NHL, t0:t0 + N])], reads=[QDB], writes=[QTsB], dma=True)
            if not samp:
                for b in range(N // 128):
                    vb = t0 // 128 + b
                    P.add("sp", lambda h, vb=vb: [h.dma_start(out=VA[:, vb, :], in_=VD[vb])], reads=[VDB, VAB], writes=[VAB], dma=True)
                qsets = [(0, N, 0, t0 + N, 0, 0, t0)]
            else:
                qsets = []
                for s_ in range(2):
                    qsets.append((s_ * 64, 64, T + s_ * PK, PK, s_ * (KB + 1), NBP + s_ * (KB + 1), PAST))
            tasks = []
            for (qc0, nq, kd0, nkeys, vab0, ckb0, qpos) in qsets:
                if samp:
                    s_ = qc0 // 64
                    for b in range(KB + 1):
                        nr = 128 if b < KB else 64
                        P.add("sp", lambda h, b=b, nr=nr, vab0=vab0, vb=NBP + s_ * (KB + 1) + b: [h.dma_start(out=VA[0:nr, vab0 + b, :], in_=VD[vb, 0:nr, :])], reads=[VDB, VAB], writes=[VAB], dma=True)
                nsub = (nq + 127) // 128
                nkb = (nkeys + 127) // 128
                for hd in range(NHL):
                    blks = []
                    for kb in range(nkb):
                        nk = min(128, nkeys - kb * 128)
                        kpos0 = kb * 128
                        jlo = max(0, kpos0 - qpos)
                        jlo = (jlo // 128) * 128 if nq > 64 else 0
                        if jlo >= nq:
                            continue
                        nqv = nq - jlo
                        diag = (kpos0 + nk - 1) > (qpos + jlo)
                        blks.append(dict(hd=hd, kb=kb, nk=nk, jlo=jlo, nqv=nqv, diag=diag, qc0=qc0, nq=nq, kd0=kd0, nkeys=nkeys,
                                         vab0=vab0, ckb0=ckb0, nsub=nsub, first=False, last=False))
                    blks[0]["first"] = True
                    blks[-1]["last"] = True
                    tasks.extend(blks)
            LA = 2
            kt_state = {"key": None, "buf": None}
            head_ctr = {"n": 0}

            def emit_front(i):
                tk = tasks[i]
                hd, kb, nk, jlo, nqv, qc0 = tk["hd"], tk["kb"], tk["nk"], tk["jlo"], tk["nqv"], tk["qc0"]
                key = (qc0, hd // 2)
                if kt_state["key"] != key:
                    ktc, ktcB = KTc[0]
                    kt_state["n"] = kt_state.get("n", 0) + 1
                    kt_state["key"] = key
                    kt_state["buf"] = (ktc, ktcB)
                    c, kd0, nkeys = hd // 2, tk["kd0"], tk["nkeys"]
                    P.add("sp", lambda h, ktc=ktc, c=c, kd0=kd0, nkeys=nkeys: [h.dma_start(out=ktc[:, 0:nkeys], in_=KD[:, c, kd0:kd0 + nkeys])], reads=[KDB], writes=[ktcB], dma=True)
                ktc, ktcB = kt_state["buf"]
                pb = 4 + i % 3
                pt, ptB = PTs[i % 4]
                tk["pt"] = (pt, ptB)

                def st_(h, ktc=ktc, kb=kb, nk=nk, pb=pb, hd=hd, qc0=qc0, jlo=jlo, nqv=nqv, t0=t0, QV=QV):
                    h.matmul(PS[pb][0:nk, 0:nqv], lhsT=ktc[:, kb * 128:kb * 128 + nk], rhs=QV[:, hd, qc0 + jlo:qc0 + jlo + nqv], start=True, stop=False)
                    return h.matmul(PS[pb][0:nk, 0:nqv], lhsT=sel[:, hd, 0:nk], rhs=CS[:, t0 + qc0 + jlo:t0 + qc0 + jlo + nqv], start=False, stop=True)
                P.add("pe", st_, reads=[ktcB, QTsB, selB, CSB], writes=[PSB[pb]])
                bias = CKT[0:nk, tk["ckb0"] + kb, hd:hd + 1]
                P.add("act", lambda h, pt=pt, pb=pb, nk=nk, nqv=nqv, bias=bias: h.activation(out=pt[0:nk, 0:nqv], in_=PS[pb][0:nk, 0:nqv], func=AF.Exp, scale=SCALE, bias=bias),
                      reads=[PSB[pb], CKTB], writes=[ptB])
                if tk["diag"]:
                    dq = min(nk, nqv)
                    P.add("pool", lambda h, pt=pt, nk=nk, dq=dq: h.tensor_tensor(out=pt[0:nk, 0:dq], in0=pt[0:nk, 0:dq], in1=tri[0:nk, 0:dq], op=ALU.mult), reads=[ptB, triB], writes=[ptB])

            def emit_back(i):
                tk = tasks[i]
                hd, kb, nk, jlo, nqv, nq = tk["hd"], tk["kb"], tk["nk"], tk["jlo"], tk["nqv"], tk["nq"]
                pt, ptB = tk["pt"]
                if tk["first"]:
                    head_ctr["n"] += 1
                acc = 2 + head_ctr["n"] % 2

                def pv(h, pt=pt, nk=nk, jlo=jlo, kb=kb, hd=hd, acc=acc, first=tk["first"], last=tk["last"], vab0=tk["vab0"], nq=nq):
                    r = None
                    for sb in range(jlo // 128, (nq + 127) // 128):
                        w = min(128, nq - sb * 128)
                        r = h.matmul(PS[acc][0:w, sb * 65:(sb + 1) * 65], lhsT=pt[0:nk, sb * 128 - jlo:sb * 128 - jlo + w], rhs=VA4[0:nk, vab0 + kb, hd, :],
                                     start=(first and sb == jlo // 128), stop=(last and sb == (nq + 127) // 128 - 1), skip_group_check=True)
                    return r
                P.add("pe", pv, reads=[ptB, VAB], writes=[PSB[acc]])
                if tk["last"]:
                    for sb in range(tk["nsub"]):
                        w = min(128, nq - sb * 128)
                        ob = sb if not samp else tk["qc0"] // 64
                        P.add("dve", lambda h, sb=sb, w=w, acc=acc: h.reciprocal(out=rec[0:w, sb:sb + 1], in_=PS[acc][0:w, sb * 65 + 64:sb * 65 + 65]), reads=[PSB[acc], recB], writes=[recB])
                        P.add("act", lambda h, sb=sb, w=w, acc=acc, ob=ob, hd=hd: h.activation(out=Osb[0:w, ob, hd * 64:(hd + 1) * 64], in_=PS[acc][0:w, sb * 65:sb * 65 + 64], func=AF.Copy,
                                                                                           scale=rec[0:w, sb:sb + 1]), reads=[PSB[acc], recB, OsbB], writes=[OsbB])

            for i in range(min(LA, len(tasks))):
                emit_front(i)
            for i in range(len(tasks)):
                if i + LA < len(tasks):
                    emit_front(i + LA)
                emit_back(i)
            x_load(src, t0, N, xt, xtB)
            if samp:
                if pend["it"] is not None:
                    fox_back(pend["it"])
                    pend["it"] = None
                obl = [(0, 64, 0), (1, 64, 64)]
                out_proj_ln(li, wo, woB, Osb, OsbB, obl, xt, xtB, N, scr, OT, OTB)
                x_store(dst, t0, N, xt, xtB)
            else:
                obl = [(b, 128, b * 128) for b in range(N // 128)]
                out_proj_ln(li, wo, woB, Osb, OsbB, obl, xt, xtB, N, scr, OT, OTB, nkc=4, rscale=ALPHA / 2, do_ln=False)
                k = it % NR
                vi = CCI[k].rearrange("p (c t) -> p c t", c=KC)
                P.add("sp", lambda h, vi=vi: [h.dma_start(out=vi, in_=xt[:, :, 0:NMB])], reads=xtB, writes=[CCIB[k]], dma=True)
                P.add("pool", lambda h, k=k: [h.collective_compute("AllReduce", ALU.add, replica_groups=CC_GROUPS, ins=[CCI[k]], outs=[CCO[k]])],
                      reads=[CCIB[k]], writes=[CCOB[k]], dma=True, inc=1)
                if pend["it"] is not None:
                    fox_back(pend["it"])
                pend["it"] = it
        if pend["it"] is not None:
            fox_back(pend["it"])

    import os
    stages = os.environ.get("KSTAGES", "in,m0,f0,m1,f1,m2,f2,m3,f3,out").split(",")
    if "in" in stages:
        phase_in()
    cur = 0
    for li in range(4):
        kind, j = li % 3, li // 3
        if f"m{li}" in stages:
            phase_reset()
            if kind == 0:
                phase_rg(j, li, cur, 1 - cur)
            elif kind == 1:
                if os.environ.get("KSWA", "1") == "1":
                    phase_swa(li, cur, 1 - cur)
                else:
                    phase_fox(li, cur, 1 - cur)
            else:
                phase_fox(li, cur, 1 - cur)
            cur = 1 - cur
        if f"f{li}" in stages:
            phase_reset()
            phase_ffn(li, cur, 1 - cur)
            cur = 1 - cur
    if "out" in stages and not (FUSE_OUT and "f3" in stages):
        phase_reset()
        phase_out(cur)
    P.emit()
    return nc


_CACHE = {}


def kernel(x_prompt, x_sample, state_rg_conv, state_rg_h, cache_swa_k, cache_swa_v,
           cache_fox_k, cache_fox_v, cache_fox_logf, ln_gain, ln_bias, ffn_w_up, ffn_w_down,
           rg_w_in, rg_conv_w, rg_conv_b, rg_gate_w, rg_gate_b, rg_lambda, rg_w_out,
           swa_w_qkv, swa_sinks, swa_w_out, rel_bias_table, fox_w_in, fox_b_f, fox_w_out):
    f = lambda a: np.ascontiguousarray(np.asarray(a), dtype=np.float32)
    x_prompt, x_sample = f(x_prompt), f(x_sample)
    B, T, _ = x_prompt.shape
    PAST = cache_fox_k.shape[2]
    key = (T, PAST)
    if key not in _CACHE:
        _CACHE[key] = build(T, PAST)
    nc = _CACHE[key]
    state_rg_conv, state_rg_h = f(state_rg_conv), f(state_rg_h)
    shared = {
        "rg_in": f(rg_w_in), "rg_gw": f(rg_gate_w), "rg_out": f(rg_w_out),
        "sw_qkv": f(swa_w_qkv), "sw_out": f(swa_w_out), "sinks": f(swa_sinks), "table": f(rel_bias_table).reshape(1, 512),

        "ident": np.eye(128, dtype=np.float32),
        "bands": np.ascontiguousarray(_swa_bands().transpose(2, 0, 1, 3).reshape(128, -1)).astype(ml_dtypes.bfloat16),
        "tri": (np.arange(128)[:, None] <= np.arange(128)[None, :]).astype(ml_dtypes.bfloat16),
    }
    sel = np.zeros((96, 16, 128), np.float32)
    for h in range(16):
        for s in range(3):
            sel[32 * s + h, h, :] = 1.0
    shared["sel"] = sel.reshape(96, -1).astype(ml_dtypes.bfloat16)
    ln_gain, ln_bias = f(ln_gain), f(ln_bias)
    rg_conv_w, rg_conv_b, rg_gate_b, rg_lambda = f(rg_conv_w), f(rg_conv_b), f(rg_gate_b), f(rg_lambda)
    cache_swa_k, cache_swa_v = f(cache_swa_k), f(cache_swa_v)
    cache_fox_k, cache_fox_v, cache_fox_logf = f(cache_fox_k), f(cache_fox_v), f(cache_fox_logf)
    ffn_w_up, ffn_w_down = f(ffn_w_up), f(ffn_w_down)
    fox_w_in, fox_b_f, fox_w_out = f(fox_w_in), f(fox_b_f), f(fox_w_out)
    perms = [list(range(0, 16)), list(range(8, 16)) + list(range(0, 8))]
    colp = [np.concatenate([np.arange(hh * 64, (hh + 1) * 64) for hh in pr]) for pr in perms]
    fx_in_l, fx_bf_l, fx_out_l = [], [], []
    for r in range(2):
        cp, pr = colp[r], np.asarray(perms[r])
        w = np.concatenate([fox_w_in[:, :, cp], fox_w_in[:, :, 1024 + cp], fox_w_in[:, :, 2048 + cp], fox_w_in[:, :, 3072 + pr]], axis=-1)
        fx_in_l.append(np.ascontiguousarray(w))
        fx_bf_l.append(np.ascontiguousarray(fox_b_f[0, pr].reshape(16, 1)))
        fx_out_l.append(np.ascontiguousarray(fox_w_out[:, cp, :]))
    w_up_h = [np.ascontiguousarray(ffn_w_up[:, :, r * 2048:(r + 1) * 2048]) for r in range(2)]
    w_dn_h = [np.ascontiguousarray(ffn_w_down[:, r * 2048:(r + 1) * 2048, :]) for r in range(2)]
    in_maps = []
    for c in range(8):
        bp = c % B
        ss = [2 * c, 2 * c + 1]
        prm = np.zeros((384, 128), np.float32)
        prm[0:64] = ln_gain.reshape(64, 128)
        prm[64:128] = ln_bias.reshape(64, 128)
        prm[128:192] = rg_conv_w.reshape(64, 128)
        prm[192:208] = rg_conv_b.reshape(16, 128)
        prm[208:240] = rg_gate_b.reshape(32, 128)
        prm[240:256] = rg_lambda.reshape(16, 128)
        prm[256:288] = state_rg_h[:, ss, :].reshape(32, 128)
        prm[288:384] = state_rg_conv[:, ss, :, :].reshape(96, 128)
        m = dict(shared)
        rk = c // 4
        m["fx_in"] = fx_in_l[rk]
        m["fx_bf"] = fx_bf_l[rk]
        m["fx_out"] = fx_out_l[rk]
        m["w_up"] = w_up_h[rk]
        m["w_dn"] = w_dn_h[rk]
        mk = np.zeros((128, 2), np.float32)
        mk[:, rk] = 1.0
        m["mk"] = mk
        m["xp"] = x_prompt[bp]
        m["xs"] = np.ascontiguousarray(x_sample[ss].reshape(128, D))
        m["prm"] = prm.reshape(3, 128, 128)
        m["cswk"] = np.ascontiguousarray(cache_swa_k[0, ss].reshape(2, 128, 256))
        m["cswv"] = np.ascontiguousarray(cache_swa_v[0, ss].reshape(2, 128, 256))
        m["cfk"] = np.ascontiguousarray(cache_fox_k[0, ss].reshape(2, PAST, D)[:, :, colp[rk]])
        m["cfv"] = np.ascontiguousarray(cache_fox_v[0, ss].reshape(2, PAST, D)[:, :, colp[rk]])
        m["cff"] = np.ascontiguousarray(cache_fox_logf[0, ss].reshape(2, PAST, 16)[:, :, perms[rk]])
        in_maps.append(m)
    res = run_bass_kernel_spmd(nc, in_maps, core_ids=list(range(8)))
    R = res.results
    global _LAST
    _LAST = R
    pc = list(range(B))
    y_p = np.stack([R[c]["y_p"] for c in pc])
    y_s = np.concatenate([R[c]["y_s"].reshape(2, 64, D) for c in range(8)])
    rgc_p = np.stack([R[c]["rgc_p"] for c in pc], axis=1)
    rgc_s = np.concatenate([R[c]["rgc_s"] for c in range(8)], axis=1)
    rgh_p = np.stack([R[c]["rgh_p"] for c in pc], axis=1)
    rgh_s = np.concatenate([R[c]["rgh_s"] for c in range(8)], axis=1)
    swk_p = np.stack([R[c]["swk_p"].reshape(128, 4, 64) for c in pc])[None]
    swk_s = np.concatenate([R[c]["swk_s"].reshape(2, 128, 4, 64) for c in range(8)])[None]
    swv_p = np.stack([R[c]["swv_p"].reshape(128, 4, 64) for c in pc])[None]
    swv_s = np.concatenate([R[c]["swv_s"].reshape(2, 128, 4, 64) for c in range(8)])[None]
    fk_p = np.stack([R[c]["fk_p"].reshape(T, 16, 64) for c in pc])[None]
    inv = [np.argsort(perms[r]) for r in range(2)]
    fk_s = np.concatenate([R[c]["fk_s"].reshape(2, 64, 16, 64)[:, :, inv[c // 4]] for c in range(8)])[None]
    fv_p = np.stack([R[c]["fv_p"].reshape(T, 16, 64) for c in pc])[None]
    fv_s = np.concatenate([R[c]["fv_s"].reshape(2, 64, 16, 64)[:, :, inv[c // 4]] for c in range(8)])[None]
    ff_p = np.stack([R[c]["ff_p"] for c in pc])[None]
    ff_s = np.concatenate([R[c]["ff_s"].reshape(2, 64, 16)[:, :, inv[c // 4]] for c in range(8)])[None]
    return (y_p, y_s, rgc_p, rgc_s, rgh_p, rgh_s, swk_p, swk_s, swv_p, swv_s, fk_p, fk_s, fv_p, fv_s, ff_p, ff_s)
```

```python
import math
import numpy as np
import ml_dtypes
import concourse.bass as bass
import concourse.mybir as mybir
from concourse.bass_utils import run_bass_kernel_spmd

F32 = mybir.dt.float32
BF16 = mybir.dt.bfloat16
AF = mybir.ActivationFunctionType
ALU = mybir.AluOpType

D = 1024
KC = 8
DFF = 4096
NH = 16
HD = 64
ALPHA = (2.0 * 4) ** 0.25
LN_EPS = 1e-5
SCALE = HD ** -0.5
NEG = -30000.0


class Buf:
    __slots__ = ("name", "wr", "wr_dma", "rd", "rd_dma", "sem", "cnt", "persist", "excl")

    def __init__(self, name, persist=False, excl=False):
        self.name = name
        self.persist = persist
        self.excl = excl
        self.wr = {}
        self.wr_dma = None
        self.rd = {}
        self.rd_dma = []
        self.sem = None
        self.cnt = 0


class Op:
    __slots__ = ("eng", "fn", "deps", "dma", "buf", "val", "sig", "cnt", "ep", "sem", "inc")

    def __init__(self, eng, fn, deps, dma, buf):
        self.eng = eng
        self.fn = fn
        self.deps = deps
        self.dma = dma
        self.buf = buf
        self.val = 0
        self.sig = False
        self.cnt = 0
        self.ep = 0


EPOCH = 20000


class Prog:
    def __init__(self, nc):
        self.nc = nc
        self.ops = []
        self.last = {}
        self.dma_since = []
        self.sem_counts = []
        self.sem_free = []
        self.local_bufs = []

    def add(self, eng, fn, reads=(), writes=(), dma=False, ndma=1, inc=16):
        idx = len(self.ops)
        deps = set()
        ex = [b for b in reads if b.excl and b not in writes]
        if ex:
            for b in ex:
                for e, i in b.wr.items():
                    deps.add((i, True))
            reads = [b for b in reads if not b.excl]
            writes = list(writes) + ex
        for b in reads:
            for e, i in b.wr.items():
                deps.add((i, True))
            if b.wr_dma is not None:
                deps.add((b.wr_dma, True))
        for b in writes:
            for e, i in b.wr.items():
                deps.add((i, False))
            if b.wr_dma is not None:
                deps.add((b.wr_dma, False))
            for e, i in b.rd.items():
                deps.add((i, False))
            for i in b.rd_dma:
                deps.add((i, False))
        buf = None
        if dma:
            buf = writes[0]
            if buf.sem is None:
                if self.sem_free:
                    buf.sem = self.sem_free.pop()
                else:
                    buf.sem = len(self.sem_counts)
                    self.sem_counts.append(0)
                buf.cnt = self.sem_counts[buf.sem]
                if not buf.persist:
                    self.local_bufs.append(buf)
            buf.cnt += inc * ndma
            self.sem_counts[buf.sem] = buf.cnt
        op = Op(eng, fn, deps, dma, buf)
        op.inc = inc
        if dma:
            op.val = buf.cnt
            op.sem = buf.sem
            self.dma_since.append(idx)
        for b in reads:
            if dma:
                b.rd_dma.append(idx)
            else:
                b.rd[eng] = idx
        for b in writes:
            b.rd = {}
            b.rd_dma = []
            if dma:
                b.wr_dma = idx
            else:
                b.wr[eng] = idx
        self.ops.append(op)
        if not dma:
            self.last[eng] = idx
        return idx

    def barrier(self):
        deps = set((i, True) for i in self.last.values())
        deps.update((i, True) for i in self.dma_since)
        self.dma_since = []
        for eng in ("pe", "act", "dve", "pool", "sp"):
            op = Op(eng, None, set(deps), False, None)
            op.inc = 0
            self.ops.append(op)
        for b in self.local_bufs:
            self.sem_free.append(b.sem)
            b.sem = None
        self.local_bufs = []

    def emit(self):
        nc = self.nc
        ops = self.ops
        for op in ops:
            for (p, raw) in op.deps:
                po = ops[p]
                if po.dma:
                    continue
                if po.eng == op.eng:
                    if op.dma or op.eng == "sp":
                        pass
                    elif po.eng == "pe":
                        continue
                po.sig = True
        counts = {}
        for op in ops:
            if op.dma or not op.sig:
                continue
            c = counts.get(op.eng, 0) + 1
            counts[op.eng] = c
            op.ep = (c - 1) // EPOCH
            op.cnt = (c - 1) % EPOCH + 1
        sems = {}

        def engsem(eng, ep):
            k = (eng, ep)
            if k not in sems:
                sems[k] = nc.alloc_semaphore(f"s_{eng}_{ep}")
            return sems[k]

        dsem = [nc.alloc_semaphore(f"d_{i}") for i in range(len(self.sem_counts))]
        print(f"[build] ops={len(ops)} dma_sems={len(dsem)}", flush=True)
        per = {"pe": [], "act": [], "dve": [], "pool": [], "sp": []}
        for i, op in enumerate(ops):
            per[op.eng].append(op)

        def run(eng, h):
            waited = {}
            for op in per[eng]:
                evs = {}
                for (p, raw) in op.deps:
                    po = ops[p]
                    if po.dma:
                        s, v = dsem[po.sem], po.val
                    else:
                        if not po.sig:
                            continue
                        if po.eng == eng and not (op.dma or eng == "sp"):
                            if po.eng == "pe":
                                continue
                        s, v = engsem(po.eng, po.ep), po.cnt
                    k = id(s)
                    if waited.get(k, 0) >= v:
                        continue
                    if k not in evs or evs[k][1] < v:
                        evs[k] = (s, v)
                for k, (s, v) in evs.items():
                    h.wait_ge(s, v)
                    waited[k] = v
                if op.fn is None:
                    continue
                r = op.fn(h)
                if op.dma:
                    for ins in r:
                        ins.then_inc(dsem[op.sem], op.inc)
                elif op.sig:
                    r.then_inc(engsem(op.eng, op.ep), 1)
            if eng == "sp":
                for i, sm in enumerate(dsem):
                    if waited.get(id(sm), 0) < self.sem_counts[i]:
                        h.wait_ge(sm, self.sem_counts[i])

        with nc.Block() as block:
            @block.tensor
            def _(h):
                run("pe", h)

            @block.scalar
            def _(h):
                run("act", h)

            @block.vector
            def _(h):
                run("dve", h)

            @block.gpsimd
            def _(h):
                run("pool", h)

            @block.sync
            def _(h):
                run("sp", h)


def _t5_bucket(rel):
    half = 16
    max_exact = 8
    n = np.abs(rel)
    n_f = np.maximum(n, 1).astype(np.float32)
    large = max_exact + (np.log(n_f / max_exact) / math.log(128 / max_exact) * (half - max_exact)).astype(np.int32)
    large = np.minimum(large, half - 1)
    return np.where(rel > 0, half, 0) + np.where(n < max_exact, n, large)


def _swa_bands():
    out = np.zeros((2, 33, 128, 128), np.float32)
    k = np.arange(128)[:, None]
    q = np.arange(128)[None, :]
    for kind in range(2):
        kpos = k - 128 * kind
        rel = kpos - q
        kch = np.floor_divide(kpos, 64)
        qch = q // 64
        valid = (kch <= qch) & (kch >= qch - 2)
        bk = _t5_bucket(rel)
        for b in range(32):
            out[kind, b] = ((bk == b) & valid).astype(np.float32)
        out[kind, 32] = (~valid).astype(np.float32)
    return out


def build(T, PAST):
    nc = bass.Bass("TRN2", target_bir_lowering=False)
    P = Prog(nc)
    TT = T + 128
    NB = TT // 128
    NBP = T // 128
    KB = PAST // 128

    def din(name, shape, dt=F32):
        return nc.dram_tensor(name, list(shape), dt, kind="ExternalInput").ap()

    def dout(name, shape, dt=F32):
        return nc.dram_tensor(name, list(shape), dt, kind="ExternalOutput").ap()

    def dscr(name, shape, dt=F32):
        return nc.dram_tensor(name, list(shape), dt).ap()

    I = {}
    I["xp"] = din("xp", [T, D])
    I["xs"] = din("xs", [128, D])
    I["prm"] = din("prm", [3, 128, 128])
    I["cswk"] = din("cswk", [2, 128, 256])
    I["cswv"] = din("cswv", [2, 128, 256])
    I["cfk"] = din("cfk", [2, PAST, D])
    I["cfv"] = din("cfv", [2, PAST, D])
    I["cff"] = din("cff", [2, PAST, 16])
    I["w_up"] = din("w_up", [4, D, DFF // 2])
    I["w_dn"] = din("w_dn", [4, DFF // 2, D])
    I["mk"] = din("mk", [128, 2])
    I["rg_in"] = din("rg_in", [2, D, 2 * D])
    I["rg_gw"] = din("rg_gw", [2, 4, 256, 512])
    I["rg_out"] = din("rg_out", [2, D, D])
    I["sw_qkv"] = din("sw_qkv", [1, D, 1536])
    I["sw_out"] = din("sw_out", [1, D, D])
    I["sinks"] = din("sinks", [1, 16])
    I["table"] = din("table", [1, 512])
    I["fx_in"] = din("fx_in", [1, D, 3088])
    I["fx_bf"] = din("fx_bf", [16, 1])
    I["fx_out"] = din("fx_out", [1, D, D])
    I["ident"] = din("ident", [128, 128])
    I["bands"] = din("bands", [128, 2 * 33 * 128], BF16)
    I["tri"] = din("tri", [128, 128], BF16)
    I["sel"] = din("sel", [96, 16 * 128], BF16)

    O = {}
    O["y_p"] = dout("y_p", [T, D])
    O["y_s"] = dout("y_s", [128, D])
    O["rgc_p"] = dout("rgc_p", [2, 3, D])
    O["rgc_s"] = dout("rgc_s", [2, 2, 3, D])
    O["rgh_p"] = dout("rgh_p", [2, D])
    O["rgh_s"] = dout("rgh_s", [2, 2, D])
    O["swk_p"] = dout("swk_p", [128, 256])
    O["swk_s"] = dout("swk_s", [2, 128, 256])
    O["swv_p"] = dout("swv_p", [128, 256])
    O["swv_s"] = dout("swv_s", [2, 128, 256])
    O["fk_p"] = dout("fk_p", [T, D])
    O["fk_s"] = dout("fk_s", [128, D])
    O["fv_p"] = dout("fv_p", [T, D])
    O["fv_s"] = dout("fv_s", [128, D])
    O["ff_p"] = dout("ff_p", [T, 16])
    O["ff_s"] = dout("ff_s", [128, 16])
    import os
    DBG = os.environ.get("KDBG", "0") == "1"
    FUSE_OUT = os.environ.get("KFUSE_OUT", "1") == "1"
    if DBG:
        for nm in ("dbg_a", "dbg_b", "dbg_cv", "dbg_h", "dbg_r", "dbg_i"):
            O[nm] = dout(nm, [128, KC, 128])
    OB = {k: Buf("o_" + k, True) for k in O}

    XD = [dscr("XA", [128, KC, TT]), dscr("XB", [128, KC, TT])]
    XDB = [[Buf(f"X{i}_{b}", True) for b in range(NB // 4 + 1)] for i in range(2)]
    NR = 3
    CCI = [dscr(f"cci{i}", [128, KC * 512]) for i in range(NR)]
    CCO = [dscr(f"cco{i}", [128, KC * 512]) for i in range(NR)]
    CCIB = [Buf(f"cci{i}", True) for i in range(NR)]
    CCOB = [Buf(f"cco{i}", True) for i in range(NR)]
    CC_GROUPS = [[0, 4], [1, 5], [2, 6], [3, 7]]
    SX = [dscr(f"ccs{i}", [128, KC * 256]) for i in range(4)]
    SXB = [Buf(f"ccs{i}", True) for i in range(4)]
    QAD = dscr("QAD", [16, 67, TT], BF16)
    KAD = dscr("KAD", [16, 67, T + 2 * (PAST + 64)], BF16)
    QADB = Buf("QAD", True)
    KADB = Buf("KAD", True)

    BASE = ((nc.SBUF_PARTITION_SIZE_BYTES - nc.sbuf_bytes_remaining + 63) // 64) * 64
    LIMIT = nc.SBUF_PARTITION_SIZE_BYTES - 64
    st = {"off": BASE, "n": 0, "mark": BASE, "persist": True}

    def alloc(shape, dt=F32):
        esz = 4 if dt == F32 else 2
        nbytes = int(np.prod(shape[1:])) * esz
        nbytes = (nbytes + 63) // 64 * 64
        off = st["off"]
        assert off + nbytes <= LIMIT, f"SBUF overflow {off + nbytes} > {LIMIT}"
        st["off"] = off + nbytes
        st["n"] += 1
        return nc.alloc_sbuf_tensor_at(f"t{st['n']}", list(shape), dt, offset=off)

    def tb(shape, dt=F32, name=""):
        return alloc(shape, dt), Buf(name, st["persist"])

    def tbl(shape, dt=F32, name="", n=KC):
        return alloc(shape, dt), [Buf(f"{name}{i}", st["persist"]) for i in range(n)]

    def phase_reset():
        P.barrier()
        st["persist"] = False
        st["off"] = st["mark"]

    PS = [nc.alloc_psum_tensor(f"ps{i}", [128, 512], F32) for i in range(7)]
    PSB = [Buf(f"ps{i}", True, True) for i in range(7)]
    PSH = nc.alloc_psum_tensor("psh", [128, 1024], BF16)
    PSHB = Buf("psh", True, True)

    ident, identB = tb([128, 128], F32, "ident")
    identh, identhB = tb([128, 128], BF16, "identh")
    onesb, onesbB = tb([128, 128], BF16, "ones")
    PT, PTB = tb([128, 384], F32, "PT")
    cvec, cvecB = tb([128, 16], F32, "cvec")
    cvec2, cvec2B = tb([128, 16], F32, "cvec2")
    P.add("sp", lambda h: [h.dma_start(out=ident[:], in_=I["ident"])], writes=[identB], dma=True)
    P.add("pool", lambda h: h.memset(onesb[:], 1.0), writes=[onesbB])
    P.add("dve", lambda h: h.tensor_copy(out=identh[:], in_=ident[:]), reads=[identB], writes=[identhB])
    MK, MKB = tb([128, 2], F32, "MK")
    P.add("sp", lambda h: [h.dma_start(out=MK[:], in_=I["mk"])], writes=[MKB], dma=True)
    prm, prmB = tb([128, 3, 128], F32, "prm")
    P.add("sp", lambda h: [h.dma_start(out=prm[:], in_=I["prm"].rearrange("g r c -> r g c"))], writes=[prmB], dma=True)
    for g in range(3):
        P.add("pe", lambda h, g=g: h.transpose(out=PS[0][:, g * 128:(g + 1) * 128], in_=prm[:, g, :], identity=ident[:]),
              reads=[prmB, identB], writes=[PSB[0]])
    P.add("dve", lambda h: h.tensor_copy(out=PT[:], in_=PS[0][:, 0:384]), reads=[PSB[0]], writes=[PTB])
    C_LNG, C_LNB, C_CW, C_CB, C_GB, C_LAM, C_H0, C_CV = 0, 64, 128, 192, 208, 240, 256, 288
    tA, tAB = tb([128, 16], F32, "tA")
    tW, tWB = tb([128, 16], F32, "tW")
    tW2, tW2B = tb([128, 16], F32, "tW2")
    st["mark"] = st["off"]
    P.add("act", lambda h: h.activation(out=tA[:], in_=PT[:, C_LAM:C_LAM + 16], func=AF.Exp, scale=-1.0), reads=[PTB], writes=[tAB])
    P.add("dve", lambda h: h.tensor_scalar(out=tW[:], in0=tA[:], scalar1=2.0, scalar2=None, op0=ALU.add), reads=[tAB], writes=[tWB])
    P.add("dve", lambda h: h.reciprocal(out=tW[:], in_=tW[:]), reads=[tWB], writes=[tWB])
    P.add("dve", lambda h: h.tensor_tensor(out=tW[:], in0=tW[:], in1=tA[:], op=ALU.mult), reads=[tWB, tAB], writes=[tWB])
    P.add("dve", lambda h: h.tensor_tensor(out=tW2[:], in0=tW[:], in1=tW[:], op=ALU.mult), reads=[tWB], writes=[tW2B])
    P.add("dve", lambda h: h.tensor_scalar(out=tA[:], in0=tW2[:], scalar1=1.0 / 9, scalar2=1.0 / 7, op0=ALU.mult, op1=ALU.add), reads=[tW2B], writes=[tAB])
    for cst in (1.0 / 5, 1.0 / 3, 1.0):
        P.add("dve", lambda h: h.tensor_tensor(out=tA[:], in0=tA[:], in1=tW2[:], op=ALU.mult), reads=[tAB, tW2B], writes=[tAB])
        P.add("dve", lambda h, cst=cst: h.tensor_scalar(out=tA[:], in0=tA[:], scalar1=cst, scalar2=None, op0=ALU.add), reads=[tAB], writes=[tAB])
    P.add("dve", lambda h: h.tensor_tensor(out=tA[:], in0=tA[:], in1=tW[:], op=ALU.mult), reads=[tAB, tWB], writes=[tAB])
    P.add("dve", lambda h: h.tensor_scalar(out=cvec[:], in0=tA[:], scalar1=-16.0, scalar2=None, op0=ALU.mult), reads=[tAB], writes=[cvecB])
    P.add("dve", lambda h: h.tensor_scalar(out=cvec2[:], in0=tA[:], scalar1=-32.0, scalar2=None, op0=ALU.mult), reads=[tAB], writes=[cvec2B])

    rr = {"cast": 0}

    def cast_eng():
        rr["cast"] += 1
        return ("pool", "dve", "act")[rr["cast"] % 3]

    def copy_op(eng, out, in_):
        if eng == "act":
            return lambda h: h.activation(out=out, in_=in_, func=AF.Copy)
        return lambda h: h.tensor_copy(out=out, in_=in_)

    def load_w(W, K, M, Wb, WbB, stg, c0=0, kc0=0):
        nkc = K // 128
        cstep = min(M, 2048)
        per = max(1, 2048 // cstep)
        kc = 0
        while kc < nkc:
            n = min(per, nkc - kc)
            for cc in range(0, M, cstep):
                cw = min(cstep, M - cc)
                s, sB = stg[rr["cast"] % len(stg)]
                src = W[kc * 128:(kc + n) * 128, cc:cc + cw].rearrange("(k p) m -> p k m", p=128)
                sv = s[:, 0:n * cw].rearrange("p (k m) -> p k m", k=n)
                P.add("sp", lambda h, sv=sv, src=src: [h.dma_start(out=sv, in_=src)], writes=[sB], dma=True)
                e = cast_eng()
                dst = Wb[:, kc0 + kc:kc0 + kc + n, c0 + cc:c0 + cc + cw]
                P.add(e, copy_op(e, dst, sv), reads=[sB], writes=[WbB])
            kc += n

    def mk_stage():
        return [tb([128, 2048], F32, f"stg{i}") for i in range(2)]

    def layer_norm(z, zB, N, li, lj, scr, pb, gen=False):
        for _ in layer_norm_g(z, zB, N, li, lj, scr, pb):
            pass

    def layer_norm_g(z, zB, N, li, lj, scr, pb, pool_ok=True):
        zb, zbB, zsq, zsqB, mean, meanB, msq, msqB, rstd, rstdB = scr
        PL = "pool" if pool_ok else "dve"
        P.add("dve", lambda h: h.tensor_copy(out=zb[:, :, 0:N], in_=z[:, :, 0:N]), reads=zB, writes=[zbB])
        P.add(PL, lambda h: h.tensor_tensor(out=zsq[:, :, 0:N], in0=z[:, :, 0:N], in1=z[:, :, 0:N], op=ALU.mult), reads=zB, writes=[zsqB])
        yield
        b1, b2 = pb

        def s1(h):
            for c in range(KC):
                r = h.matmul(PS[b1][:, 0:N], lhsT=onesb[:], rhs=zb[:, c, 0:N], start=(c == 0), stop=(c == KC - 1))
            return r

        def s2(h):
            for c in range(KC):
                r = h.matmul(PS[b2][:, 0:N], lhsT=onesb[:], rhs=zsq[:, c, 0:N], start=(c == 0), stop=(c == KC - 1))
            return r
        P.add("pe", s1, reads=[zbB, onesbB], writes=[PSB[b1]])
        P.add("pe", s2, reads=[zsqB, onesbB], writes=[PSB[b2]])
        yield
        P.add("act", lambda h: h.activation(out=mean[:, 0:N], in_=PS[b1][:, 0:N], func=AF.Copy, scale=1.0 / D), reads=[PSB[b1]], writes=[meanB])
        P.add("dve", lambda h: h.tensor_tensor(out=msq[:, 0:N], in0=mean[:, 0:N], in1=mean[:, 0:N], op=ALU.mult), reads=[meanB], writes=[msqB])
        P.add("dve", lambda h: h.scalar_tensor_tensor(out=msq[:, 0:N], in0=PS[b2][:, 0:N], scalar=1.0 / D, in1=msq[:, 0:N], op0=ALU.mult, op1=ALU.subtract),
              reads=[PSB[b2], msqB], writes=[msqB])
        P.add("dve", lambda h: h.tensor_scalar(out=msq[:, 0:N], in0=msq[:, 0:N], scalar1=LN_EPS, scalar2=None, op0=ALU.add), reads=[msqB], writes=[msqB])
        yield
        P.add("act", lambda h: h.activation(out=rstd[:, 0:N], in_=msq[:, 0:N], func=AF.Ln), reads=[msqB], writes=[rstdB])
        P.add("act", lambda h: h.activation(out=rstd[:, 0:N], in_=rstd[:, 0:N], func=AF.Exp, scale=-0.5), reads=[rstdB], writes=[rstdB])
        yield
        for c in range(KC):
            e = "dve" if c % 2 == 0 else PL
            P.add(e, lambda h, c=c: h.tensor_tensor(out=z[:, c, 0:N], in0=z[:, c, 0:N], in1=mean[:, 0:N], op=ALU.subtract), reads=[zB[c], meanB], writes=[zB[c]])
            e2 = PL if c % 2 == 0 else "dve"
            P.add(e2, lambda h, c=c: h.tensor_tensor(out=z[:, c, 0:N], in0=z[:, c, 0:N], in1=rstd[:, 0:N], op=ALU.mult), reads=[zB[c], rstdB], writes=[zB[c]])
            col = (li * 2 + lj) * 8 + c
            P.add("act", lambda h, c=c, col=col: h.activation(out=z[:, c, 0:N], in_=z[:, c, 0:N], func=AF.Identity,
                                                              scale=PT[:, C_LNG + col:C_LNG + col + 1], bias=PT[:, C_LNB + col:C_LNB + col + 1]),
                  reads=[zB[c], PTB], writes=[zB[c]])
            yield

    def ln_scratch(N):
        st["zb_off"] = st["off"]
        zb = tb([128, KC, N], BF16, "zb")
        zsq = tb([128, KC, N], BF16, "zsq")
        mean = tb([128, N], F32, "mean")
        msq = tb([128, N], F32, "msq")
        rstd = tb([128, N], F32, "rstd")
        return (*zb, *zsq, *mean, *msq, *rstd)

    def x_load(src, t0, N, xt, xtB):
        blks = [XDB[src][t0 // 512]]
        P.add("sp", lambda h: [h.dma_start(out=xt[:, :, 0:N], in_=XD[src][:, :, t0:t0 + N])], reads=blks, writes=xtB, dma=True)

    def x_store(dst, t0, N, xt, xtB):
        P.add("sp", lambda h: [h.dma_start(out=XD[dst][:, :, t0:t0 + N], in_=xt[:, :, 0:N])],
              reads=xtB, writes=[XDB[dst][t0 // 512]], dma=True)

    def mixer_tiles(n=256):
        tl = [(i * n, n) for i in range(T // n)]
        tl.append((T, 128))
        return tl

    def phase_in():
        tin = [tb([128, 4, D], F32, f"tin{i}") for i in range(2)]
        xo = [tbl([128, KC, 512], F32, f"xo{i}_") for i in range(2)]
        it = 0
        for (t0, N) in mixer_tiles(512):
            nb = N // 128
            ti, tiB = tin[it % 2]
            xt, xtB = xo[it % 2]
            src = (I["xp"][t0:t0 + N, :] if t0 < T else I["xs"]).rearrange("(b p) d -> p b d", p=128)
            P.add("sp", lambda h, ti=ti, src=src, nb=nb: [h.dma_start(out=ti[:, 0:nb, :], in_=src)], writes=[tiB], dma=True)
            for c in range(KC):
                pb = c % 2

                def tr(h, ti=ti, c=c, nb=nb, pb=pb):
                    for b in range(nb):
                        r = h.transpose(out=PS[pb][:, b * 128:(b + 1) * 128], in_=ti[:, b, c * 128:(c + 1) * 128], identity=ident[:])
                    return r
                P.add("pe", tr, reads=[tiB, identB], writes=[PSB[pb]])
                e = "dve" if c % 2 == 0 else "act"
                P.add(e, copy_op(e, xt[:, c, 0:N], PS[pb][:, 0:N]), reads=[PSB[pb]], writes=[xtB[c]])
            x_store(0, t0, N, xt, xtB)
            it += 1

    def phase_out(src):
        xi = [tbl([128, KC, 512], F32, f"xi{i}_") for i in range(2)]
        to = [tb([128, 4, D], F32, f"to{i}") for i in range(2)]
        it = 0
        for (t0, N) in mixer_tiles(512):
            nb = N // 128
            xt, xtB = xi[it % 2]
            tt, ttB = to[it % 2]
            x_load(src, t0, N, xt, xtB)
            for b in range(nb):
                for half in range(2):
                    pb = (b * 2 + half) % 2

                    def tr(h, xt=xt, b=b, half=half, pb=pb):
                        for c4 in range(4):
                            c = half * 4 + c4
                            r = h.transpose(out=PS[pb][:, c4 * 128:(c4 + 1) * 128], in_=xt[:, c, b * 128:(b + 1) * 128], identity=ident[:])
                        return r
                    P.add("pe", tr, reads=xtB[half * 4:half * 4 + 4] + [identB], writes=[PSB[pb]])
                    e = "dve" if half == 0 else "act"
                    P.add(e, copy_op(e, tt[:, b, half * 512:(half + 1) * 512], PS[pb][:, 0:512]), reads=[PSB[pb]], writes=[ttB])
            if t0 < T:
                dst, dB = O["y_p"][t0:t0 + N, :], OB["y_p"]
            else:
                dst, dB = O["y_s"], OB["y_s"]
            dst = dst.rearrange("(b p) d -> p b d", p=128)
            P.add("sp", lambda h, tt=tt, dst=dst, nb=nb: [h.dma_start(out=dst, in_=tt[:, 0:nb, :])], reads=[ttB], writes=[dB], dma=True)
            it += 1

    def phase_ffn(li, src, dst):
        NF = 512
        HC = 16
        w1, w1B = tb([128, KC, DFF // 2], BF16, "w1")
        w2, w2B = tb([128, HC, D], BF16, "w2")
        stg = mk_stage()
        xts = [tbl([128, KC, NF], F32, f"xt{i}_") for i in range(2)]
        zts = [tbl([128, KC, NF], F32, f"zt{i}_") for i in range(2)]
        xb, xbB = tb([128, KC, NF], BF16, "xb")
        hT, hTB = tbl([128, HC, NF], BF16, "hT", HC)
        rl = [tb([128, NF], F32, f"rl{i}") for i in range(2)]
        scr = ln_scratch(NF)
        tout = nc.alloc_sbuf_tensor_at(f"tout{li}", [128, 4, D], F32, offset=st["zb_off"])
        zt0, zt0B = zts[0]
        P.add("sp", lambda h: [h.dma_start(out=zt0[:, :, 0:128], in_=XD[src][:, :, T:T + 128])], reads=[XDB[src][T // 512]], writes=zt0B, dma=True)
        P.add("dve", lambda h: h.tensor_scalar(out=zt0[:, :, 128:256], in0=zt0[:, :, 0:128], scalar1=MK[:, 1:2], scalar2=None, op0=ALU.mult), reads=zt0B + [MKB], writes=zt0B)
        P.add("dve", lambda h: h.tensor_scalar(out=zt0[:, :, 0:128], in0=zt0[:, :, 0:128], scalar1=MK[:, 0:1], scalar2=None, op0=ALU.mult), reads=zt0B + [MKB], writes=zt0B)
        SI, SO = SX[0], SX[1]
        sview = lambda d: d.rearrange("p (c t) -> p c t", c=KC)
        P.add("sp", lambda h: [h.dma_start(out=sview(SI), in_=zt0[:, :, 0:256])], reads=zt0B, writes=[SXB[0]], dma=True)
        P.add("pool", lambda h: [h.collective_compute("AllReduce", ALU.add, replica_groups=CC_GROUPS, ins=[SI], outs=[SO])],
              reads=[SXB[0]], writes=[SXB[1]], dma=True, inc=1)
        load_w(I["w_up"][li], D, DFF // 2, w1, w1B, stg)
        load_w(I["w_dn"][li], DFF // 2, D, w2, w2B, stg)
        tiles = [(i * NF, NF) for i in range(T // NF)] + [(T, 256)]
        nt = len(tiles)

        def load_x(it):
            t0, N = tiles[it]
            xt, xtB = xts[it % 2]
            if t0 < T:
                x_load(src, t0, N, xt, xtB)
            else:
                P.add("sp", lambda h: [h.dma_start(out=xt[:, :, 0:256], in_=sview(SO))], reads=[SXB[1]], writes=xtB, dma=True)

        def front(it):
            t0, N = tiles[it]
            xt, xtB = xts[it % 2]
            k = it % NR
            P.add("dve", lambda h: h.tensor_copy(out=xb[:, :, 0:N], in_=xt[:, :, 0:N]), reads=xtB, writes=[xbB])
            for j in range(HC):
                pb = j % 2

                def up(h, j=j, pb=pb):
                    for kk in range(KC):
                        r = h.matmul(PS[pb][:, 0:N], lhsT=w1[:, kk, j * 128:(j + 1) * 128], rhs=xb[:, kk, 0:N], start=(kk == 0), stop=(kk == KC - 1))
                    return r
                P.add("pe", up, reads=[w1B, xbB], writes=[PSB[pb]])
                r_, rB = rl[j % 2]
                P.add("act", lambda h, r_=r_, pb=pb: h.activation(out=r_[:, 0:N], in_=PS[pb][:, 0:N], func=AF.Relu), reads=[PSB[pb]], writes=[rB])
                P.add("dve", lambda h, r_=r_, j=j: h.tensor_tensor(out=hT[:, j, 0:N], in0=r_[:, 0:N], in1=r_[:, 0:N], op=ALU.mult), reads=[rB], writes=[hTB[j]])
            for m in range(KC):
                pb = 2 + m % 2

                def dn(h, m=m, pb=pb):
                    for j in range(HC):
                        r = h.matmul(PS[pb][:, 0:N], lhsT=w2[:, j, m * 128:(m + 1) * 128], rhs=hT[:, j, 0:N], start=(j == 0), stop=(j == HC - 1))
                    return r
                P.add("pe", dn, reads=[w2B] + hTB, writes=[PSB[pb]])
                P.add("dve", lambda h, m=m, pb=pb: h.scalar_tensor_tensor(out=xt[:, m, 0:N], in0=xt[:, m, 0:N], scalar=ALPHA / 2, in1=PS[pb][:, 0:N],
                                                                          op0=ALU.mult, op1=ALU.add), reads=[xtB[m], PSB[pb]], writes=[xtB[m]])
            if N == NF:
                ci, co, ciB, coB = CCI[k], CCO[k], CCIB[k], CCOB[k]
            else:
                ci, co, ciB, coB = SX[2], SX[3], SXB[2], SXB[3]
            vi = ci.rearrange("p (c t) -> p c t", c=KC)
            P.add("sp", lambda h: [h.dma_start(out=vi, in_=xt[:, :, 0:N])], reads=xtB, writes=[ciB], dma=True)
            P.add("pool", lambda h, ci=ci, co=co: [h.collective_compute("AllReduce", ALU.add, replica_groups=CC_GROUPS, ins=[ci], outs=[co])],
                  reads=[ciB], writes=[coB], dma=True, inc=1)

        def back(it):
            t0, N = tiles[it]
            zt, ztB = zts[it % 2]
            k = it % NR
            co, coB = (CCO[k], CCOB[k]) if N == NF else (SX[3], SXB[3])
            vo = co.rearrange("p (c t) -> p c t", c=KC)
            P.add("sp", lambda h, N=N: [h.dma_start(out=zt[:, :, 0:N], in_=vo)], reads=[coB], writes=ztB, dma=True)
            if t0 >= T:
                P.add("dve", lambda h: h.tensor_scalar(out=zt[:, :, 0:128], in0=zt[:, :, 0:128], scalar1=MK[:, 0:1], scalar2=None, op0=ALU.mult), reads=ztB + [MKB], writes=ztB)
                P.add("dve", lambda h: h.scalar_tensor_tensor(out=zt[:, :, 0:128], in0=zt[:, :, 128:256], scalar=MK[:, 1:2], in1=zt[:, :, 0:128], op0=ALU.mult, op1=ALU.add),
                      reads=ztB + [MKB], writes=ztB)
                N = 128
            for _ in layer_norm_g(zt, ztB, N, li, 1, scr, (4, 5), pool_ok=False):
                pass
            if li != 3 or not FUSE_OUT:
                x_store(dst, t0, N, zt, ztB)
                return
            zbB_, zsqB_ = scr[1], scr[3]
            nb = N // 128
            for b in range(nb):
                for half in range(2):
                    pb = 6 if (b * 2 + half) % 2 == 0 else 4

                    def tr(h, b=b, half=half, pb=pb):
                        for c4 in range(4):
                            c = half * 4 + c4
                            r = h.transpose(out=PS[pb][:, c4 * 128:(c4 + 1) * 128], in_=zt[:, c, b * 128:(b + 1) * 128], identity=ident[:])
                        return r
                    P.add("pe", tr, reads=ztB[half * 4:half * 4 + 4] + [identB], writes=[PSB[pb]])
                    e = "dve" if half == 0 else "act"
                    P.add(e, copy_op(e, tout[:, b, half * 512:(half + 1) * 512], PS[pb][:, 0:512]), reads=[PSB[pb], zbB_, zsqB_], writes=[zbB_, zsqB_])
            if t0 < T:
                dd, dB = O["y_p"][t0:t0 + N, :], OB["y_p"]
            else:
                dd, dB = O["y_s"], OB["y_s"]
            dd = dd.rearrange("(b p) d -> p b d", p=128)
            P.add("sp", lambda h, dd=dd, nb=nb: [h.dma_start(out=dd, in_=tout[:, 0:nb, :])], reads=[zbB_, zsqB_], writes=[dB], dma=True)

        LAG = 2
        load_x(0)
        if nt > 1:
            load_x(1)
        for it in range(nt + LAG):
            if it < nt:
                front(it)
                if it + 2 < nt:
                    load_x(it + 2)
            if it - LAG >= 0:
                back(it - LAG)

    def phase_rg(l, li, src, dst):
        NM = 256
        win, winB = tb([128, KC, 2 * D], BF16, "win")
        gw, gwB = tb([128, 2, 4 * 512], BF16, "gw")
        wo, woB = tb([128, KC, D], BF16, "wo")
        stg = mk_stage()
        load_w(I["rg_in"][l], D, 2 * D, win, winB, stg)
        for n in range(4):
            load_w(I["rg_gw"][l, n], 256, 512, gw, gwB, stg, c0=n * 512)
        load_w(I["rg_out"][l], D, D, wo, woB, stg)
        xts = [tbl([128, KC, NM], F32, f"xt{i}_") for i in range(2)]
        xb, xbB = tb([128, KC, NM], BF16, "xb")
        ggs = [tbl([128, KC, NM], F32, f"gg{i}_") for i in range(2)]
        U, UB = tbl([128, KC, NM + 6], F32, "U")
        cv, cvB = tbl([128, KC, NM], F32, "cv")
        cvb, cvbB = tbl([128, KC, NM], BF16, "cvb")
        rgs = [tbl([128, KC, NM], F32, f"r{i}_") for i in range(2)]
        igs = [tbl([128, KC, NM], F32, f"i{i}_") for i in range(2)]
        sq, sqB = tbl([128, KC, NM], F32, "sq")
        hh, hhB = tbl([128, KC, NM], F32, "h")
        mb, mbB = tbl([128, KC, NM], BF16, "mb")
        hc, hcB = tbl([128, KC, 2], F32, "hc")
        scr = ln_scratch(NM)
        P.add("pool", lambda h: h.memset(hc[:], 0.0), writes=hcB)
        P.add("pool", lambda h: h.memset(U[:], 0.0), writes=UB)
        tiles = mixer_tiles()
        FUSE_IN = (li == 0) and os.environ.get("KFUSE_IN", "1") == "1"
        if FUSE_IN:
            tin, tinB = tb([128, 2, D], F32, "tin")

        def rg_load(it):
            t0, N = tiles[it]
            xt, xtB = xts[it % 2]
            if not FUSE_IN:
                x_load(src, t0, N, xt, xtB)
                return
            nb = N // 128
            srcap = (I["xp"][t0:t0 + N, :] if t0 < T else I["xs"]).rearrange("(b p) d -> p b d", p=128)
            P.add("sp", lambda h, srcap=srcap, nb=nb: [h.dma_start(out=tin[:, 0:nb, :], in_=srcap)], writes=[tinB], dma=True)
            for c in range(KC):
                def tr(h, c=c, nb=nb):
                    for b in range(nb):
                        r = h.transpose(out=PS[6][:, b * 128:(b + 1) * 128], in_=tin[:, b, c * 128:(c + 1) * 128], identity=ident[:])
                    return r
                P.add("pe", tr, reads=[tinB, identB], writes=[PSB[6]])
                e = "dve" if c % 2 == 0 else "act"
                P.add(e, copy_op(e, xt[:, c, 0:N], PS[6][:, 0:N]), reads=[PSB[6]], writes=[xtB[c]])

        def info(it):
            t0, N = tiles[it]
            samp = t0 >= T
            segs = [(0, N, 0)] if not samp else [(0, 64, 0), (64, 64, 67)]
            return t0, N, samp, segs

        def A1(it):
            t0, N, samp, segs = info(it)
            xt, xtB = xts[it % 2]
            gg, ggB = ggs[it % 2]
            P.add("dve", lambda h: h.tensor_copy(out=xb[:, :, 0:N], in_=xt[:, :, 0:N]), reads=xtB, writes=[xbB])
            if samp:
                for s_ in range(2):
                    for c in range(KC):
                        for k in range(3):
                            col = C_CV + ((l * 2 + s_) * 3 + k) * 8 + c
                            P.add("pool", lambda h, s_=s_, c=c, k=k, col=col: h.tensor_copy(out=U[:, c, s_ * 67 + k:s_ * 67 + k + 1], in_=PT[:, col:col + 1]),
                                  reads=[PTB, UB[c]], writes=[UB[c]])
            for c in range(KC):
                pb = c % 2

                def g1(h, c=c, pb=pb):
                    for k in range(KC):
                        r = h.matmul(PS[pb][:, 0:N], lhsT=win[:, k, c * 128:(c + 1) * 128], rhs=xb[:, k, 0:N], start=(k == 0), stop=(k == KC - 1))
                    return r
                P.add("pe", g1, reads=[winB, xbB], writes=[PSB[pb]])
                P.add("act", lambda h, c=c, pb=pb: h.activation(out=gg[:, c, 0:N], in_=PS[pb][:, 0:N], func=AF.Gelu_apprx_tanh), reads=[PSB[pb]], writes=[ggB[c]])
                if c % 2 == 1:
                    yield
            for c in range(KC):
                pb = 2 + c % 2

                def u1(h, c=c, pb=pb):
                    for k in range(KC):
                        r = h.matmul(PS[pb][:, 0:N], lhsT=win[:, k, D + c * 128:D + (c + 1) * 128], rhs=xb[:, k, 0:N], start=(k == 0), stop=(k == KC - 1))
                    return r
                P.add("pe", u1, reads=[winB, xbB], writes=[PSB[pb]])
                for (s0, L, u0) in segs:
                    P.add("dve", lambda h, c=c, pb=pb, s0=s0, L=L, u0=u0: h.tensor_copy(out=U[:, c, u0 + 3:u0 + 3 + L], in_=PS[pb][:, s0:s0 + L]),
                          reads=[PSB[pb], UB[c]], writes=[UB[c]])
                for (s0, L, u0) in segs:
                    cw = [PT[:, C_CW + (l * 4 + k) * 8 + c:C_CW + (l * 4 + k) * 8 + c + 1] for k in range(4)]
                    cbv = PT[:, C_CB + l * 8 + c:C_CB + l * 8 + c + 1]
                    P.add("pool", lambda h, c=c, s0=s0, L=L, u0=u0, cw=cw, cbv=cbv: h.tensor_scalar(out=cv[:, c, s0:s0 + L], in0=U[:, c, u0:u0 + L], scalar1=cw[0], scalar2=cbv,
                                                                                                      op0=ALU.mult, op1=ALU.add), reads=[UB[c], PTB], writes=[cvB[c]])
                    for k in range(1, 4):
                        P.add("dve", lambda h, c=c, s0=s0, L=L, u0=u0, k=k, cw=cw: h.scalar_tensor_tensor(out=cv[:, c, s0:s0 + L], in0=U[:, c, u0 + k:u0 + k + L], scalar=cw[k],
                                                                                                            in1=cv[:, c, s0:s0 + L], op0=ALU.mult, op1=ALU.add),
                              reads=[UB[c], PTB, cvB[c]], writes=[cvB[c]])
                P.add("pool", lambda h, c=c: h.tensor_copy(out=cvb[:, c, 0:N], in_=cv[:, c, 0:N]), reads=[cvB[c]], writes=[cvbB[c]])
                yield
            last_prompt = (t0 + N == T)
            if last_prompt or samp:
                for (si, (s0, L, u0)) in enumerate(segs):
                    for k in range(3):
                        dd = (O["rgc_p"][l, k, :] if not samp else O["rgc_s"][l, si, k, :]).rearrange("(c p) -> p c", p=128)
                        dB = OB["rgc_p"] if not samp else OB["rgc_s"]
                        P.add("sp", lambda h, dd=dd, u0=u0, L=L, k=k: [h.dma_start(out=dd, in_=U[:, :, u0 + L + k], allow_slow_non_contiguous=True)],
                              reads=UB, writes=[dB], dma=True)
            if not samp:
                P.add("pool", lambda h: h.tensor_copy(out=U[:, :, 0:3], in_=U[:, :, N:N + 3]), reads=UB, writes=UB)

        def A2(it):
            t0, N, samp, segs = info(it)
            rg_, rgB = rgs[it % 2]
            ig, igB = igs[it % 2]
            for n in range(4):
                for oc in range(4):
                    pb = (n * 4 + oc) % 2

                    def gt(h, n=n, oc=oc, pb=pb):
                        for kk in range(2):
                            r = h.matmul(PS[pb][:, 0:N], lhsT=gw[:, kk, n * 512 + oc * 128:n * 512 + (oc + 1) * 128], rhs=cvb[:, 2 * n + kk, 0:N], start=(kk == 0), stop=(kk == 1))
                        return r
                    P.add("pe", gt, reads=[gwB, cvbB[2 * n], cvbB[2 * n + 1]], writes=[PSB[pb]])
                    tgt, tgtB = (rg_, rgB) if oc < 2 else (ig, igB)
                    cc = 2 * n + (oc % 2)
                    gb = PT[:, C_GB + l * 16 + n * 4 + oc:C_GB + l * 16 + n * 4 + oc + 1]
                    P.add("act", lambda h, tgt=tgt, cc=cc, pb=pb, gb=gb: h.activation(out=tgt[:, cc, 0:N], in_=PS[pb][:, 0:N], func=AF.Sigmoid, bias=gb), reads=[PSB[pb], PTB], writes=[tgtB[cc]])
                    if oc % 2 == 1:
                        yield
            for c in range(KC):
                cvb2 = cvec2[:, l * 8 + c:l * 8 + c + 1]
                P.add("act", lambda h, c=c, cvb2=cvb2: h.activation(out=sq[:, c, 0:N], in_=rg_[:, c, 0:N], func=AF.Exp, scale=cvb2), reads=[rgB[c], cvec2B], writes=[sqB[c]])
                P.add("pool", lambda h, c=c: h.tensor_tensor(out=ig[:, c, 0:N], in0=ig[:, c, 0:N], in1=cv[:, c, 0:N], op=ALU.mult), reads=[igB[c], cvB[c]], writes=[igB[c]])
                if c % 2 == 1:
                    yield
            for c in range(KC):
                cva = cvec[:, l * 8 + c:l * 8 + c + 1]
                P.add("act", lambda h, c=c, cva=cva: h.activation(out=rg_[:, c, 0:N], in_=rg_[:, c, 0:N], func=AF.Exp, scale=cva), reads=[rgB[c], cvecB], writes=[rgB[c]])
                if c % 2 == 1:
                    yield
            for c in range(KC):
                P.add("act", lambda h, c=c: h.activation(out=sq[:, c, 0:N], in_=sq[:, c, 0:N], func=AF.Sqrt, scale=-1.0, bias=1.0), reads=[sqB[c]], writes=[sqB[c]])
                P.add("pool", lambda h, c=c: h.tensor_tensor(out=ig[:, c, 0:N], in0=ig[:, c, 0:N], in1=sq[:, c, 0:N], op=ALU.mult), reads=[igB[c], sqB[c]], writes=[igB[c]])
                if c % 2 == 1:
                    yield

        def B(it):
            t0, N, samp, segs = info(it)
            xt, xtB = xts[it % 2]
            gg, ggB = ggs[it % 2]
            rg_, rgB = rgs[it % 2]
            ig, igB = igs[it % 2]
            last_prompt = (t0 + N == T)
            for c in range(KC):
                for (si, (s0, L, u0)) in enumerate(segs):
                    if samp:
                        init = PT[:, C_H0 + (l * 2 + si) * 8 + c:C_H0 + (l * 2 + si) * 8 + c + 1]
                        rds = [rgB[c], igB[c], PTB]
                    else:
                        init = hc[:, c, 0:1]
                        rds = [rgB[c], igB[c], hcB[c]]
                    P.add("dve", lambda h, c=c, s0=s0, L=L, init=init: h.tensor_tensor_scan(out=hh[:, c, s0:s0 + L], data0=rg_[:, c, s0:s0 + L], data1=ig[:, c, s0:s0 + L],
                                                                                           initial=init, op0=ALU.mult, op1=ALU.add), reads=rds, writes=[hhB[c]])
                if not samp:
                    P.add("dve", lambda h, c=c: h.tensor_copy(out=hc[:, c, 0:1], in_=hh[:, c, N - 1:N]), reads=[hhB[c], hcB[c]], writes=[hcB[c]])
                P.add("pool", lambda h, c=c: h.tensor_tensor(out=mb[:, c, 0:N], in0=hh[:, c, 0:N], in1=gg[:, c, 0:N], op=ALU.mult), reads=[hhB[c], ggB[c]], writes=[mbB[c]])
                if c % 2 == 1:
                    yield
            if last_prompt or samp:
                for (si, (s0, L, u0)) in enumerate(segs):
                    dd = (O["rgh_p"][l, :] if not samp else O["rgh_s"][l, si, :]).rearrange("(c p) -> p c", p=128)
                    dB = OB["rgh_p"] if not samp else OB["rgh_s"]
                    P.add("sp", lambda h, dd=dd, s0=s0, L=L: [h.dma_start(out=dd, in_=hh[:, :, s0 + L - 1], allow_slow_non_contiguous=True)], reads=hhB, writes=[dB], dma=True)
            for m in range(KC):
                pb = 4 + m % 2

                def op_(h, m=m, pb=pb):
                    for k in range(KC):
                        r = h.matmul(PS[pb][:, 0:N], lhsT=wo[:, k, m * 128:(m + 1) * 128], rhs=mb[:, k, 0:N], start=(k == 0), stop=(k == KC - 1))
                    return r
                P.add("pe", op_, reads=[woB] + mbB, writes=[PSB[pb]])
                P.add("dve", lambda h, m=m, pb=pb: h.scalar_tensor_tensor(out=xt[:, m, 0:N], in0=xt[:, m, 0:N], scalar=ALPHA, in1=PS[pb][:, 0:N],
                                                                          op0=ALU.mult, op1=ALU.add), reads=[xtB[m], PSB[pb]], writes=[xtB[m]])
                if m % 2 == 1:
                    yield
            yield from layer_norm_g(xt, xtB, N, li, 0, scr, (6, 5))
            x_store(dst, t0, N, xt, xtB)

        def A(it):
            yield from A1(it)
            yield from A2(it)

        def interleave(*gens):
            gens = list(gens)
            while gens:
                for g_ in list(gens):
                    try:
                        next(g_)
                    except StopIteration:
                        gens.remove(g_)

        nt = len(tiles)
        rg_load(0)
        if nt > 1:
            rg_load(1)
        interleave(A(0))
        for it in range(nt):
            if it + 1 < nt:
                interleave(A(it + 1), B(it))
            else:
                interleave(B(it))
            if it + 2 < nt:
                rg_load(it + 2)

    def out_proj_ln(li, wo, woB, Osb, OsbB, oblocks, xt, xtB, N, scr, OT, OTB, nkc=KC, rscale=ALPHA, do_ln=True):
        for bi, (ob, nr, c0) in enumerate(oblocks):
            def tr(h, ob=ob, nr=nr):
                for c in range(nkc):
                    r = h.transpose(out=PSH[:, c * 128:c * 128 + nr], in_=Osb[0:nr, ob, c * 128:(c + 1) * 128], identity=identh[0:nr, 0:nr])
                return r
            P.add("pe", tr, reads=[OsbB, identhB], writes=[PSHB])
            e = "dve" if bi % 2 == 0 else "act"
            P.add(e, copy_op(e, OT[:, 0:nkc, c0:c0 + nr], PSH[:].rearrange("p (c t) -> p c t", c=KC)[:, 0:nkc, 0:nr]), reads=[PSHB, OTB], writes=[OTB])
        for m in range(KC):
            pb = m % 2

            def op_(h, m=m, pb=pb):
                for k in range(nkc):
                    r = h.matmul(PS[pb][:, 0:N], lhsT=wo[:, k, m * 128:(m + 1) * 128], rhs=OT[:, k, 0:N], start=(k == 0), stop=(k == nkc - 1))
                return r
            P.add("pe", op_, reads=[woB, OTB], writes=[PSB[pb]])
            P.add("dve", lambda h, m=m, pb=pb: h.scalar_tensor_tensor(out=xt[:, m, 0:N], in0=xt[:, m, 0:N], scalar=rscale, in1=PS[pb][:, 0:N],
                                                                      op0=ALU.mult, op1=ALU.add), reads=[xtB[m], PSB[pb]], writes=[xtB[m]])
        if do_ln:
            layer_norm(xt, xtB, N, li, 0, scr, (2, 3))

    def phase_swa(li, src, dst):
        NM = 256
        CUT = int(os.environ.get("KSWA_CUT", "9"))
        SKIP = os.environ.get("KSWA_SKIP", "").split(",")

        def stub_tail(xt, xtB, N, t0):
            P.add("dve", lambda h: h.tensor_scalar(out=xt[:, :, 0:N], in0=xt[:, :, 0:N], scalar1=ALPHA, scalar2=None, op0=ALU.mult), reads=xtB, writes=xtB)
            layer_norm(xt, xtB, N, li, 0, scr, (2, 3))
            x_store(dst, t0, N, xt, xtB)
        wq, wqB = tb([128, KC, 1536], BF16, "wqkv")
        wo, woB = tb([128, KC, D], BF16, "wo")
        stg = mk_stage()
        load_w(I["sw_qkv"][0], D, 1536, wq, wqB, stg)
        load_w(I["sw_out"][0], D, D, wo, woB, stg)
        bands, bandsB = tb([128, 2 * 33 * 128], BF16, "bands")
        TBt, TBB = tb([128, 512], F32, "table")
        esk, eskB = tb([128, 16], F32, "esink")
        EBf, EBfB = tb([128, 2, 16, 128], F32, "EBf")
        EB, EBB = tb([128, 2, 16, 128], BF16, "EB")
        P.add("sp", lambda h: [h.dma_start(out=bands[:], in_=I["bands"])], writes=[bandsB], dma=True)
        P.add("sp", lambda h: [h.dma_start(out=TBt[:], in_=I["table"].to_broadcast([128, 512]))], writes=[TBB], dma=True)
        P.add("sp", lambda h: [h.dma_start(out=esk[:], in_=I["sinks"].to_broadcast([128, 16]))], writes=[eskB], dma=True)
        P.add("act", lambda h: h.activation(out=esk[:], in_=esk[:], func=AF.Exp), reads=[eskB], writes=[eskB])
        EBk = [[Buf(f"EBf{k}_{h_}") for h_ in range(16)] for k in range(2)]
        EBall = [x for r_ in EBk for x in r_]
        P.add("pool", lambda h: h.memset(EBf[:], 0.0), writes=EBall)
        _bn = _swa_bands()
        for b in range(33):
            for kind in range(2):
                if not _bn[kind, b].any():
                    continue
                bm = bands[:, (kind * 33 + b) * 128:(kind * 33 + b + 1) * 128]
                for hd in range(16):
                    sc = TBt[:, b * 16 + hd:b * 16 + hd + 1] if b < 32 else NEG
                    P.add("dve", lambda h, kind=kind, hd=hd, bm=bm, sc=sc: h.scalar_tensor_tensor(out=EBf[:, kind, hd, :], in0=bm, scalar=sc, in1=EBf[:, kind, hd, :],
                                                                                                 op0=ALU.mult, op1=ALU.add), reads=[bandsB, TBB, EBk[kind][hd]], writes=[EBk[kind][hd]])
        P.add("act", lambda h: h.activation(out=EB[:].rearrange("p a b c -> p (a b c)"), in_=EBf[:].rearrange("p a b c -> p (a b c)"), func=AF.Exp), reads=EBall, writes=[EBB])

        xts = [tbl([128, KC, NM], F32, f"xt{i}_") for i in range(2)]
        xb, xbB = tb([128, KC, NM], BF16, "xb")
        QT, QTB = tb([128, 16, NM], BF16, "QT")
        KT, KTB = tb([128, 2, 128 + NM], BF16, "KT")
        KT2, KT2B = tb([128, 2, 128 + NM], BF16, "KT2")
        VA, VAB = tb([128, 5, 4, 65], BF16, "VA")
        kvo, kvoB = tb([128, 512], F32, "kvo")
        ctm, ctmB = tb([128, 512], F32, "ctm")
        PTs = [tb([128, 512], BF16, f"pt{i}") for i in range(4)]
        pi_state = {"n": 0}
        Osb, OsbB = tb([128, 4, D], BF16, "Osb")
        OT, OTB = tb([128, KC, NM], BF16, "OT")
        rec, recB = tb([128, 16], F32, "rec")
        scr = ln_scratch(NM)
        P.add("pool", lambda h: h.memset(VA[:], 1.0), writes=[VAB])
        P.add("pool", lambda h: h.memset(KT[:], 0.0), writes=[KTB])
        P.add("pool", lambda h: h.memset(KT2[:], 0.0), writes=[KT2B])
        P.add("pool", lambda h: h.memset(QT[:], 0.0), writes=[QTB])
        wk2, wk2B = tb([128, KC, 256], BF16, "wk2")
        for g in range(4):
            g2 = g ^ 1
            e = cast_eng()
            P.add(e, copy_op(e, wk2[:, :, g * 64:(g + 1) * 64], wq[:, :, D + g2 * 64:D + (g2 + 1) * 64]), reads=[wqB], writes=[wk2B])
        tiles = mixer_tiles()
        x_load(src, tiles[0][0], tiles[0][1], *xts[0])
        pi = 0
        for it, (t0, N) in enumerate(tiles):
            xt, xtB = xts[it % 2]
            if it + 1 < len(tiles):
                x_load(src, tiles[it + 1][0], tiles[it + 1][1], *xts[(it + 1) % 2])
            samp = t0 >= T
            nb = N // 128
            if CUT < 1:
                stub_tail(xt, xtB, N, t0)
                continue
            P.add("pool", lambda h, xt=xt, N=N: h.tensor_copy(out=xb[:, :, 0:N], in_=xt[:, :, 0:N]), reads=xtB, writes=[xbB])
            for c in range(KC):
                pb = c % 2

                def qp(h, c=c, pb=pb, N=N):
                    for k in range(KC):
                        r = h.matmul(PS[pb][:, 0:N], lhsT=wq[:, k, c * 128:(c + 1) * 128], rhs=xb[:, k, 0:N], start=(k == 0), stop=(k == KC - 1))
                    return r
                P.add("pe", qp, reads=[wqB, xbB], writes=[PSB[pb]])
                P.add("act", copy_op("act", QT[0:64, 2 * c, 0:N], PS[pb][0:64, 0:N]), reads=[PSB[pb], QTB], writes=[QTB])
                P.add("dve", copy_op("dve", QT[64:128, 2 * c + 1, 0:N], PS[pb][64:128, 0:N]), reads=[PSB[pb], QTB], writes=[QTB])
            if not samp:
                for (wsrc, wB, c0w, kt, ktB) in (((wq, wqB, D, KT, KTB), (wk2, wk2B, 0, KT2, KT2B)) if "kp" not in SKIP else ()):
                    for c in range(2):
                        pb = 2 + c

                        def kp(h, c=c, pb=pb, wsrc=wsrc, c0w=c0w, N=N):
                            for k in range(KC):
                                r = h.matmul(PS[pb][:, 0:N], lhsT=wsrc[:, k, c0w + c * 128:c0w + (c + 1) * 128], rhs=xb[:, k, 0:N], start=(k == 0), stop=(k == KC - 1))
                            return r
                        P.add("pe", kp, reads=[wB, xbB], writes=[PSB[pb]])
                        P.add("dve", lambda h, c=c, pb=pb, kt=kt, N=N: h.tensor_copy(out=kt[:, c, 128:128 + N], in_=PS[pb][:, 0:N]), reads=[PSB[pb], ktB], writes=[ktB])
                for b in range(nb):
                    def kvp(h, b=b):
                        for k in range(KC):
                            r = h.matmul(PS[4][:, 0:512], lhsT=xb[:, k, b * 128:(b + 1) * 128], rhs=wq[:, k, D:D + 512], start=(k == 0), stop=(k == KC - 1))
                        return r
                    if "kvp" in SKIP:
                        continue
                    P.add("pe", kvp, reads=[wqB, xbB], writes=[PSB[4]])
                    if "va" not in SKIP:
                        P.add("act", lambda h, b=b: h.activation(out=VA[:, 1 + b, :, 0:64], in_=PS[4][:, 256:512].rearrange("p (g d) -> p g d", g=4), func=AF.Copy),
                              reads=[PSB[4], VAB], writes=[VAB])
                    if t0 + (b + 1) * 128 == T and "kvo" not in SKIP:
                        P.add("dve", lambda h: h.tensor_copy(out=kvo[:], in_=PS[4][:, 0:512]), reads=[PSB[4]], writes=[kvoB])
                        P.add("sp", lambda h: [h.dma_start(out=O["swk_p"], in_=kvo[:, 0:256])], reads=[kvoB], writes=[OB["swk_p"]], dma=True)
                        P.add("sp", lambda h: [h.dma_start(out=O["swv_p"], in_=kvo[:, 256:512])], reads=[kvoB], writes=[OB["swv_p"]], dma=True)
                segl = [(0, nb, None)]
            else:
                segl = [(0, 1, 0), (0, 1, 1)]
            if CUT < 2 or (CUT == 2 and samp):
                stub_tail(xt, xtB, N, t0)
                continue
            for (sg_i, (q0blk, nqb, sseq)) in enumerate(segl):
                if samp:
                    s = sseq
                    P.add("sp", lambda h, s=s: [h.dma_start(out=ctm[:, 0:256], in_=I["cswk"][s]), h.dma_start(out=ctm[:, 256:512], in_=I["cswv"][s])],
                          writes=[ctmB], dma=True, ndma=2)
                    P.add("act", lambda h: h.activation(out=VA[:, 0, :, 0:64], in_=ctm[:, 256:512].rearrange("p (g d) -> p g d", g=4), func=AF.Copy), reads=[ctmB, VAB], writes=[VAB])
                    def trk(h):
                        for c in range(2):
                            r = h.transpose(out=PS[2][:, c * 128:(c + 1) * 128], in_=ctm[:, c * 128:(c + 1) * 128], identity=ident[:])
                        return r
                    P.add("pe", trk, reads=[ctmB, identB], writes=[PSB[2]])
                    P.add("dve", lambda h: h.tensor_copy(out=KT[:, :, 0:128], in_=PS[2][:, 0:256].rearrange("p (c t) -> p c t", c=2)), reads=[PSB[2], KTB], writes=[KTB])
                    for g in range(4):
                        g2 = g ^ 1
                        P.add("pool", lambda h, g=g, g2=g2: h.tensor_copy(out=kvo[:, g * 64:(g + 1) * 64], in_=ctm[:, g2 * 64:(g2 + 1) * 64]), reads=[ctmB, kvoB], writes=[kvoB])

                    def trk2(h):
                        for c in range(2):
                            r = h.transpose(out=PS[3][:, c * 128:(c + 1) * 128], in_=kvo[:, c * 128:(c + 1) * 128], identity=ident[:])
                        return r
                    P.add("pe", trk2, reads=[kvoB, identB], writes=[PSB[3]])
                    P.add("dve", lambda h: h.tensor_copy(out=KT2[:, :, 0:128], in_=PS[3][:, 0:256].rearrange("p (c t) -> p c t", c=2)), reads=[PSB[3], KT2B], writes=[KT2B])
                    for (wsrc, wB, c0w, kt, ktB) in ((wq, wqB, D, KT, KTB), (wk2, wk2B, 0, KT2, KT2B)):
                        for c in range(2):
                            pb = 2 + c

                            def kp(h, c=c, pb=pb, wsrc=wsrc, c0w=c0w, s=s):
                                for k in range(KC):
                                    r = h.matmul(PS[pb][:, 0:64], lhsT=wsrc[:, k, c0w + c * 128:c0w + (c + 1) * 128], rhs=xb[:, k, s * 64:(s + 1) * 64], start=(k == 0), stop=(k == KC - 1))
                                return r
                            P.add("pe", kp, reads=[wB, xbB], writes=[PSB[pb]])
                            P.add("dve", lambda h, c=c, pb=pb, kt=kt: h.tensor_copy(out=kt[:, c, 128:192], in_=PS[pb][:, 0:64]), reads=[PSB[pb], ktB], writes=[ktB])

                    def kvp(h, s=s):
                        for k in range(KC):
                            r = h.matmul(PS[4][0:64, 0:512], lhsT=xb[:, k, s * 64:(s + 1) * 64], rhs=wq[:, k, D:D + 512], start=(k == 0), stop=(k == KC - 1))
                        return r
                    P.add("pe", kvp, reads=[wqB, xbB], writes=[PSB[4]])
                    P.add("act", lambda h: h.activation(out=VA[0:64, 1, :, 0:64], in_=PS[4][0:64, 256:512].rearrange("p (g d) -> p g d", g=4), func=AF.Copy),
                          reads=[PSB[4], VAB], writes=[VAB])
                    P.add("dve", lambda h: h.tensor_copy(out=kvo[0:64, :], in_=PS[4][0:64, 0:512]), reads=[PSB[4], kvoB], writes=[kvoB])
                    P.add("sp", lambda h, s=s: [h.dma_start(out=O["swk_s"][s, 64:128, :], in_=kvo[0:64, 0:256]), h.dma_start(out=O["swk_s"][s, 0:64, :], in_=ctm[64:128, 0:256])],
                          reads=[kvoB, ctmB], writes=[OB["swk_s"]], dma=True, ndma=2)
                    P.add("sp", lambda h, s=s: [h.dma_start(out=O["swv_s"][s, 64:128, :], in_=kvo[0:64, 256:512]), h.dma_start(out=O["swv_s"][s, 0:64, :], in_=ctm[64:128, 256:512])],
                          reads=[kvoB, ctmB], writes=[OB["swv_s"]], dma=True, ndma=2)
                tasks = []
                for qb in range(nqb):
                    if samp:
                        qc0, nq = sseq * 64, 64
                        kblocks = [(1, 0, 128, 0), (0, 128, 64, 1)]
                    else:
                        qc0, nq = qb * 128, 128
                        kblocks = [(1, qb * 128, 128, qb), (0, 128 + qb * 128, 128, qb + 1)]
                    if (not samp) and t0 == 0 and qb == 0:
                        kblocks = kblocks[1:]
                    for g in range(4):
                        for ki, (kind, kc0, nk, vb) in enumerate(kblocks):
                            tasks.append(dict(qb=qb, g=g, ki=ki, kind=kind, kc0=kc0, nk=nk, vb=vb, qc0=qc0, nq=nq, nkb=len(kblocks)))
                LA = 2
                unit = {"n": 0}

                def emit_front(i):
                    tk = tasks[i]
                    g, kc0, nk, qc0, nq, kind = tk["g"], tk["kc0"], tk["nk"], tk["qc0"], tk["nq"], tk["kind"]
                    pb = (2, 5, 6)[pi_state["n"] % 3]
                    pt, ptB = PTs[pi_state["n"] % 4]
                    pi_state["n"] += 1
                    tk["pt"] = (pt, ptB)

                    def st_(h, g=g, kc0=kc0, nk=nk, pb=pb, qc0=qc0, nq=nq):
                        for j in range(4):
                            hd = 4 * g + j
                            half = hd % 2
                            ktile = KT if (g % 2) == half else KT2
                            r = h.matmul(PS[pb][0:nk, j * nq:(j + 1) * nq], lhsT=ktile[:, g // 2, kc0:kc0 + nk],
                                         rhs=QT[:, hd, qc0:qc0 + nq], start=True, stop=True)
                        return r
                    P.add("pe", st_, reads=[KTB, KT2B, QTB], writes=[PSB[pb]])
                    P.add("act", lambda h, pb=pb, pt=pt, nk=nk, nq=nq: h.activation(out=pt[0:nk, 0:4 * nq], in_=PS[pb][0:nk, 0:4 * nq], func=AF.Exp, scale=SCALE), reads=[PSB[pb]], writes=[ptB])
                    ebv = EB[0:nk, kind, 4 * g:4 * g + 4, 0:nq]
                    P.add("pool", lambda h, pt=pt, nk=nk, nq=nq, ebv=ebv: h.tensor_tensor(out=pt[0:nk, 0:4 * nq].rearrange("p (j q) -> p j q", j=4),
                                                                                          in0=pt[0:nk, 0:4 * nq].rearrange("p (j q) -> p j q", j=4), in1=ebv, op=ALU.mult),
                          reads=[ptB, EBB], writes=[ptB])

                def emit_back(i):
                    tk = tasks[i]
                    g, ki, nk, nq, vb, nkb, qb = tk["g"], tk["ki"], tk["nk"], tk["nq"], tk["vb"], tk["nkb"], tk["qb"]
                    pt, ptB = tk["pt"]
                    if ki == 0:
                        unit["n"] += 1
                    acc = 3 + unit["n"] % 2

                    def pv(h, g=g, ki=ki, pt=pt, nk=nk, nq=nq, vb=vb, nkb=nkb, acc=acc):
                        for j in range(4):
                            r = h.matmul(PS[acc][0:nq, j * 65:(j + 1) * 65], lhsT=pt[0:nk, j * nq:(j + 1) * nq], rhs=VA[0:nk, vb, g, :],
                                         start=(ki == 0 and j == 0), stop=(ki == nkb - 1 and j == 3), skip_group_check=True)
                        return r
                    P.add("pe", pv, reads=[ptB, VAB], writes=[PSB[acc]])
                    if ki == nkb - 1:
                        ob = qb if not samp else sseq
                        pv4 = PS[acc][0:nq, 0:260].rearrange("p (j e) -> p j e", j=4)
                        P.add("dve", lambda h, g=g, nq=nq, pv4=pv4: h.tensor_tensor(out=rec[0:nq, 4 * g:4 * g + 4], in0=pv4[:, :, 64], in1=esk[0:nq, 4 * g:4 * g + 4], op=ALU.add),
                              reads=[PSB[acc], eskB, recB], writes=[recB])
                        P.add("dve", lambda h, g=g, nq=nq: h.reciprocal(out=rec[0:nq, 4 * g:4 * g + 4], in_=rec[0:nq, 4 * g:4 * g + 4]), reads=[recB], writes=[recB])
                        for j in range(4):
                            hd = 4 * g + j
                            P.add("act", lambda h, j=j, hd=hd, nq=nq, ob=ob, pv4=pv4: h.activation(out=Osb[0:nq, ob, hd * 64:(hd + 1) * 64], in_=pv4[:, j, 0:64], func=AF.Copy,
                                                                                                    scale=rec[0:nq, hd:hd + 1]),
                                  reads=[PSB[acc], recB, OsbB], writes=[OsbB])

                for i in range(min(LA, len(tasks))):
                    emit_front(i)
                for i in range(len(tasks)):
                    if i + LA < len(tasks):
                        emit_front(i + LA)
                    emit_back(i)
            if not samp:
                P.add("pool", lambda h, N=N: h.tensor_copy(out=KT[:, :, 0:128], in_=KT[:, :, N:N + 128]), reads=[KTB], writes=[KTB])
                P.add("pool", lambda h, N=N: h.tensor_copy(out=KT2[:, :, 0:128], in_=KT2[:, :, N:N + 128]), reads=[KT2B], writes=[KT2B])
                P.add("pool", lambda h, nb=nb: h.tensor_copy(out=VA[:, 0, :, :], in_=VA[:, nb, :, :]), reads=[VAB], writes=[VAB])
            if CUT < 4:
                stub_tail(xt, xtB, N, t0)
                continue
            obl = [(b, 128, b * 128) for b in range(nb)] if not samp else [(0, 64, 0), (1, 64, 64)]
            out_proj_ln(li, wo, woB, Osb, OsbB, obl, xt, xtB, N, scr, OT, OTB)
            x_store(dst, t0, N, xt, xtB)

    def phase_fox(li, src, dst):
        NM = 256
        PK = PAST + 64
        TK = T + 2 * PK
        NVB = NBP + 2 * (KB + 1)
        QD = dscr("QD", [128, 16, TT], BF16)
        KD = dscr("KD", [128, KC, TK], BF16)
        VD = dscr("VD", [NVB, 128, 16 * 65], BF16)
        QDB, KDB, VDB = Buf("QD", True), Buf("KD", True), Buf("VD", True)
        CKT, CKTB = tb([128, NVB, 16], F32, "CKT")
        CS, CSB = tb([96, TT], BF16, "CS")
        sel, selB = tb([96, 16, 128], BF16, "sel")
        tri, triB = tb([128, 128], BF16, "tri")
        nbf, nbfB = tb([96, 1], F32, "nbf")
        P.add("sp", lambda h: [h.dma_start(out=sel[:].rearrange("p a b -> p (a b)"), in_=I["sel"])], writes=[selB], dma=True)
        P.add("sp", lambda h: [h.dma_start(out=tri[:], in_=I["tri"])], writes=[triB], dma=True)
        P.add("pool", lambda h: h.memset(nbf[:], 0.0), writes=[nbfB])
        P.add("pool", lambda h: h.memset(CS[:], 0.0), writes=[CSB])
        for s3 in range(3):
            P.add("sp", lambda h, s3=s3: [h.dma_start(out=nbf[32 * s3:32 * s3 + 16, :], in_=I["fx_bf"])], reads=[nbfB], writes=[nbfB], dma=True)
        P.add("dve", lambda h: h.tensor_scalar(out=nbf[:], in0=nbf[:], scalar1=-1.0, scalar2=None, op0=ALU.mult), reads=[nbfB], writes=[nbfB])
        wo, woB = tb([128, KC, D], BF16, "wo")
        mark2 = st["off"]

        wqk, wqkB = tb([128, KC, 2048], BF16, "wqk")
        wv, wvB = tb([128, KC, 1024], BF16, "wv")
        wf3, wf3B = tb([128, KC, 96], BF16, "wf3")
        wfs, wfsB = tb([128, KC, 16], F32, "wfs")
        stg = mk_stage()
        load_w(I["fx_in"][0][:, 0:2048], D, 2048, wqk, wqkB, stg)
        load_w(I["fx_in"][0][:, 2048:3072], D, 1024, wv, wvB, stg)
        load_w(I["fx_out"][0], D, D, wo, woB, stg)
        P.add("sp", lambda h: [h.dma_start(out=wfs[:], in_=I["fx_in"][0][:, 3072:3088].rearrange("(k p) m -> p k m", p=128))], writes=[wfsB], dma=True)
        P.add("pool", lambda h: h.memset(wf3[:], 0.0), writes=[wf3B])
        for s3 in range(3):
            P.add("dve", lambda h, s3=s3: h.tensor_copy(out=wf3[:, :, 32 * s3:32 * s3 + 16], in_=wfs[:]), reads=[wfsB, wf3B], writes=[wf3B])
        xts = [tbl([128, KC, NM], F32, f"xt{i}_") for i in range(2)]
        xb, xbB = tb([128, KC, NM], BF16, "xb")
        QTm, QTmB = tb([128, 16, NM], BF16, "QTm")
        KTt, KTtB = tb([128, KC, NM], BF16, "KTt")
        kvo = [tb([128, 1024], F32, f"kvo{i}") for i in range(2)]
        VAt, VAtB = tb([128, 16, 65], BF16, "VAt")
        LF, LFB = tb([96, NM], F32, "LF")
        C3, C3B = tb([96, NM], F32, "C3")
        HI, HIB = tb([96, NM], BF16, "HI")
        R1, R1B = tb([96, NM], F32, "R1")
        ones96, ones96B = tb([96, NM], F32, "ones96")
        ccar, ccarB = tb([96, 1], F32, "ccar")
        tmo, tmoB = tb([128, 96], F32, "tmo")
        ctm, ctmB = tb([128, 1024], F32, "ctm")
        lfc, lfcB = tb([128, 96], F32, "lfc")
        P.add("pool", lambda h: h.memset(QTm[:], 0.0), writes=[QTmB])
        P.add("pool", lambda h: h.memset(VAt[:], 1.0), writes=[VAtB])
        P.add("pool", lambda h: h.memset(ones96[:], 1.0), writes=[ones96B])
        P.add("pool", lambda h: h.memset(ccar[:], 0.0), writes=[ccarB])
        P.add("pool", lambda h: h.memset(lfc[:], 0.0), writes=[lfcB])

        def c_chain(n, col0, vb0, nblk, carry_in, rows_last):
            P.add("dve", lambda h: h.tensor_tensor_scan(out=C3[:, 0:n], data0=ones96[:, 0:n], data1=LF[:, 0:n], initial=carry_in, op0=ALU.mult, op1=ALU.add),
                  reads=[ones96B, LFB, ccarB], writes=[C3B])
            P.add("dve", lambda h: h.tensor_copy(out=ccar[:], in_=C3[:, n - 1:n]), reads=[C3B], writes=[ccarB])
            if col0 is not None:
                P.add("act", lambda h: h.activation(out=HI[:, 0:n], in_=C3[:, 0:n], func=AF.Copy, scale=8.0), reads=[C3B], writes=[HIB])
                P.add("pool", lambda h: h.tensor_copy(out=CS[0:16, col0:col0 + n], in_=HI[0:16, 0:n]), reads=[HIB, CSB], writes=[CSB])
                P.add("dve", lambda h: h.scalar_tensor_tensor(out=R1[:, 0:n], in0=C3[:, 0:n], scalar=8.0, in1=HI[:, 0:n], op0=ALU.mult, op1=ALU.subtract), reads=[C3B, HIB], writes=[R1B])
                P.add("act", lambda h: h.activation(out=HI[:, 0:n], in_=R1[:, 0:n], func=AF.Copy), reads=[R1B], writes=[HIB])
                P.add("pool", lambda h: h.tensor_copy(out=CS[32:48, col0:col0 + n], in_=HI[32:48, 0:n]), reads=[HIB, CSB], writes=[CSB])
                P.add("dve", lambda h: h.tensor_tensor(out=R1[:, 0:n], in0=R1[:, 0:n], in1=HI[:, 0:n], op=ALU.subtract), reads=[R1B, HIB], writes=[R1B])
                P.add("pool", lambda h: h.tensor_copy(out=CS[64:80, col0:col0 + n], in_=R1[64:80, 0:n]), reads=[R1B, CSB], writes=[CSB])
            for b in range(nblk):
                nr = 128 if b < nblk - 1 else rows_last
                P.add("pe", lambda h, b=b, nr=nr: h.transpose(out=PS[6][0:nr, 0:96], in_=C3[:, b * 128:b * 128 + nr], identity=ident[0:96, 0:96]), reads=[C3B, identB], writes=[PSB[6]])
                P.add("act", lambda h, b=b, nr=nr: h.activation(out=CKT[0:nr, vb0 + b, :], in_=PS[6][0:nr, 0:16], func=AF.Copy, scale=-1.0), reads=[PSB[6], CKTB], writes=[CKTB])

        def logf_from_psum(n):
            P.add("act", lambda h: h.activation(out=LF[:, 0:n], in_=PS[6][0:96, 0:n], func=AF.Exp, scale=-1.0, bias=nbf[:]), reads=[PSB[6], nbfB], writes=[LFB])
            P.add("act", lambda h: h.activation(out=LF[:, 0:n], in_=LF[:, 0:n], func=AF.Ln, bias=1.0), reads=[LFB], writes=[LFB])
            P.add("dve", lambda h: h.tensor_scalar(out=LF[:, 0:n], in0=LF[:, 0:n], scalar1=-1.0, scalar2=None, op0=ALU.mult), reads=[LFB], writes=[LFB])

        def proj_tile(xt, xtB, t0, N, samp):
            nb = N // 128
            QC = KC if samp else 4
            P.add("pool", lambda h: h.tensor_copy(out=xb[:, :, 0:N], in_=xt[:, :, 0:N]), reads=xtB, writes=[xbB])
            for c in range(QC):
                pb = c % 2

                def qp(h, c=c, pb=pb):
                    for k in range(KC):
                        r = h.matmul(PS[pb][:, 0:N], lhsT=wqk[:, k, c * 128:(c + 1) * 128], rhs=xb[:, k, 0:N], start=(k == 0), stop=(k == KC - 1))
                    return r
                P.add("pe", qp, reads=[wqkB, xbB], writes=[PSB[pb]])
                P.add("act", copy_op("act", QTm[0:64, 2 * c, 0:N], PS[pb][0:64, 0:N]), reads=[PSB[pb], QTmB], writes=[QTmB])
                P.add("dve", copy_op("dve", QTm[64:128, 2 * c + 1, 0:N], PS[pb][64:128, 0:N]), reads=[PSB[pb], QTmB], writes=[QTmB])
            P.add("sp", lambda h: [h.dma_start(out=QD[:, 0:2 * QC, t0:t0 + N], in_=QTm[:, 0:2 * QC, 0:N])], reads=[QTmB], writes=[QDB], dma=True)
            for c in range(QC):
                pb = 2 + c % 2

                def kp(h, c=c, pb=pb):
                    for k in range(KC):
                        r = h.matmul(PS[pb][:, 0:N], lhsT=wqk[:, k, D + c * 128:D + (c + 1) * 128], rhs=xb[:, k, 0:N], start=(k == 0), stop=(k == KC - 1))
                    return r
                P.add("pe", kp, reads=[wqkB, xbB], writes=[PSB[pb]])
                e = "act" if c % 2 == 0 else "dve"
                P.add(e, copy_op(e, KTt[:, c, 0:N], PS[pb][:, 0:N]), reads=[PSB[pb], KTtB], writes=[KTtB])
            if not samp:
                P.add("sp", lambda h: [h.dma_start(out=KD[:, 0:QC, t0:t0 + N], in_=KTt[:, 0:QC, 0:N])], reads=[KTtB], writes=[KDB], dma=True)
            else:
                for s_ in range(2):
                    kc0 = T + s_ * PK + PAST
                    P.add("sp", lambda h, s_=s_, kc0=kc0: [h.dma_start(out=KD[:, :, kc0:kc0 + 64], in_=KTt[:, :, s_ * 64:(s_ + 1) * 64])], reads=[KTtB], writes=[KDB], dma=True)
            for b in range(nb):
                ko, koB = kvo[0]
                vo, voB = kvo[1]
                for (wsrc, wB, c0w, dstt, dstB) in ((wqk, wqkB, D, ko, koB), (wv, wvB, 0, vo, voB)):
                    for hf in range(2):
                        pb = 4 + hf

                        def tp(h, b=b, wsrc=wsrc, c0w=c0w, hf=hf, pb=pb):
                            for k in range(KC):
                                r = h.matmul(PS[pb][:, 0:512], lhsT=xb[:, k, b * 128:(b + 1) * 128], rhs=wsrc[:, k, c0w + hf * 512:c0w + (hf + 1) * 512], start=(k == 0), stop=(k == KC - 1))
                            return r
                        P.add("pe", tp, reads=[wB, xbB], writes=[PSB[pb]])
                        e = "act" if hf == 0 else "dve"
                        P.add(e, copy_op(e, dstt[:, hf * 512:(hf + 1) * 512], PS[pb][:, 0:512]), reads=[PSB[pb], dstB], writes=[dstB])
                P.add("pool", lambda h: h.tensor_copy(out=VAt[:, :, 0:64], in_=vo[:].rearrange("p (g d) -> p g d", g=16)), reads=[voB, VAtB], writes=[VAtB])
                if not samp:
                    r0 = t0 + b * 128
                    P.add("sp", lambda h, r0=r0: [h.dma_start(out=O["fk_p"][r0:r0 + 128, :], in_=ko[:])], reads=[koB], writes=[OB["fk_p"]], dma=True)
                    P.add("sp", lambda h, r0=r0: [h.dma_start(out=O["fv_p"][r0:r0 + 128, :], in_=vo[:])], reads=[voB], writes=[OB["fv_p"]], dma=True)
                    vb = r0 // 128
                    P.add("sp", lambda h, vb=vb: [h.dma_start(out=VD[vb], in_=VAt[:].rearrange("p g d -> p (g d)"))], reads=[VAtB], writes=[VDB], dma=True)
                else:
                    P.add("sp", lambda h: [h.dma_start(out=O["fk_s"], in_=ko[:])], reads=[koB], writes=[OB["fk_s"]], dma=True)
                    P.add("sp", lambda h: [h.dma_start(out=O["fv_s"], in_=vo[:])], reads=[voB], writes=[OB["fv_s"]], dma=True)
                    for s_ in range(2):
                        vb = NBP + s_ * (KB + 1) + KB
                        P.add("sp", lambda h, s_=s_, vb=vb: [h.dma_start(out=VD[vb, 0:64, :], in_=VAt[s_ * 64:(s_ + 1) * 64].rearrange("p g d -> p (g d)"))], reads=[VAtB], writes=[VDB], dma=True)
            def fp_(h):
                for k in range(KC):
                    r = h.matmul(PS[6][0:96, 0:N], lhsT=wf3[:, k, :], rhs=xb[:, k, 0:N], start=(k == 0), stop=(k == KC - 1))
                return r
            P.add("pe", fp_, reads=[wf3B, xbB], writes=[PSB[6]])
            logf_from_psum(N)
            for b in range(nb):
                P.add("pe", lambda h, b=b: h.transpose(out=PS[6][:, 0:96], in_=LF[:, b * 128:(b + 1) * 128], identity=ident[0:96, 0:96]), reads=[LFB, identB], writes=[PSB[6]])
                P.add("dve", lambda h: h.tensor_copy(out=tmo[:], in_=PS[6][:, 0:96]), reads=[PSB[6]], writes=[tmoB])
                if not samp:
                    r0 = t0 + b * 128
                    P.add("sp", lambda h, r0=r0: [h.dma_start(out=O["ff_p"][r0:r0 + 128, :], in_=tmo[:, 0:16])], reads=[tmoB], writes=[OB["ff_p"]], dma=True)
                else:
                    P.add("sp", lambda h: [h.dma_start(out=O["ff_s"], in_=tmo[:, 0:16])], reads=[tmoB], writes=[OB["ff_s"]], dma=True)

        tiles = mixer_tiles(NM)
        x_load(src, tiles[0][0], tiles[0][1], *xts[0])
        for it, (t0, N) in enumerate(tiles):
            xt, xtB = xts[it % 2]
            if it + 1 < len(tiles):
                x_load(src, tiles[it + 1][0], tiles[it + 1][1], *xts[(it + 1) % 2])
            samp = t0 >= T
            proj_tile(xt, xtB, t0, N, samp)
            if not samp:
                c_chain(N, t0, t0 // 128, N // 128, ccar[:], 128)
            else:
                newlf, newlfB = R1, R1B
                P.add("pool", lambda h: h.tensor_copy(out=newlf[:, 0:128], in_=LF[:, 0:128]), reads=[LFB], writes=[newlfB])
                NLF, NLFB = tb([96, 128], F32, "NLF")
                P.add("pool", lambda h: h.tensor_copy(out=NLF[:], in_=LF[:, 0:128]), reads=[LFB], writes=[NLFB])
                for s_ in range(2):
                    P.add("pool", lambda h: h.memset(ccar[:], 0.0), reads=[ccarB], writes=[ccarB])
                    vb0 = NBP + s_ * (KB + 1)
                    for cb in range(0, KB, 2):
                        n2 = min(2, KB - cb)
                        for b2 in range(n2):
                            b = cb + b2
                            P.add("sp", lambda h, s_=s_, b=b: [h.dma_start(out=ctm[:], in_=I["cfk"][s_, b * 128:(b + 1) * 128, :])], writes=[ctmB], dma=True)
                            for c in range(KC):
                                pb = c % 2
                                P.add("pe", lambda h, c=c, pb=pb: h.transpose(out=PS[pb][:, 0:128], in_=ctm[:, c * 128:(c + 1) * 128], identity=ident[:]), reads=[ctmB, identB], writes=[PSB[pb]])
                                e = "act" if c % 2 == 0 else "dve"
                                P.add(e, copy_op(e, KTt[:, c, 0:128], PS[pb][:, 0:128]), reads=[PSB[pb], KTtB], writes=[KTtB])
                            kc0 = T + s_ * PK + b * 128
                            P.add("sp", lambda h, kc0=kc0: [h.dma_start(out=KD[:, :, kc0:kc0 + 128], in_=KTt[:, :, 0:128])], reads=[KTtB], writes=[KDB], dma=True)
                            vo, voB = kvo[1]
                            P.add("sp", lambda h, s_=s_, b=b: [h.dma_start(out=vo[:], in_=I["cfv"][s_, b * 128:(b + 1) * 128, :])], writes=[voB], dma=True)
                            P.add("pool", lambda h: h.tensor_copy(out=VAt[:, :, 0:64], in_=vo[:].rearrange("p (g d) -> p g d", g=16)), reads=[voB, VAtB], writes=[VAtB])
                            P.add("sp", lambda h, vb=vb0 + b: [h.dma_start(out=VD[vb], in_=VAt[:].rearrange("p g d -> p (g d)"))], reads=[VAtB], writes=[VDB], dma=True)
                            for s3 in range(3):
                                P.add("sp", lambda h, s_=s_, b=b, s3=s3: [h.dma_start(out=lfc[:, 32 * s3:32 * s3 + 16], in_=I["cff"][s_, b * 128:(b + 1) * 128, :])],
                                      reads=[lfcB], writes=[lfcB], dma=True)
                            P.add("pe", lambda h: h.transpose(out=PS[6][0:96, 0:128], in_=lfc[:], identity=ident[:]), reads=[lfcB, identB], writes=[PSB[6]])
                            P.add("dve", lambda h, b2=b2: h.tensor_copy(out=LF[:, b2 * 128:(b2 + 1) * 128], in_=PS[6][0:96, 0:128]), reads=[PSB[6], LFB], writes=[LFB])
                        c_chain(n2 * 128, None, vb0 + cb, n2, ccar[:], 128)
                    P.add("pool", lambda h, s_=s_: h.tensor_copy(out=LF[:, 0:64], in_=NLF[:, s_ * 64:(s_ + 1) * 64]), reads=[NLFB, LFB], writes=[LFB])
                    c_chain(64, T + s_ * 64, vb0 + KB, 1, ccar[:], 64)

        P.barrier()
        st["off"] = mark2
        NMB = 512
        NKM = max(T, PK)
        VA, VAB = tb([128, max(NBP, 2 * (KB + 1)), 16 * 65], BF16, "VA")
        KTc = [tb([128, NKM], BF16, f"KTc{i}") for i in range(1)]
        QTf, QTsB = tb([128, 8 * NMB], BF16, "QTs")
        QTs = QTf[:].rearrange("p (g t) -> p g t", t=NMB)
        QTs_s = QTf[:, 0:16 * 128].rearrange("p (g t) -> p g t", t=128)
        xts = [tbl([128, KC, NMB], F32, f"xt{i}_") for i in range(1)]
        zt, ztB = tbl([128, KC, NMB], F32, "zt_")
        PTs = [tb([128, NMB], BF16, f"pt{i}") for i in range(4)]
        Osb, OsbB = tb([128, 4, D], BF16, "Osb")
        OT, OTB = tb([128, KC, NMB], BF16, "OT")
        recs = [tb([128, 4], F32, f"rec{i}") for i in range(2)]
        rec, recB = recs[0]
        scr = ln_scratch(NMB)
        tiles = mixer_tiles(NMB)
        VA4 = VA[:].rearrange("p b (g e) -> p b g e", g=16)
        pi = 0
        kci = 0
        pend = {"it": None}

        def fox_back(jt):
            tt0, NN = tiles[jt]
            kk = jt % NR
            vo = CCO[kk].rearrange("p (c t) -> p c t", c=KC)
            P.add("sp", lambda h, vo=vo: [h.dma_start(out=zt[:, :, 0:NMB], in_=vo)], reads=[CCOB[kk]], writes=ztB, dma=True)
            layer_norm(zt, ztB, NN, li, 0, scr, (2, 3))
            x_store(dst, tt0, NN, zt, ztB)

        for it, (t0, N) in enumerate(tiles):
            xt, xtB = xts[0]
            samp = t0 >= T
            NHL = 8 if not samp else 16
            QV = QTs if not samp else QTs_s
            P.add("sp", lambda h, t0=t0, N=N, QV=QV, NHL=NHL: [h.dma_start(out=QV[:, 0:NHL, 0:N], in_=QD[:, 0:NHL, t0:t0 + N])], reads=[QDB], writes=[QTsB], dma=True)
            if not samp:
                for b in range(N // 128):
                    vb = t0 // 128 + b
                    P.add("sp", lambda h, vb=vb: [h.dma_start(out=VA[:, vb, :], in_=VD[vb])], reads=[VDB, VAB], writes=[VAB], dma=True)
                qsets = [(0, N, 0, t0 + N, 0, 0, t0)]
            else:
                qsets = []
                for s_ in range(2):
                    qsets.append((s_ * 64, 64, T + s_ * PK, PK, s_ * (KB + 1), NBP + s_ * (KB + 1), PAST))
            tasks = []
            for (qc0, nq, kd0, nkeys, vab0, ckb0, qpos) in qsets:
                if samp:
                    s_ = qc0 // 64
                    for b in range(KB + 1):
                        nr = 128 if b < KB else 64
                        P.add("sp", lambda h, b=b, nr=nr, vab0=vab0, vb=NBP + s_ * (KB + 1) + b: [h.dma_start(out=VA[0:nr, vab0 + b, :], in_=VD[vb, 0:nr, :])], reads=[VDB, VAB], writes=[VAB], dma=True)
                nsub = (nq + 127) // 128
                nkb = (nkeys + 127) // 128
                for hd in range(NHL):
                    blks = []
                    for kb in range(nkb):
                        nk = min(128, nkeys - kb * 128)
                        kpos0 = kb * 128
                        jlo = max(0, kpos0 - qpos)
                        jlo = (jlo // 128) * 128 if nq > 64 else 0
                        if jlo >= nq:
                            continue
                        nqv = nq - jlo
                        diag = (kpos0 + nk - 1) > (qpos + jlo)
                        blks.append(dict(hd=hd, kb=kb, nk=nk, jlo=jlo, nqv=nqv, diag=diag, qc0=qc0, nq=nq, kd0=kd0, nkeys=nkeys,
                                         vab0=vab0, ckb0=ckb0, nsub=nsub, first=False, last=False))
                    blks[0]["first"] = True
                    blks[-1]["last"] = True
                    tasks.extend(blks)
            LA = 2
            kt_state = {"key": None, "buf": None}
            head_ctr = {"n": 0}

            def emit_front(i):
                tk = tasks[i]
                hd, kb, nk, jlo, nqv, qc0 = tk["hd"], tk["kb"], tk["nk"], tk["jlo"], tk["nqv"], tk["qc0"]
                key = (qc0, hd // 2)
                if kt_state["key"] != key:
                    ktc, ktcB = KTc[0]
                    kt_state["n"] = kt_state.get("n", 0) + 1
                    kt_state["key"] = key
                    kt_state["buf"] = (ktc, ktcB)
                    c, kd0, nkeys = hd // 2, tk["kd0"], tk["nkeys"]
                    P.add("sp", lambda h, ktc=ktc, c=c, kd0=kd0, nkeys=nkeys: [h.dma_start(out=ktc[:, 0:nkeys], in_=KD[:, c, kd0:kd0 + nkeys])], reads=[KDB], writes=[ktcB], dma=True)
                ktc, ktcB = kt_state["buf"]
                pb = 4 + i % 3
                pt, ptB = PTs[i % 4]
                tk["pt"] = (pt, ptB)

                def st_(h, ktc=ktc, kb=kb, nk=nk, pb=pb, hd=hd, qc0=qc0, jlo=jlo, nqv=nqv, t0=t0, QV=QV):
                    h.matmul(PS[pb][0:nk, 0:nqv], lhsT=ktc[:, kb * 128:kb * 128 + nk], rhs=QV[:, hd, qc0 + jlo:qc0 + jlo + nqv], start=True, stop=False)
                    return h.matmul(PS[pb][0:nk, 0:nqv], lhsT=sel[:, hd, 0:nk], rhs=CS[:, t0 + qc0 + jlo:t0 + qc0 + jlo + nqv], start=False, stop=True)
                P.add("pe", st_, reads=[ktcB, QTsB, selB, CSB], writes=[PSB[pb]])
                bias = CKT[0:nk, tk["ckb0"] + kb, hd:hd + 1]
                P.add("act", lambda h, pt=pt, pb=pb, nk=nk, nqv=nqv, bias=bias: h.activation(out=pt[0:nk, 0:nqv], in_=PS[pb][0:nk, 0:nqv], func=AF.Exp, scale=SCALE, bias=bias),
                      reads=[PSB[pb], CKTB], writes=[ptB])
                if tk["diag"]:
                    dq = min(nk, nqv)
                    P.add("pool", lambda h, pt=pt, nk=nk, dq=dq: h.tensor_tensor(out=pt[0:nk, 0:dq], in0=pt[0:nk, 0:dq], in1=tri[0:nk, 0:dq], op=ALU.mult), reads=[ptB, triB], writes=[ptB])

            def emit_back(i):
                tk = tasks[i]
                hd, kb, nk, jlo, nqv, nq = tk["hd"], tk["kb"], tk["nk"], tk["jlo"], tk["nqv"], tk["nq"]
                pt, ptB = tk["pt"]
                if tk["first"]:
                    head_ctr["n"] += 1
                acc = 2 + head_ctr["n"] % 2

                def pv(h, pt=pt, nk=nk, jlo=jlo, kb=kb, hd=hd, acc=acc, first=tk["first"], last=tk["last"], vab0=tk["vab0"], nq=nq):
                    r = None
                    for sb in range(jlo // 128, (nq + 127) // 128):
                        w = min(128, nq - sb * 128)
                        r = h.matmul(PS[acc][0:w, sb * 65:(sb + 1) * 65], lhsT=pt[0:nk, sb * 128 - jlo:sb * 128 - jlo + w], rhs=VA4[0:nk, vab0 + kb, hd, :],
                                     start=(first and sb == jlo // 128), stop=(last and sb == (nq + 127) // 128 - 1), skip_group_check=True)
                    return r
                P.add("pe", pv, reads=[ptB, VAB], writes=[PSB[acc]])
                if tk["last"]:
                    for sb in range(tk["nsub"]):
                        w = min(128, nq - sb * 128)
                        ob = sb if not samp else tk["qc0"] // 64
                        P.add("dve", lambda h, sb=sb, w=w, acc=acc: h.reciprocal(out=rec[0:w, sb:sb + 1], in_=PS[acc][0:w, sb * 65 + 64:sb * 65 + 65]), reads=[PSB[acc], recB], writes=[recB])
                        P.add("act", lambda h, sb=sb, w=w, acc=acc, ob=ob, hd=hd: h.activation(out=Osb[0:w, ob, hd * 64:(hd + 1) * 64], in_=PS[acc][0:w, sb * 65:sb * 65 + 64], func=AF.Copy,
                                                                                           scale=rec[0:w, sb:sb + 1]), reads=[PSB[acc], recB, OsbB], writes=[OsbB])

            for i in range(min(LA, len(tasks))):
                emit_front(i)
            for i in range(len(tasks)):
                if i + LA < len(tasks):
                    emit_front(i + LA)
                emit_back(i)
            x_load(src, t0, N, xt, xtB)
            if samp:
                if pend["it"] is not None:
                    fox_back(pend["it"])
                    pend["it"] = None
                obl = [(0, 64, 0), (1, 64, 64)]
                out_proj_ln(li, wo, woB, Osb, OsbB, obl, xt, xtB, N, scr, OT, OTB)
                x_store(dst, t0, N, xt, xtB)
            else:
                obl = [(b, 128, b * 128) for b in range(N // 128)]
                out_proj_ln(li, wo, woB, Osb, OsbB, obl, xt, xtB, N, scr, OT, OTB, nkc=4, rscale=ALPHA / 2, do_ln=False)
                k = it % NR
                vi = CCI[k].rearrange("p (c t) -> p c t", c=KC)
                P.add("sp", lambda h, vi=vi: [h.dma_start(out=vi, in_=xt[:, :, 0:NMB])], reads=xtB, writes=[CCIB[k]], dma=True)
                P.add("pool", lambda h, k=k: [h.collective_compute("AllReduce", ALU.add, replica_groups=CC_GROUPS, ins=[CCI[k]], outs=[CCO[k]])],
                      reads=[CCIB[k]], writes=[CCOB[k]], dma=True, inc=1)
                if pend["it"] is not None:
                    fox_back(pend["it"])
                pend["it"] = it
        if pend["it"] is not None:
            fox_back(pend["it"])

    import os
    stages = os.environ.get("KSTAGES", "in,m0,f0,m1,f1,m2,f2,m3,f3,out").split(",")
    if "in" in stages and not ("m0" in stages and os.environ.get("KFUSE_IN", "1") == "1"):
        phase_in()
    cur = 0
    for li in range(4):
        kind, j = li % 3, li // 3
        if f"m{li}" in stages:
            phase_reset()
            if kind == 0:
                phase_rg(j, li, cur, 1 - cur)
            elif kind == 1:
                if os.environ.get("KSWA", "1") == "1":
                    phase_swa(li, cur, 1 - cur)
                else:
                    phase_fox(li, cur, 1 - cur)
            else:
                phase_fox(li, cur, 1 - cur)
            cur = 1 - cur
        if f"f{li}" in stages:
            phase_reset()
            phase_ffn(li, cur, 1 - cur)
            cur = 1 - cur
    if "out" in stages and not (FUSE_OUT and "f3" in stages):
        phase_reset()
        phase_out(cur)
    P.emit()
    return nc


_CACHE = {}


def kernel(x_prompt, x_sample, state_rg_conv, state_rg_h, cache_swa_k, cache_swa_v,
           cache_fox_k, cache_fox_v, cache_fox_logf, ln_gain, ln_bias, ffn_w_up, ffn_w_down,
           rg_w_in, rg_conv_w, rg_conv_b, rg_gate_w, rg_gate_b, rg_lambda, rg_w_out,
           swa_w_qkv, swa_sinks, swa_w_out, rel_bias_table, fox_w_in, fox_b_f, fox_w_out):
    f = lambda a: np.ascontiguousarray(np.asarray(a), dtype=np.float32)
    x_prompt, x_sample = f(x_prompt), f(x_sample)
    B, T, _ = x_prompt.shape
    PAST = cache_fox_k.shape[2]
    key = (T, PAST)
    if key not in _CACHE:
        _CACHE[key] = build(T, PAST)
    nc = _CACHE[key]
    state_rg_conv, state_rg_h = f(state_rg_conv), f(state_rg_h)
    shared = {
        "rg_in": f(rg_w_in), "rg_gw": f(rg_gate_w), "rg_out": f(rg_w_out),
        "sw_qkv": f(swa_w_qkv), "sw_out": f(swa_w_out), "sinks": f(swa_sinks), "table": f(rel_bias_table).reshape(1, 512),

        "ident": np.eye(128, dtype=np.float32),
        "bands": np.ascontiguousarray(_swa_bands().transpose(2, 0, 1, 3).reshape(128, -1)).astype(ml_dtypes.bfloat16),
        "tri": (np.arange(128)[:, None] <= np.arange(128)[None, :]).astype(ml_dtypes.bfloat16),
    }
    sel = np.zeros((96, 16, 128), np.float32)
    for h in range(16):
        for s in range(3):
            sel[32 * s + h, h, :] = 1.0
    shared["sel"] = sel.reshape(96, -1).astype(ml_dtypes.bfloat16)
    ln_gain, ln_bias = f(ln_gain), f(ln_bias)
    rg_conv_w, rg_conv_b, rg_gate_b, rg_lambda = f(rg_conv_w), f(rg_conv_b), f(rg_gate_b), f(rg_lambda)
    cache_swa_k, cache_swa_v = f(cache_swa_k), f(cache_swa_v)
    cache_fox_k, cache_fox_v, cache_fox_logf = f(cache_fox_k), f(cache_fox_v), f(cache_fox_logf)
    ffn_w_up, ffn_w_down = f(ffn_w_up), f(ffn_w_down)
    fox_w_in, fox_b_f, fox_w_out = f(fox_w_in), f(fox_b_f), f(fox_w_out)
    perms = [list(range(0, 16)), list(range(8, 16)) + list(range(0, 8))]
    colp = [np.concatenate([np.arange(hh * 64, (hh + 1) * 64) for hh in pr]) for pr in perms]
    fx_in_l, fx_bf_l, fx_out_l = [], [], []
    for r in range(2):
        cp, pr = colp[r], np.asarray(perms[r])
        w = np.concatenate([fox_w_in[:, :, cp], fox_w_in[:, :, 1024 + cp], fox_w_in[:, :, 2048 + cp], fox_w_in[:, :, 3072 + pr]], axis=-1)
        fx_in_l.append(np.ascontiguousarray(w))
        fx_bf_l.append(np.ascontiguousarray(fox_b_f[0, pr].reshape(16, 1)))
        fx_out_l.append(np.ascontiguousarray(fox_w_out[:, cp, :]))
    w_up_h = [np.ascontiguousarray(ffn_w_up[:, :, r * 2048:(r + 1) * 2048]) for r in range(2)]
    w_dn_h = [np.ascontiguousarray(ffn_w_down[:, r * 2048:(r + 1) * 2048, :]) for r in range(2)]
    in_maps = []
    for c in range(8):
        bp = c % B
        ss = [2 * c, 2 * c + 1]
        prm = np.zeros((384, 128), np.float32)
        prm[0:64] = ln_gain.reshape(64, 128)
        prm[64:128] = ln_bias.reshape(64, 128)
        prm[128:192] = rg_conv_w.reshape(64, 128)
        prm[192:208] = rg_conv_b.reshape(16, 128)
        prm[208:240] = rg_gate_b.reshape(32, 128)
        prm[240:256] = rg_lambda.reshape(16, 128)
        prm[256:288] = state_rg_h[:, ss, :].reshape(32, 128)
        prm[288:384] = state_rg_conv[:, ss, :, :].reshape(96, 128)
        m = dict(shared)
        rk = c // 4
        m["fx_in"] = fx_in_l[rk]
        m["fx_bf"] = fx_bf_l[rk]
        m["fx_out"] = fx_out_l[rk]
        m["w_up"] = w_up_h[rk]
        m["w_dn"] = w_dn_h[rk]
        mk = np.zeros((128, 2), np.float32)
        mk[:, rk] = 1.0
        m["mk"] = mk
        m["xp"] = x_prompt[bp]
        m["xs"] = np.ascontiguousarray(x_sample[ss].reshape(128, D))
        m["prm"] = prm.reshape(3, 128, 128)
        m["cswk"] = np.ascontiguousarray(cache_swa_k[0, ss].reshape(2, 128, 256))
        m["cswv"] = np.ascontiguousarray(cache_swa_v[0, ss].reshape(2, 128, 256))
        m["cfk"] = np.ascontiguousarray(cache_fox_k[0, ss].reshape(2, PAST, D)[:, :, colp[rk]])
        m["cfv"] = np.ascontiguousarray(cache_fox_v[0, ss].reshape(2, PAST, D)[:, :, colp[rk]])
        m["cff"] = np.ascontiguousarray(cache_fox_logf[0, ss].reshape(2, PAST, 16)[:, :, perms[rk]])
        in_maps.append(m)
    res = run_bass_kernel_spmd(nc, in_maps, core_ids=list(range(8)))
    R = res.results
    global _LAST
    _LAST = R
    pc = list(range(B))
    y_p = np.stack([R[c]["y_p"] for c in pc])
    y_s = np.concatenate([R[c]["y_s"].reshape(2, 64, D) for c in range(8)])
    rgc_p = np.stack([R[c]["rgc_p"] for c in pc], axis=1)
    rgc_s = np.concatenate([R[c]["rgc_s"] for c in range(8)], axis=1)
    rgh_p = np.stack([R[c]["rgh_p"] for c in pc], axis=1)
    rgh_s = np.concatenate([R[c]["rgh_s"] for c in range(8)], axis=1)
    swk_p = np.stack([R[c]["swk_p"].reshape(128, 4, 64) for c in pc])[None]
    swk_s = np.concatenate([R[c]["swk_s"].reshape(2, 128, 4, 64) for c in range(8)])[None]
    swv_p = np.stack([R[c]["swv_p"].reshape(128, 4, 64) for c in pc])[None]
    swv_s = np.concatenate([R[c]["swv_s"].reshape(2, 128, 4, 64) for c in range(8)])[None]
    fk_p = np.stack([R[c]["fk_p"].reshape(T, 16, 64) for c in pc])[None]
    inv = [np.argsort(perms[r]) for r in range(2)]
    fk_s = np.concatenate([R[c]["fk_s"].reshape(2, 64, 16, 64)[:, :, inv[c // 4]] for c in range(8)])[None]
    fv_p = np.stack([R[c]["fv_p"].reshape(T, 16, 64) for c in pc])[None]
    fv_s = np.concatenate([R[c]["fv_s"].reshape(2, 64, 16, 64)[:, :, inv[c // 4]] for c in range(8)])[None]
    ff_p = np.stack([R[c]["ff_p"] for c in pc])[None]
    ff_s = np.concatenate([R[c]["ff_s"].reshape(2, 64, 16)[:, :, inv[c // 4]] for c in range(8)])[None]
    return (y_p, y_s, rgc_p, rgc_s, rgh_p, rgh_s, swk_p, swk_s, swv_p, swv_s, fk_p, fk_s, fv_p, fv_s, ff_p, ff_s)
```
